# Optimizing a Trainium2 kernel written in Bass

```python
import math
import jax, jax.numpy as jnp
from jax import lax
import numpy as np

D_MODEL = 2048
BATCH = 1
SEQ = 8192
DEPTH = 2

S5_WIDTH = D_MODEL // 2
S5_GROUP = 16
S5_GROUPS = S5_WIDTH // S5_GROUP
S5_STATE = 64
RET_WIDTH = D_MODEL // 2
RET_HEADS = 4
RET_HEAD_DIM = RET_WIDTH // RET_HEADS
RET_CHUNK = 128
RET_ROPE_THETA = 10000.0
EVEN_IN_WIDTH = S5_WIDTH + 4 * RET_WIDTH
ATT_HEADS = 16
ATT_KV_HEADS = 4
ATT_HEAD_DIM = D_MODEL // ATT_HEADS
ATT_GROUP = ATT_HEADS // ATT_KV_HEADS
ATT_WINDOW = 128
ATT_BLOCK = 128
ROPE_THETA = 500000.0
ROPE_DIM = ATT_HEAD_DIM // 4
ODD_IN_WIDTH = (ATT_HEADS + 2 * ATT_KV_HEADS) * ATT_HEAD_DIM
D_FF = 4 * D_MODEL
N_EVEN = (DEPTH + 1) // 2
N_ODD = DEPTH // 2
DEEPNORM_ALPHA = (2 * DEPTH) ** 0.25
DEEPNORM_BETA = (8 * DEPTH) ** -0.25
LN_EPS = 1e-5
HEAD_NORM_EPS = 1e-6
NEG_INF = -1e30

kernel_name = 'hybrid_s5_retention_swa_encoder'

F32 = jnp.float32


def _layer_norm(x, g, b):
    xf = x.astype(F32)
    mu = xf.mean(-1, keepdims=True)
    var = jnp.square(xf - mu).mean(-1, keepdims=True)
    return ((xf - mu) * lax.rsqrt(var + LN_EPS) * g.astype(F32) + b.astype(F32)).astype(x.dtype)


def _sq_relu_mlp(x, w1, w2):
    return jnp.square(jax.nn.relu(x @ w1)) @ w2


def _rotary(x, pos, rot_dim, theta):
    half = rot_dim // 2
    inv_freq = 1.0 / (theta ** (jnp.arange(half, dtype=F32) / half))
    ang = pos.astype(F32)[:, None] * inv_freq[None, :]
    cos = jnp.cos(ang)[None, :, None, :]
    sin = jnp.sin(ang)[None, :, None, :]
    xr = x[..., :rot_dim].astype(F32)
    x1, x2 = xr[..., :half], xr[..., half:]
    rot = jnp.concatenate([x1 * cos - x2 * sin, x2 * cos + x1 * sin], axis=-1).astype(x.dtype)
    return jnp.concatenate([rot, x[..., rot_dim:]], axis=-1)


def _cplx_combine(e1, e2):
    a1r, a1i, b1r, b1i = e1
    a2r, a2i, b2r, b2i = e2
    return (a2r * a1r - a2i * a1i,
            a2r * a1i + a2i * a1r,
            a2r * b1r - a2i * b1i + b2r,
            a2r * b1i + a2i * b1r + b2i)


def _s5_scan(u, lam_re, lam_im, log_step, b_re, b_im, c_re, c_im):
    lam_re = jnp.minimum(lam_re.astype(F32), -1e-4)
    lam_im = lam_im.astype(F32)
    step = jnp.exp(log_step.astype(F32))[:, None]
    mag = jnp.exp(lam_re * step)
    ar = mag * jnp.cos(lam_im * step)
    ai = mag * jnp.sin(lam_im * step)
    nr, ni = ar - 1.0, ai
    den = lam_re * lam_re + lam_im * lam_im
    zr = (nr * lam_re + ni * lam_im) / den
    zi = (ni * lam_re - nr * lam_im) / den
    b_re = b_re.astype(F32)
    b_im = b_im.astype(F32)
    bbr = zr[..., None] * b_re - zi[..., None] * b_im
    bbi = zr[..., None] * b_im + zi[..., None] * b_re
    bu_r = jnp.einsum('blgc,gpc->blgp', u, bbr)
    bu_i = jnp.einsum('blgc,gpc->blgp', u, bbi)
    a_r = jnp.broadcast_to(ar, bu_r.shape)
    a_i = jnp.broadcast_to(ai, bu_r.shape)
    _, _, h_r, h_i = lax.associative_scan(_cplx_combine, (a_r, a_i, bu_r, bu_i), axis=1)
    return (jnp.einsum('gcp,blgp->blgc', c_re.astype(F32), h_r)
            - jnp.einsum('gcp,blgp->blgc', c_im.astype(F32), h_i))


def _retention_dir(q, k, v, log_g, include_diag):
    bsz, L, H, dk = q.shape
    dv = v.shape[-1]
    C = RET_CHUNK
    nc = L // C
    qc = q.reshape(bsz, nc, C, H, dk)
    kc = k.reshape(bsz, nc, C, H, dk)
    vc = v.reshape(bsz, nc, C, H, dv)
    idx = jnp.arange(C)
    diff = idx[:, None] - idx[None, :]
    mask = diff >= 0 if include_diag else diff > 0
    decay_intra = jnp.where(mask[None],
                            jnp.exp(log_g[:, None, None] * jnp.maximum(diff, 0)[None].astype(F32)),
                            0.0)
    scores = jnp.einsum('bnihd,bnjhd->bnhij', qc, kc) * decay_intra
    intra = jnp.einsum('bnhij,bnjhe->bnihe', scores, vc)
    idx_f = idx.astype(F32)
    k_decay = jnp.exp((C - 1.0 - idx_f)[:, None] * log_g[None, :])
    q_decay = jnp.exp((idx_f + 1.0)[:, None] * log_g[None, :])
    chunk_decay = jnp.exp(C * log_g)
    chunk_kv = jnp.einsum('bnjhd,jh,bnjhe->bnhde', kc, k_decay, vc)

    def step(state, kv):
        return state * chunk_decay[None, :, None, None] + kv, state

    init = jnp.zeros((bsz, H, dk, dv), F32)
    _, s_prev = lax.scan(step, init, jnp.moveaxis(chunk_kv, 1, 0))
    s_prev = jnp.moveaxis(s_prev, 0, 1)
    cross = jnp.einsum('bnihd,ih,bnhde->bnihe', qc, q_decay, s_prev)
    return (intra + cross).reshape(bsz, L, H, dv)


def _even_mixer(x, w_in, w_out, lam_re, lam_im, log_step, b_re, b_im, c_re, c_im,
                d_skip, w_glu, b_glu, ret_log_decay):
    bsz, L, _ = x.shape
    pos = jnp.arange(L)
    h = x @ w_in
    u = h[..., :S5_WIDTH]
    q, k, v, gate = jnp.split(h[..., S5_WIDTH:], 4, axis=-1)

    uf = u.astype(F32).reshape(bsz, L, S5_GROUPS, S5_GROUP)
    y_fwd = _s5_scan(uf, lam_re[0], lam_im[0], log_step[0], b_re[0], b_im[0], c_re[0], c_im[0])
    y_bwd = jnp.flip(_s5_scan(jnp.flip(uf, 1), lam_re[1], lam_im[1], log_step[1],
                              b_re[1], b_im[1], c_re[1], c_im[1]), 1)
    y = jax.nn.gelu((y_fwd + y_bwd + d_skip.astype(F32) * uf).reshape(bsz, L, S5_WIDTH))
    s5_out = (y * jax.nn.sigmoid(y @ w_glu.astype(F32) + b_glu.astype(F32))).astype(x.dtype)

    q = _rotary(q.reshape(bsz, L, RET_HEADS, RET_HEAD_DIM), pos, RET_HEAD_DIM, RET_ROPE_THETA).astype(F32)
    k = _rotary(k.reshape(bsz, L, RET_HEADS, RET_HEAD_DIM), pos, RET_HEAD_DIM, RET_ROPE_THETA).astype(F32)
    k = k * (RET_HEAD_DIM ** -0.5)
    v = v.reshape(bsz, L, RET_HEADS, RET_HEAD_DIM).astype(F32)
    log_g = -jnp.abs(ret_log_decay.astype(F32))
    o = (_retention_dir(q, k, v, log_g[0], True)
         + jnp.flip(_retention_dir(jnp.flip(q, 1), jnp.flip(k, 1), jnp.flip(v, 1), log_g[1], False), 1))
    mu = o.mean(-1, keepdims=True)
    var = jnp.square(o - mu).mean(-1, keepdims=True)
    o = (o - mu) * lax.rsqrt(var + HEAD_NORM_EPS)
    ret_out = (o.reshape(bsz, L, RET_WIDTH) * jax.nn.silu(gate.astype(F32))).astype(x.dtype)

    return jnp.concatenate([s5_out, ret_out], axis=-1) @ w_out


def _band(t, bsz, nb):
    tb = t.reshape(bsz, nb, ATT_BLOCK, ATT_KV_HEADS, ATT_HEAD_DIM)
    tp = jnp.pad(tb, ((0, 0), (1, 1), (0, 0), (0, 0), (0, 0)))
    return jnp.concatenate([tp[:, :-2], tp[:, 1:-1], tp[:, 2:]], axis=2)


def _odd_mixer(x, w_in, w_out, sink):
    bsz, L, _ = x.shape
    nb = L // ATT_BLOCK
    pos = jnp.arange(L)
    hd = ATT_HEAD_DIM
    h = x @ w_in
    q = h[..., :ATT_HEADS * hd].reshape(bsz, L, ATT_HEADS, hd)
    k = h[..., ATT_HEADS * hd:(ATT_HEADS + ATT_KV_HEADS) * hd].reshape(bsz, L, ATT_KV_HEADS, hd)
    v = h[..., (ATT_HEADS + ATT_KV_HEADS) * hd:].reshape(bsz, L, ATT_KV_HEADS, hd)
    q = _rotary(q, pos, ROPE_DIM, ROPE_THETA)
    k = _rotary(k, pos, ROPE_DIM, ROPE_THETA)
    qb = q.reshape(bsz, nb, ATT_BLOCK, ATT_KV_HEADS, ATT_GROUP, hd)
    kb = _band(k, bsz, nb)
    vb = _band(v, bsz, nb)
    s = jnp.einsum('bnqhgd,bnkhd->bnhgqk', qb, kb).astype(F32) * (hd ** -0.5)
    t_idx = jnp.arange(ATT_BLOCK)[:, None]
    s_idx = jnp.arange(3 * ATT_BLOCK)[None, :]
    in_win = jnp.abs(t_idx - s_idx + ATT_BLOCK) <= ATT_WINDOW
    key_pos = (jnp.arange(nb)[:, None] - 1) * ATT_BLOCK + jnp.arange(3 * ATT_BLOCK)[None, :]
    valid = (key_pos >= 0) & (key_pos < L)
    mask = in_win[None] & valid[:, None, :]
    s = jnp.where(mask[None, :, None, None], s, NEG_INF)
    sink_b = sink.astype(F32).reshape(1, 1, ATT_KV_HEADS, ATT_GROUP, 1, 1)
    m = jnp.maximum(s.max(-1, keepdims=True), sink_b)
    p = jnp.exp(s - m)
    p = p / (p.sum(-1, keepdims=True) + jnp.exp(sink_b - m))
    o = jnp.einsum('bnhgqk,bnkhd->bnqhgd', p.astype(vb.dtype), vb)
    return o.reshape(bsz, L, ATT_HEADS * hd) @ w_out


def setup_inputs(seed: int = 0) -> dict:
    key = jax.random.key(seed)
    ks = jax.random.split(key, 24)
    nrm = jax.random.normal
    G, P, Cg = S5_GROUPS, S5_STATE, S5_GROUP
    x = nrm(ks[0], (BATCH, SEQ, D_MODEL), F32)
    ln_g = 1.0 + 0.02 * nrm(ks[1], (DEPTH, 2, D_MODEL), F32)
    ln_b = 0.02 * nrm(ks[2], (DEPTH, 2, D_MODEL), F32)
    mlp_w1 = nrm(ks[3], (DEPTH, D_MODEL, D_FF), F32) * D_MODEL ** -0.5
    mlp_w2 = nrm(ks[4], (DEPTH, D_FF, D_MODEL), F32) * (D_FF ** -0.5 * DEEPNORM_BETA)
    even_w_in = nrm(ks[5], (N_EVEN, D_MODEL, EVEN_IN_WIDTH), F32) * D_MODEL ** -0.5
    even_w_out = nrm(ks[6], (N_EVEN, S5_WIDTH + RET_WIDTH, D_MODEL), F32) * ((S5_WIDTH + RET_WIDTH) ** -0.5 * DEEPNORM_BETA)
    n_idx = jnp.arange(P, dtype=F32)
    s5_lambda_re = -0.5 * (1.0 + 0.02 * nrm(ks[7], (N_EVEN, 2, G, P), F32))
    s5_lambda_im = math.pi * n_idx + 0.02 * nrm(ks[8], (N_EVEN, 2, G, P), F32)
    s5_log_step = jax.random.uniform(ks[9], (N_EVEN, 2, G), F32, math.log(1e-3), math.log(1e-1))
    s5_b_re = nrm(ks[10], (N_EVEN, 2, G, P, Cg), F32) * (2 * Cg) ** -0.5
    s5_b_im = nrm(ks[11], (N_EVEN, 2, G, P, Cg), F32) * (2 * Cg) ** -0.5
    s5_c_re = nrm(ks[12], (N_EVEN, 2, G, Cg, P), F32) * P ** -0.5
    s5_c_im = nrm(ks[13], (N_EVEN, 2, G, Cg, P), F32) * P ** -0.5
    s5_d = nrm(ks[14], (N_EVEN, G, Cg), F32)
    s5_w_glu = nrm(ks[15], (N_EVEN, S5_WIDTH, S5_WIDTH), F32) * S5_WIDTH ** -0.5
    s5_b_glu = 0.01 * nrm(ks[16], (N_EVEN, S5_WIDTH), F32)
    base_decay = jnp.log(1.0 - 2.0 ** (-5.0 - jnp.arange(RET_HEADS, dtype=F32)))
    ret_log_decay = base_decay * (1.0 + 0.05 * nrm(ks[17], (N_EVEN, 2, RET_HEADS), F32))
    odd_w_in = nrm(ks[18], (N_ODD, D_MODEL, ODD_IN_WIDTH), F32) * D_MODEL ** -0.5
    odd_w_out = nrm(ks[19], (N_ODD, ATT_HEADS * ATT_HEAD_DIM, D_MODEL), F32) * ((ATT_HEADS * ATT_HEAD_DIM) ** -0.5 * DEEPNORM_BETA)
    attn_sink = 0.5 * nrm(ks[20], (N_ODD, ATT_HEADS), F32)
    return {'x': x, 'ln_g': ln_g, 'ln_b': ln_b, 'mlp_w1': mlp_w1, 'mlp_w2': mlp_w2,
            'even_w_in': even_w_in, 'even_w_out': even_w_out,
            's5_lambda_re': s5_lambda_re, 's5_lambda_im': s5_lambda_im, 's5_log_step': s5_log_step,
            's5_b_re': s5_b_re, 's5_b_im': s5_b_im, 's5_c_re': s5_c_re, 's5_c_im': s5_c_im,
            's5_d': s5_d, 's5_w_glu': s5_w_glu, 's5_b_glu': s5_b_glu, 'ret_log_decay': ret_log_decay,
            'odd_w_in': odd_w_in, 'odd_w_out': odd_w_out, 'attn_sink': attn_sink}


def reference(x, ln_g, ln_b, mlp_w1, mlp_w2, even_w_in, even_w_out,
              s5_lambda_re, s5_lambda_im, s5_log_step, s5_b_re, s5_b_im, s5_c_re, s5_c_im,
              s5_d, s5_w_glu, s5_b_glu, ret_log_decay, odd_w_in, odd_w_out, attn_sink):
    for layer in range(DEPTH):
        if layer % 2 == 0:
            e = layer // 2
            mix = _even_mixer(x, even_w_in[e], even_w_out[e], s5_lambda_re[e], s5_lambda_im[e],
                              s5_log_step[e], s5_b_re[e], s5_b_im[e], s5_c_re[e], s5_c_im[e],
                              s5_d[e], s5_w_glu[e], s5_b_glu[e], ret_log_decay[e])
        else:
            o = layer // 2
            mix = _odd_mixer(x, odd_w_in[o], odd_w_out[o], attn_sink[o])
        x = _layer_norm(DEEPNORM_ALPHA * x + mix, ln_g[layer, 0], ln_b[layer, 0])
        x = _layer_norm(DEEPNORM_ALPHA * x + _sq_relu_mlp(x, mlp_w1[layer], mlp_w2[layer]),
                        ln_g[layer, 1], ln_b[layer, 1])
    return x
```

```python
import numpy as np
from contextlib import ExitStack
import concourse.bass as bass
import concourse.mybir as mybir
from concourse.bass_utils import run_bass_kernel_spmd

F32 = mybir.dt.float32
BF16 = mybir.dt.bfloat16
ALU = mybir.AluOpType
AF = mybir.ActivationFunctionType
ENGS = ['pe', 'dve', 'act', 'pool', 'sp']

NCORES = 8
D = 2048
L = 8192
T = L // NCORES
TH = 512
ALPHA = float(4 ** 0.25)
LN_EPS = 1e-5


class Prog:
    def __init__(self):
        self.nc = bass.Bass("TRN2", target_bir_lowering=False)
        self.q = {e: [] for e in ENGS}
        self.cnt = {}
        self.seen = {e: {} for e in ENGS}
        self.lastw = {}
        self.readers = {}
        self.semkeys = []
        self.es = ExitStack()
        self.plan = False
        self.panels = []
        self.pi = 0
        self.issued = 0
        self.psi = 0

    def sb(self, name, shape, dt):
        return self.es.enter_context(self.nc.sbuf_tensor(name, shape, dt))

    def ps(self, name, shape, dt=F32):
        return self.es.enter_context(self.nc.psum_tensor(name, shape, dt))

    def dram_in(self, name, shape, dt=F32):
        return self.nc.dram_tensor(name, list(shape), dt, kind="ExternalInput").ap()

    def dram_out(self, name, shape, dt=F32):
        return self.nc.dram_tensor(name, list(shape), dt, kind="ExternalOutput").ap()

    def _deps(self, eng, reads, writes):
        deps = []
        for r in reads:
            t = self.lastw.get(r)
            if t: deps.append(t)
            if isinstance(r, tuple) and isinstance(r[0], str) and r[0].startswith('ps'):
                deps.extend(x for x in self.readers.get(r, []) if x[2] != eng)
        for w in writes:
            t = self.lastw.get(w)
            if t: deps.append(t)
            deps.extend(self.readers.get(w, []))
        waits = []
        for (sk, val, deng) in deps:
            if deng == eng and eng == 'pe':
                continue
            if self.seen[eng].get(sk, 0) >= val:
                continue
            self.seen[eng][sk] = val
            waits.append((sk, val))
        return waits

    def _bump(self, sk, inc):
        if sk not in self.cnt:
            self.cnt[sk] = 0
            self.semkeys.append(sk)
        self.cnt[sk] += inc
        return self.cnt[sk]

    def _record(self, tok, reads, writes):
        for w in writes:
            self.lastw[w] = tok
            self.readers[w] = []
        for r in reads:
            self.readers.setdefault(r, []).append(tok)

    def emit(self, eng, fn, reads=(), writes=()):
        if self.plan: return
        waits = self._deps(eng, reads, writes)
        sk = 'e_' + eng
        v = self._bump(sk, 1)
        self.q[eng].append((waits, fn, sk, 1))
        self._record((sk, v, eng), reads, writes)

    def dma(self, eng, fn, reads=(), writes=(), key=None):
        if self.plan: return
        waits = self._deps(eng, reads, writes)
        sk = 'd_' + str(key)
        v = self._bump(sk, 16)
        self.q[eng].append((waits, fn, sk, 16))
        self._record((sk, v, 'dma'), reads, writes)

    def barrier(self):
        if self.plan: return
        toks = []
        for sk in self.semkeys:
            toks.append((sk, self.cnt[sk]))
        for e in ENGS:
            waits = []
            for (sk, v) in toks:
                if self.seen[e].get(sk, 0) >= v: continue
                self.seen[e][sk] = v
                waits.append((sk, v))
            if waits:
                self.q[e].append((waits, None, None, 0))

    def tt(self, eng, out, in0, in1, op, reads, writes):
        self.emit(eng, lambda e: e.tensor_tensor(out=out, in0=in0, in1=in1, op=op), reads, writes)

    def ts(self, eng, out, in0, s1, s2, op0, op1, reads, writes):
        if op1 is None:
            self.emit(eng, lambda e: e.tensor_scalar(out=out, in0=in0, scalar1=s1, scalar2=None, op0=op0), reads, writes)
        else:
            self.emit(eng, lambda e: e.tensor_scalar(out=out, in0=in0, scalar1=s1, scalar2=s2, op0=op0, op1=op1), reads, writes)

    def stt(self, out, in0, scalar, in1, op0, op1, reads, writes):
        self.emit('dve', lambda e: e.scalar_tensor_tensor(out=out, in0=in0, scalar=scalar, in1=in1, op0=op0, op1=op1), reads, writes)

    def act(self, out, in_, func, reads, writes, bias=None, scale=None):
        kw = {}
        if bias is not None: kw['bias'] = bias
        if scale is not None: kw['scale'] = scale
        self.emit('act', lambda e: e.activation(out=out, in_=in_, func=func, **kw), reads, writes)

    def mm(self, out, lhsT, rhs, start, stop, reads, writes):
        self.emit('pe', lambda e: e.matmul(out, lhsT=lhsT, rhs=rhs, start=start, stop=stop), reads, writes)

    def finish(self, final_res):
        waits = self._deps('sp', final_res, ())
        self.q['sp'].append((waits, None, None, 0))
        nc = self.nc
        sems = {}
        for i, sk in enumerate(self.semkeys):
            sems[sk] = self.es.enter_context(nc.semaphore(f"s{i}"))
        with nc.Block() as block:
            def run(e, engobj):
                for (waits, fn, sk, inc) in self.q[e]:
                    for (wk, wv) in waits:
                        engobj.wait_ge(sems[wk], wv)
                    if fn is not None:
                        fn(engobj).then_inc(sems[sk], inc)

            @block.tensor
            def _(e): run('pe', e)

            @block.vector
            def _(e): run('dve', e)

            @block.scalar
            def _(e): run('act', e)

            @block.gpsimd
            def _(e): run('pool', e)

            @block.sync
            def _(e): run('sp', e)
        self.es.close()
        return nc


class TokProg(Prog):
    NSLOT = 2
    WELEMS = 8192

    def dcp(self, eng, out, in_, reads, writes, key):
        self.dma(eng, lambda e: e.dma_start(out=out, in_=in_), reads=reads, writes=writes, key=key)

    def setup_common(self):
        self.wbuf = [self.sb(f"wbuf{i}", [128, self.WELEMS], BF16) for i in range(self.NSLOT)]
        self.pss = [self.ps(f"ps{i}", [128, 512]) for i in range(8)]
        self.ones32 = self.sb("ones32", [128, 128], F32)
        self.sq = [self.sb(f"sq{i}", [128, TH], F32) for i in range(2)]
        self.tmp = [self.sb(f"tmp{i}", [128, TH], F32) for i in range(2)]
        self.mean = self.sb("mean", [128, TH], F32)
        self.m2 = self.sb("m2", [128, TH], F32)
        self.rstd = self.sb("rstd", [128, TH], F32)
        self.sqi = 0
        self.tmpi = 0

    def start(self):
        self.psi = 0
        self.emit('dve', lambda e: e.memset(self.ones32[:, :], 1.0), writes=[('ones32',)])

    def next_ps(self):
        i = self.psi % 4
        self.psi += 1
        return self.pss[i], ('ps', i)

    def issue_panel(self, i):
        (w_ap, KC, c0, CW) = self.panels[i]
        slot = i % self.NSLOT
        wv = self.wbuf[slot][:, 0:KC * CW].rearrange("p (kc n) -> p kc n", kc=KC)
        src = w_ap[:, c0:c0 + CW].rearrange("(kc p) n -> p kc n", p=128)
        self.dma('pool', lambda e: e.dma_start(out=wv, in_=src), writes=[('wb', slot)], key=('wb', slot))

    def next_panel(self, w_ap, KC, c0, CW):
        if self.plan:
            self.panels.append((w_ap, KC, c0, CW))
            return None
        i = self.pi
        self.pi += 1
        assert self.panels[i][1:] == (KC, c0, CW)
        while self.issued < min(len(self.panels), i + self.NSLOT):
            self.issue_panel(self.issued)
            self.issued += 1
        slot = i % self.NSLOT
        return self.wbuf[slot][:, 0:KC * CW].rearrange("p (kc n) -> p kc n", kc=KC), slot

    def lin(self, w_ap, K, N, rhs_fn, rhs_res, evac_fn, tw=TH):
        KC = K // 128
        CW = min(N, self.WELEMS // KC, 512)
        for c0 in range(0, N, CW):
            r = self.next_panel(w_ap, KC, c0, CW)
            if self.plan: continue
            wv, slot = r
            for nn in range(CW // 128):
                n = c0 // 128 + nn
                ps, pr = self.next_ps()
                for kc in range(KC):
                    self.mm(ps[:, 0:tw], wv[:, kc, nn * 128:(nn + 1) * 128], rhs_fn(kc), kc == 0, kc == KC - 1,
                            reads=[('wb', slot), rhs_res(kc)], writes=[pr])
                evac_fn(n, ps, pr)

    def stats(self, chunks, eps):
        nck = len(chunks)
        ps1, pr1 = self.pss[4], ('ps', 4)
        ps2, pr2 = self.pss[5], ('ps', 5)
        for kc, (ap, rs) in enumerate(chunks):
            self.mm(ps1[:, :], self.ones32[:, :], ap, kc == 0, kc == nck - 1, reads=[('ones32',), rs], writes=[pr1])
        for kc, (ap, rs) in enumerate(chunks):
            sq = self.sq[self.sqi % 2]; sqr = ('sq', self.sqi % 2); self.sqi += 1
            self.act(sq[:, :], ap, AF.Square, reads=[rs], writes=[sqr])
            self.mm(ps2[:, :], self.ones32[:, :], sq[:, :], kc == 0, kc == nck - 1, reads=[('ones32',), sqr], writes=[pr2])
        inv = 1.0 / (nck * 128)
        self.act(self.mean[:, :], ps1[:, :], AF.Copy, reads=[pr1], writes=[('mean',)], scale=inv)
        self.tt('dve', self.m2[:, :], self.mean[:, :], self.mean[:, :], ALU.mult, reads=[('mean',)], writes=[('m2',)])
        self.stt(self.m2[:, :], ps2[:, :], inv, self.m2[:, :], ALU.mult, ALU.subtract, reads=[pr2, ('m2',)], writes=[('m2',)])
        self.ts('dve', self.m2[:, :], self.m2[:, :], eps, None, ALU.add, None, reads=[('m2',)], writes=[('m2',)])
        self.act(self.m2[:, :], self.m2[:, :], AF.Sqrt, reads=[('m2',)], writes=[('m2',)])
        self.emit('dve', lambda e: e.reciprocal(out=self.rstd[:, :], in_=self.m2[:, :]), reads=[('m2',)], writes=[('rstd',)])

    def layernorm(self, S, sres, Dst, dres, Bd, bres, g_ap, b_ap, nck=16):
        self.stats([(S[:, kc, :], sres(kc)) for kc in range(nck)], LN_EPS)
        for kc in range(nck):
            tmp = self.tmp[self.tmpi % 2]; tr = ('tmp', self.tmpi % 2); self.tmpi += 1
            self.tt('dve', tmp[:, :], S[:, kc, :], self.mean[:, :], ALU.subtract, reads=[sres(kc), ('mean',)], writes=[tr])
            self.stt(tmp[:, :], tmp[:, :], g_ap[:, kc:kc + 1], self.rstd[:, :], ALU.mult, ALU.mult,
                     reads=[tr, ('rstd',), ('lnp',)], writes=[tr])
            self.act(Dst[:, kc, :], tmp[:, :], AF.Identity, reads=[tr, ('lnp',)], writes=[dres(kc)], bias=b_ap[:, kc:kc + 1])
            if Bd is not None:
                self.act(Bd[:, kc, :], Dst[:, kc, :], AF.Copy, reads=[dres(kc)], writes=[bres(kc)])


def build_tail(layer_has_inproj=False):
    P = TokProg()
    xT = P.dram_in("xT", [D, T])
    aT = P.dram_in("aT", [D, T])
    w_out = P.dram_in("w_out", [D, D])
    w1 = P.dram_in("w1", [D, 4 * D])
    w2 = P.dram_in("w2", [4 * D, D])
    lnp = P.dram_in("lnp", [128, 4, 16])
    yT = P.dram_out("yT", [D, T])

    X = P.sb("X", [128, 16, TH], F32)
    Rb = P.sb("R", [128, 16, TH], F32)
    Bb = P.sb("Bb", [128, 16, TH], BF16)
    H = P.sb("H", [128, 64, TH], BF16)
    lnp_sb = P.sb("lnp_sb", [128, 4, 16], F32)
    P.setup_common()
    xres = lambda kc: ('X', kc)
    rres = lambda kc: ('R', kc)
    bres = lambda kc: ('B', kc)
    hres = lambda kc: ('H', kc)
    xTv = xT.rearrange("(kc p) t -> p kc t", p=128)
    aTv = aT.rearrange("(kc p) t -> p kc t", p=128)
    yTv = yT.rearrange("(kc p) t -> p kc t", p=128)

    def body():
        P.start()
        P.dcp('sp', lnp_sb[:, :, :], lnp[:, :, :], (), [('lnp',)], 'lnp')
        outs = []
        for half in range(T // TH):
            t0 = half * TH
            for kc in range(16):
                P.dcp('sp', X[:, kc, :], xTv[:, kc, t0:t0 + TH], (), [xres(kc)], ('X', kc))
                P.dcp('sp', Rb[:, kc, :], aTv[:, kc, t0:t0 + TH], (), [rres(kc)], ('R', kc))
                P.act(Bb[:, kc, :], Rb[:, kc, :], AF.Copy, reads=[rres(kc)], writes=[bres(kc)])

            def ev1(n, ps, pr):
                P.stt(Rb[:, n, :], X[:, n, :], ALPHA, ps[:, :], ALU.mult, ALU.add, reads=[xres(n), pr], writes=[rres(n)])
            P.lin(w_out, D, D, lambda kc: Bb[:, kc, :], bres, ev1)
            P.layernorm(Rb, rres, X, xres, Bb, bres, lnp_sb[:, 0, :], lnp_sb[:, 1, :])

            def ev2(n, ps, pr):
                P.act(H[:, n, :], ps[:, :], AF.Relu, reads=[pr], writes=[hres(n)])
                P.tt('dve', H[:, n, :], H[:, n, :], H[:, n, :], ALU.mult, reads=[hres(n)], writes=[hres(n)])
            P.lin(w1, D, 4 * D, lambda kc: Bb[:, kc, :], bres, ev2)

            def ev3(n, ps, pr):
                P.stt(X[:, n, :], X[:, n, :], ALPHA, ps[:, :], ALU.mult, ALU.add, reads=[xres(n), pr], writes=[xres(n)])
            P.lin(w2, 4 * D, D, lambda kc: H[:, kc, :], hres, ev3)
            P.layernorm(X, xres, Rb, rres, None, None, lnp_sb[:, 2, :], lnp_sb[:, 3, :])
            for kc in range(16):
                P.dcp('sp', yTv[:, kc, t0:t0 + TH], Rb[:, kc, :], [rres(kc)], [('out', half, kc)], ('R', kc))
                outs.append(('out', half, kc))
        return outs

    P.plan = True
    body()
    P.plan = False
    outs = body()
    return P.finish(outs)


def lnp_layout(ln_g, ln_b, layer):
    a = np.stack([ln_g[layer, 0], ln_b[layer, 0], ln_g[layer, 1], ln_b[layer, 1]], 0)
    return np.ascontiguousarray(a.reshape(4, 16, 128).transpose(2, 0, 1))


def run_tail(x_tok, a_tok, w_out, w1, w2, lnp):
    nc = build_tail()
    in_maps = []
    for c in range(NCORES):
        sl = slice(c * T, (c + 1) * T)
        in_maps.append({"xT": np.ascontiguousarray(x_tok[sl].T), "aT": np.ascontiguousarray(a_tok[sl].T),
                        "w_out": w_out, "w1": w1, "w2": w2, "lnp": lnp})
    res = run_bass_kernel_spmd(nc, in_maps, core_ids=list(range(NCORES)))
    return np.concatenate([r["yT"].T for r in res.results], 0)


def rope_tables(pos, half, theta):
    inv_freq = (np.float32(1.0) / np.power(np.float32(theta), (np.arange(half, dtype=np.float32) / np.float32(half)))).astype(np.float32)
    ang = (pos.astype(np.float32)[None, :] * inv_freq[:, None]).astype(np.float32)
    return np.cos(ang.astype(np.float64)).astype(np.float32), np.sin(ang.astype(np.float64)).astype(np.float32)


def build_inproj_even():
    P = TokProg()
    NOUT = 5120
    xT = P.dram_in("xT", [D, T])
    w_in = P.dram_in("w_in", [D, NOUT])
    cosT = P.dram_in("cosT", [128, T])
    sinT = P.dram_in("sinT", [128, T])
    hT = P.dram_out("hT", [NOUT, T])
    Rb = P.sb("R", [128, 16, TH], F32)
    Bb = P.sb("Bb", [128, 16, TH], BF16)
    O = P.sb("O", [128, 40, TH], F32)
    cs = P.sb("cs", [128, 2, T], F32)
    P.setup_common()
    rres = lambda kc: ('R', kc)
    bres = lambda kc: ('B', kc)
    xTv = xT.rearrange("(kc p) t -> p kc t", p=128)
    hTv = hT.rearrange("(kc p) t -> p kc t", p=128)

    def body():
        P.start()
        P.dcp('sp', cs[:, 0, :], cosT[:, :], (), [('cs',)], 'cs0')
        P.dcp('sp', cs[:, 1, :], sinT[:, :], (), [('cs',)], 'cs1')
        outs = []
        for half in range(T // TH):
            t0 = half * TH
            for kc in range(16):
                P.dcp('sp', Rb[:, kc, :], xTv[:, kc, t0:t0 + TH], (), [rres(kc)], ('R', kc))
                P.act(Bb[:, kc, :], Rb[:, kc, :], AF.Copy, reads=[rres(kc)], writes=[bres(kc)])
            pend = {}
            cosv = cs[:, 0, t0:t0 + TH]
            sinv = cs[:, 1, t0:t0 + TH]

            def ev(n, ps, pr):
                ores = ('O', n)
                if 8 <= n < 24:
                    if n % 2 == 0:
                        pend['a'] = (ps, pr)
                        return
                    pa, pra = pend['a']
                    t1 = P.tmp[0]; t2 = P.tmp[1]
                    P.tt('dve', t1[:, :], pa[:, :], cosv, ALU.mult, reads=[pra, ('cs',)], writes=[('tmp', 0)])
                    P.tt('dve', t2[:, :], ps[:, :], sinv, ALU.mult, reads=[pr, ('cs',)], writes=[('tmp', 1)])
                    P.tt('dve', O[:, n - 1, :], t1[:, :], t2[:, :], ALU.subtract, reads=[('tmp', 0), ('tmp', 1)], writes=[('O', n - 1)])
                    P.tt('dve', t1[:, :], ps[:, :], cosv, ALU.mult, reads=[pr, ('cs',)], writes=[('tmp', 0)])
                    P.tt('dve', t2[:, :], pa[:, :], sinv, ALU.mult, reads=[pra, ('cs',)], writes=[('tmp', 1)])
                    P.tt('dve', O[:, n, :], t1[:, :], t2[:, :], ALU.add, reads=[('tmp', 0), ('tmp', 1)], writes=[('O', n)])
                    for m in (n - 1, n):
                        P.dcp('sp', hTv[:, m, t0:t0 + TH], O[:, m, :], [('O', m)], [('out', half, m)], ('O', m))
                        outs.append(('out', half, m))
                else:
                    P.act(O[:, n, :], ps[:, :], AF.Copy, reads=[pr], writes=[ores])
                    P.dcp('sp', hTv[:, n, t0:t0 + TH], O[:, n, :], [ores], [('out', half, n)], ('O', n))
                    outs.append(('out', half, n))
            P.lin(w_in, D, NOUT, lambda kc: Bb[:, kc, :], bres, ev)
        return outs

    P.plan = True
    body()
    P.plan = False
    outs = body()
    return P.finish(outs)


def run_inproj_even(x_tok, w_in):
    nc = build_inproj_even()
    in_maps = []
    for c in range(NCORES):
        sl = slice(c * T, (c + 1) * T)
        cosT, sinT = rope_tables(np.arange(c * T, (c + 1) * T), 128, 10000.0)
        in_maps.append({"xT": np.ascontiguousarray(x_tok[sl].T), "w_in": w_in, "cosT": cosT, "sinT": sinT})
    res = run_bass_kernel_spmd(nc, in_maps, core_ids=list(range(NCORES)))
    return np.concatenate([r["hT"].T for r in res.results], 0)


NB5 = L // TH


def s5_disc(P, pre, F):
    names = ['step', 'lr', 'ex', 'mag', 'th', 's', 'c', 't1', 't2', 's2', 'c2', 'ar', 'ai', 'nr', 'den', 'zr', 'zi', 't3']
    tl = {n: P.sb(f"{pre}_{n}", [128, F], F32) for n in names}
    r = lambda n: (pre, n)
    A = lambda n: tl[n][:, :]

    def run(lamre, lamim, lstep):
        P.act(A('step'), lstep, AF.Exp, reads=[(pre, 'in')], writes=[r('step')])
        P.ts('dve', A('lr'), lamre, -1e-4, None, ALU.min, None, reads=[(pre, 'in')], writes=[r('lr')])
        P.tt('dve', A('ex'), A('lr'), A('step'), ALU.mult, reads=[r('lr'), r('step')], writes=[r('ex')])
        P.act(A('mag'), A('ex'), AF.Exp, reads=[r('ex')], writes=[r('mag')])
        P.tt('dve', A('th'), lamim, A('step'), ALU.mult, reads=[(pre, 'in'), r('step')], writes=[r('th')])
        P.act(A('s'), A('th'), AF.Sin, reads=[r('th')], writes=[r('s')], scale=1.0 / 16)
        P.act(A('c'), A('th'), AF.Sin, reads=[r('th'), ('halfpi',)], writes=[r('c')], scale=1.0 / 16, bias=P.halfpi[:, 0:1])
        cs, ss = 'c', 's'
        for it in range(4):
            cn, sn = ('c2', 's2') if it % 2 == 0 else ('c', 's')
            P.tt('dve', A('t1'), A(cs), A(cs), ALU.mult, reads=[r(cs)], writes=[r('t1')])
            P.tt('dve', A('t2'), A(ss), A(ss), ALU.mult, reads=[r(ss)], writes=[r('t2')])
            P.stt(A(sn), A(cs), 2.0, A(ss), ALU.mult, ALU.mult, reads=[r(cs), r(ss)], writes=[r(sn)])
            P.tt('dve', A(cn), A('t1'), A('t2'), ALU.subtract, reads=[r('t1'), r('t2')], writes=[r(cn)])
            cs, ss = cn, sn
        assert cs == 'c'
        P.tt('dve', A('ar'), A('mag'), A('c'), ALU.mult, reads=[r('mag'), r('c')], writes=[r('ar')])
        P.tt('dve', A('ai'), A('mag'), A('s'), ALU.mult, reads=[r('mag'), r('s')], writes=[r('ai')])
        P.ts('dve', A('nr'), A('ar'), -1.0, None, ALU.add, None, reads=[r('ar')], writes=[r('nr')])
        P.tt('dve', A('t1'), A('lr'), A('lr'), ALU.mult, reads=[r('lr')], writes=[r('t1')])
        P.tt('dve', A('t2'), lamim, lamim, ALU.mult, reads=[(pre, 'in')], writes=[r('t2')])
        P.tt('dve', A('den'), A('t1'), A('t2'), ALU.add, reads=[r('t1'), r('t2')], writes=[r('den')])
        P.emit('dve', lambda e: e.reciprocal(out=A('den'), in_=A('den')), reads=[r('den')], writes=[r('den')])
        P.tt('dve', A('t1'), A('nr'), A('lr'), ALU.mult, reads=[r('nr'), r('lr')], writes=[r('t1')])
        P.tt('dve', A('t2'), A('ai'), lamim, ALU.mult, reads=[r('ai'), (pre, 'in')], writes=[r('t2')])
        P.tt('dve', A('t3'), A('t1'), A('t2'), ALU.add, reads=[r('t1'), r('t2')], writes=[r('t3')])
        P.tt('dve', A('zr'), A('t3'), A('den'), ALU.mult, reads=[r('t3'), r('den')], writes=[r('zr')])
        P.tt('dve', A('t1'), A('ai'), A('lr'), ALU.mult, reads=[r('ai'), r('lr')], writes=[r('t1')])
        P.tt('dve', A('t2'), A('nr'), lamim, ALU.mult, reads=[r('nr'), (pre, 'in')], writes=[r('t2')])
        P.tt('dve', A('t3'), A('t1'), A('t2'), ALU.subtract, reads=[r('t1'), r('t2')], writes=[r('t3')])
        P.tt('dve', A('zi'), A('t3'), A('den'), ALU.mult, reads=[r('t3'), r('den')], writes=[r('zi')])
    return tl, run


def build_s5():
    P = Prog()
    uT = P.dram_in("uT", [2, 128, L])
    prow = P.dram_in("prow", [128, 3, 1024])
    pcol = P.dram_in("pcol", [128, 3, 8])
    BT = P.dram_in("BT", [128, 2, 1024])
    CT = P.dram_in("CT", [128, 2, 8, 128])
    dsk = P.dram_in("dsk", [128, 1])
    yT = P.dram_out("yT", [2, 128, L])

    u32 = [[P.sb(f"u32_{d}_{i}", [128, TH], F32) for i in range(2)] for d in range(2)]
    ub = [[P.sb(f"ub{d}_{i}", [128, TH], BF16) for i in range(2)] for d in range(2)]
    prow_sb = P.sb("prow_sb", [128, 3, 1024], F32)
    pcol_sb = P.sb("pcol_sb", [128, 3, 8], F32)
    BT_sb = P.sb("BT_sb", [128, 2, 1024], F32)
    CT_sb = P.sb("CT_sb", [128, 2, 8, 128], F32)
    dsk_sb = P.sb("dsk_sb", [128, 1], F32)
    P.halfpi = P.sb("halfpi", [128, 1], F32)
    Bbar = P.sb("Bbar", [128, 2, 1024], BF16)
    Cb = P.sb("Cb", [128, 2, 8, 128], BF16)
    tw = [P.sb(f"tw{i}", [128, 512], F32) for i in range(2)]
    tabc = P.sb("tabc", [128, 8, TH], F32)
    tabs = P.sb("tabs", [128, 8, TH], F32)
    magT = P.sb("magT", [128, 8, TH], F32)
    cur = [P.sb(f"cur{i}", [128, 2, 8], F32) for i in range(2)]
    curt = P.sb("curt", [128, 2, 8], F32)
    ttmp = P.sb("ttmp", [128, TH], F32)
    init = P.sb("init", [128, 2, 8], F32)
    ini_t = P.sb("ini_t", [128, 2, 8], F32)
    NW = 2
    m = [[P.sb(f"m{i}_{k}", [128, TH], F32) for k in range(4)] for i in range(NW)]
    v = [[P.sb(f"v{i}_{k}", [128, TH], F32) for k in range(2)] for i in range(NW)]
    w = [[P.sb(f"w{i}_{k}", [128, TH], F32) for k in range(2)] for i in range(NW)]
    pm = [[P.sb(f"pm{i}_{k}", [128, TH], F32) for k in range(4)] for i in range(NW)]
    hb = [[P.sb(f"hb{i}_{k}", [128, TH], BF16) for k in range(2)] for i in range(NW)]
    yo = [P.sb(f"yo{i}", [128, TH], F32) for i in range(2)]
    psb = [P.ps(f"psb{i}", [128, 512]) for i in range(4)]
    psy = [P.ps(f"psy{i}", [128, 512]) for i in range(2)]

    rowt, rowrun = s5_disc(P, 'row', 512)
    colt, colrun = s5_disc(P, 'col', 8)

    P.dcp = lambda eng, out, in_, reads, writes, key: P.dma(eng, lambda e: e.dma_start(out=out, in_=in_), reads=reads, writes=writes, key=key)
    P.dcp('sp', prow_sb[:, :, :], prow[:, :, :], (), [('row', 'in')], 'prow')
    P.dcp('sp', pcol_sb[:, :, :], pcol[:, :, :], (), [('col', 'in')], 'pcol')
    P.dcp('sp', BT_sb[:, :, :], BT[:, :, :], (), [('BT',)], 'BT')
    P.dcp('sp', CT_sb[:, :, :, :], CT[:, :, :, :], (), [('CT',)], 'CT')
    P.dcp('sp', dsk_sb[:, :], dsk[:, :], (), [('dsk',)], 'dsk')
    P.emit('dve', lambda e: e.memset(P.halfpi[:, :], float(np.pi / 2)), writes=[('halfpi',)])
    colrun(pcol_sb[:, 0, :], pcol_sb[:, 1, :], pcol_sb[:, 2, :])
    for d in range(2):
        ds_ = slice(d * 512, (d + 1) * 512)
        rowrun(prow_sb[:, 0, ds_], prow_sb[:, 1, ds_], prow_sb[:, 2, ds_])
        zr, zi = rowt['zr'][:, :], rowt['zi'][:, :]
        zres = [('row', 'zr'), ('row', 'zi'), ('BT',)]
        P.tt('dve', tw[0][:, :], zr, BT_sb[:, 0, ds_], ALU.mult, reads=zres, writes=[('tw', 0)])
        P.tt('dve', tw[1][:, :], zi, BT_sb[:, 1, ds_], ALU.mult, reads=zres, writes=[('tw', 1)])
        P.tt('dve', Bbar[:, 0, ds_], tw[0][:, :], tw[1][:, :], ALU.subtract, reads=[('tw', 0), ('tw', 1)], writes=[('Bbar',)])
        P.tt('dve', tw[0][:, :], zr, BT_sb[:, 1, ds_], ALU.mult, reads=zres + [('Bbar',)], writes=[('tw', 0)])
        P.tt('dve', tw[1][:, :], zi, BT_sb[:, 0, ds_], ALU.mult, reads=zres + [('Bbar',)], writes=[('tw', 1)])
        P.tt('dve', Bbar[:, 1, ds_], tw[0][:, :], tw[1][:, :], ALU.add, reads=[('tw', 0), ('tw', 1)], writes=[('Bbar',)])
    P.act(Cb[:, 0, :, :], CT_sb[:, 0, :, :], AF.Copy, reads=[('CT',)], writes=[('Cb',)])
    P.act(Cb[:, 1, :, :], CT_sb[:, 1, :, :], AF.Copy, reads=[('CT',)], writes=[('Cb',)], scale=-1.0)
    for dj in range(8):
        P.emit('dve', lambda e, dj=dj: e.memset(magT[:, dj, :], 1.0), writes=[('magT',)])
        P.ts('dve', magT[:, dj, :], magT[:, dj, :], colt['mag'][:, dj:dj + 1], None, ALU.mult, None,
             reads=[('magT',), ('col', 'mag')], writes=[('magT',)])
    P.emit('dve', lambda e: e.memset(tabc[:, :, 0:1], 1.0), writes=[('tab',)])
    P.emit('dve', lambda e: e.memset(tabs[:, :, 0:1], 0.0), writes=[('tab',)])
    P.emit('dve', lambda e: e.tensor_copy(out=cur[0][:, 0, :], in_=colt['c'][:, :]), reads=[('col', 'c')], writes=[('cur', 0)])
    P.emit('dve', lambda e: e.tensor_copy(out=cur[0][:, 1, :], in_=colt['s'][:, :]), reads=[('col', 's')], writes=[('cur', 0)])
    ci = 0
    for k in range(9):
        n = 1 << k
        cc = cur[ci]
        for dj in range(8):
            cs_c = cc[:, 0, dj:dj + 1]; cs_s = cc[:, 1, dj:dj + 1]
            rd = [('tab',), ('cur', ci)]
            P.ts('dve', ttmp[:, 0:n], tabs[:, dj, 0:n], cs_s, None, ALU.mult, None, reads=rd, writes=[('ttmp',)])
            P.stt(tabc[:, dj, n:2 * n], tabc[:, dj, 0:n], cs_c, ttmp[:, 0:n], ALU.mult, ALU.subtract, reads=rd + [('ttmp',)], writes=[('tab',)])
            P.ts('dve', ttmp[:, 0:n], tabs[:, dj, 0:n], cs_c, None, ALU.mult, None, reads=rd, writes=[('ttmp',)])
            P.stt(tabs[:, dj, n:2 * n], tabc[:, dj, 0:n], cs_s, ttmp[:, 0:n], ALU.mult, ALU.add, reads=rd + [('ttmp',)], writes=[('tab',)])
        cn = cur[1 - ci]
        P.tt('dve', curt[:, 0, :], cc[:, 0, :], cc[:, 0, :], ALU.mult, reads=[('cur', ci)], writes=[('curt',)])
        P.tt('dve', curt[:, 1, :], cc[:, 1, :], cc[:, 1, :], ALU.mult, reads=[('cur', ci)], writes=[('curt',)])
        P.tt('dve', cn[:, 0, :], curt[:, 0, :], curt[:, 1, :], ALU.subtract, reads=[('curt',)], writes=[('cur', 1 - ci)])
        P.stt(cn[:, 1, :], cc[:, 0, :], 2.0, cc[:, 1, :], ALU.mult, ALU.mult, reads=[('cur', ci)], writes=[('cur', 1 - ci)])
        ci = 1 - ci
    Rc = cur[ci]
    Rres = ('cur', ci)
    P.emit('dve', lambda e: e.memset(init[:, :, :], 0.0), writes=[('init',)])

    outs = []
    wi_ = 0
    pbi = 0
    for b in range(NB5):
        t0 = b * TH
        ui = b % 2
        for d in range(2):
            P.dcp('sp', u32[d][ui][:, :], uT[d, :, t0:t0 + TH], (), [('u32', d, ui)], ('u32', d, ui))
            P.act(ub[d][ui][:, :], u32[d][ui][:, :], AF.Copy, reads=[('u32', d, ui)], writes=[('ub', d, ui)])
        for d in range(2):
            py = psy[d]; pyr = ('psy', d)
            for j in range(4):
                dj = d * 4 + j
                i = wi_ % NW; wi_ += 1
                pr_ = psb[(pbi * 2) % 4]; pi_ = psb[(pbi * 2 + 1) % 4]
                prr = ('psb', (pbi * 2) % 4); pir = ('psb', (pbi * 2 + 1) % 4); pbi += 1
                col0 = d * 512 + j * 128
                P.mm(pr_[:, :], Bbar[:, 0, col0:col0 + 128], ub[d][ui][:, :], True, True, reads=[('Bbar',), ('ub', d, ui)], writes=[prr])
                P.mm(pi_[:, :], Bbar[:, 1, col0:col0 + 128], ub[d][ui][:, :], True, True, reads=[('Bbar',), ('ub', d, ui)], writes=[pir])
                tc = tabc[:, dj, :]; tsn = tabs[:, dj, :]
                mr = lambda k: ('m', i, k)
                P.tt('dve', m[i][0][:, :], pr_[:, :], tc, ALU.mult, reads=[prr, ('tab',)], writes=[mr(0)])
                P.tt('dve', m[i][1][:, :], pi_[:, :], tsn, ALU.mult, reads=[pir, ('tab',)], writes=[mr(1)])
                P.tt('dve', m[i][2][:, :], pi_[:, :], tc, ALU.mult, reads=[pir, ('tab',)], writes=[mr(2)])
                P.tt('dve', m[i][3][:, :], pr_[:, :], tsn, ALU.mult, reads=[prr, ('tab',)], writes=[mr(3)])
                P.tt('pool', v[i][0][:, :], m[i][0][:, :], m[i][1][:, :], ALU.add, reads=[mr(0), mr(1)], writes=[('v', i, 0)])
                P.tt('pool', v[i][1][:, :], m[i][2][:, :], m[i][3][:, :], ALU.subtract, reads=[mr(2), mr(3)], writes=[('v', i, 1)])
                for k in range(2):
                    P.emit('dve', lambda e, i=i, k=k, dj=dj: e.tensor_tensor_scan(
                        out=w[i][k][:, :], data0=magT[:, dj, :], data1=v[i][k][:, :], initial=init[:, k, dj:dj + 1],
                        op0=ALU.mult, op1=ALU.add), reads=[('magT',), ('v', i, k), ('init',)], writes=[('w', i, k)])
                wl_r = w[i][0][:, TH - 1:TH]; wl_i = w[i][1][:, TH - 1:TH]
                wres = [('w', i, 0), ('w', i, 1), Rres]
                P.tt('dve', ini_t[:, 0, dj:dj + 1], wl_i, Rc[:, 1, dj:dj + 1], ALU.mult, reads=wres, writes=[('ini_t',)])
                P.tt('dve', ini_t[:, 1, dj:dj + 1], wl_i, Rc[:, 0, dj:dj + 1], ALU.mult, reads=wres, writes=[('ini_t',)])
                P.stt(init[:, 0, dj:dj + 1], wl_r, Rc[:, 0, dj:dj + 1], ini_t[:, 0, dj:dj + 1], ALU.mult, ALU.subtract,
                      reads=wres + [('ini_t',)], writes=[('init',)])
                P.stt(init[:, 1, dj:dj + 1], wl_r, Rc[:, 1, dj:dj + 1], ini_t[:, 1, dj:dj + 1], ALU.mult, ALU.add,
                      reads=wres + [('ini_t',)], writes=[('init',)])
                pmr = lambda k: ('pm', i, k)
                P.tt('pool', pm[i][0][:, :], w[i][0][:, :], tc, ALU.mult, reads=[('w', i, 0), ('tab',)], writes=[pmr(0)])
                P.tt('pool', pm[i][1][:, :], w[i][1][:, :], tsn, ALU.mult, reads=[('w', i, 1), ('tab',)], writes=[pmr(1)])
                P.tt('pool', pm[i][2][:, :], w[i][0][:, :], tsn, ALU.mult, reads=[('w', i, 0), ('tab',)], writes=[pmr(2)])
                P.tt('pool', pm[i][3][:, :], w[i][1][:, :], tc, ALU.mult, reads=[('w', i, 1), ('tab',)], writes=[pmr(3)])
                P.tt('pool', hb[i][0][:, :], pm[i][0][:, :], pm[i][1][:, :], ALU.subtract, reads=[pmr(0), pmr(1)], writes=[('hb', i, 0)])
                P.tt('pool', hb[i][1][:, :], pm[i][2][:, :], pm[i][3][:, :], ALU.add, reads=[pmr(2), pmr(3)], writes=[('hb', i, 1)])
                P.mm(py[:, :], Cb[:, 0, dj, :], hb[i][0][:, :], j == 0, False, reads=[('Cb',), ('hb', i, 0)], writes=[pyr])
                P.mm(py[:, :], Cb[:, 1, dj, :], hb[i][1][:, :], False, j == 3, reads=[('Cb',), ('hb', i, 1)], writes=[pyr])
            yr = ('yo', d)
            if d == 0:
                P.stt(yo[d][:, :], u32[0][ui][:, :], dsk_sb[:, 0:1], py[:, :], ALU.mult, ALU.add,
                      reads=[('u32', 0, ui), ('dsk',), pyr], writes=[yr])
            else:
                P.act(yo[d][:, :], py[:, :], AF.Copy, reads=[pyr], writes=[yr])
            P.dcp('sp', yT[d, :, t0:t0 + TH], yo[d][:, :], [yr], [('out', b, d)], ('yo', d))
            outs.append(('out', b, d))
    return P.finish(outs)


def s5_host_layout(inp, c):
    g0 = 8 * c
    lamre = inp['s5_lambda_re'][0][:, g0:g0 + 8, :]
    lamim = inp['s5_lambda_im'][0][:, g0:g0 + 8, :]
    lstep = np.broadcast_to(inp['s5_log_step'][0][:, g0:g0 + 8, None], (2, 8, 64))
    st = np.stack([lamre, lamim, lstep], 0).reshape(3, 2, 512)
    prow = np.ascontiguousarray(np.broadcast_to(st.reshape(1, 3, 1024), (128, 3, 1024))).astype(np.float32)
    pcol = np.ascontiguousarray(st.reshape(3, 2, 4, 128).transpose(3, 0, 1, 2).reshape(128, 3, 8)).astype(np.float32)
    BT = np.zeros((128, 2, 2, 512), np.float32)
    CT = np.zeros((128, 2, 2, 4, 128), np.float32)
    for ri, (bk, ck) in enumerate([('s5_b_re', 's5_c_re'), ('s5_b_im', 's5_c_im')]):
        Bm = inp[bk][0][:, g0:g0 + 8]
        Cm = inp[ck][0][:, g0:g0 + 8]
        for d in range(2):
            for g in range(8):
                BT[g * 16:(g + 1) * 16, ri, d, g * 64:(g + 1) * 64] = Bm[d, g].T
                j, gg = g // 2, g % 2
                CT[gg * 64:(gg + 1) * 64, ri, d, j, g * 16:(g + 1) * 16] = Cm[d, g].T
    dsk = inp['s5_d'][0][g0:g0 + 8].reshape(128, 1).astype(np.float32)
    return {"prow": prow, "pcol": pcol, "BT": BT.reshape(128, 2, 1024), "CT": CT.reshape(128, 2, 8, 128), "dsk": np.ascontiguousarray(dsk)}


def run_s5(u_tok, inp):
    nc = build_s5()
    in_maps = []
    for c in range(NCORES):
        uc = u_tok[:, c * 128:(c + 1) * 128].T
        m = s5_host_layout(inp, c)
        m["uT"] = np.ascontiguousarray(np.stack([uc, uc[:, ::-1]], 0))
        in_maps.append(m)
    res = run_bass_kernel_spmd(nc, in_maps, core_ids=list(range(NCORES)))
    yf = np.concatenate([r["yT"][0].T for r in res.results], 1)
    yb = np.concatenate([r["yT"][1][:, ::-1].T for r in res.results], 1)
    return yf, yb


def build_ret():
    P = Prog()
    NCH = L // 128
    qT = P.dram_in("qT", [2, 128, L])
    kT = P.dram_in("kT", [2, 128, L])
    ktok = P.dram_in("ktok", [L, 256])
    vtok = P.dram_in("vtok", [L, 256])
    lg = P.dram_in("lg", [128, 1])
    maskT = P.dram_in("maskT", [128, 128])
    irow = P.dram_in("irow", [128, TH])
    icol = P.dram_in("icol", [128, 1])
    o = P.dram_out("o", [L, 256])
    P.dcp = lambda eng, out, in_, reads, writes, key: P.dma(eng, lambda e: e.dma_start(out=out, in_=in_), reads=reads, writes=writes, key=key)

    lg_sb = P.sb("lg_sb", [128, 1], F32)
    lgn = P.sb("lgn", [128, 1], F32)
    lgp = P.sb("lgp", [128, 1], F32)
    mask_sb = P.sb("mask_sb", [128, 128], F32)
    irow_sb = P.sb("irow_sb", [128, TH], F32)
    icol_sb = P.sb("icol_sb", [128, 1], F32)
    gq = P.sb("gq", [128, TH], F32)
    gk = P.sb("gk", [128, TH], F32)
    gkc = P.sb("gkc", [128, 1], F32)
    gC = P.sb("gC", [128, 1], F32)
    c128 = P.sb("c128", [128, 1], F32)
    q32 = [P.sb(f"q32_{i}", [128, 2, TH], F32) for i in range(2)]
    k32 = [P.sb(f"k32_{i}", [128, 2, TH], F32) for i in range(2)]
    qb = [P.sb(f"qb_{i}", [128, 2, TH], BF16) for i in range(2)]
    kb = [P.sb(f"kb_{i}", [128, 2, TH], BF16) for i in range(2)]
    kt32 = [P.sb(f"kt32_{i}", [128, 4, 256], F32) for i in range(2)]
    vt32 = [P.sb(f"vt32_{i}", [128, 4, 256], F32) for i in range(2)]
    ktb = [P.sb(f"ktb_{i}", [128, 4, 256], BF16) for i in range(2)]
    vtb = [P.sb(f"vtb_{i}", [128, 4, 256], BF16) for i in range(2)]
    Sm = [P.sb(f"Sm{i}", [128, 128], BF16) for i in range(2)]
    Tst = P.sb("Tst", [128, 2, 256], F32)
    Sbf = P.sb("Sbf", [128, 2, 256], BF16)
    osb = [P.sb(f"osb{i}", [128, 256], F32) for i in range(2)]
    psS = [P.ps(f"psS{i}", [128, 512]) for i in range(2)]
    psO = [P.ps(f"psO{i}", [128, 512]) for i in range(2)]
    psK = [P.ps(f"psK{i}", [128, 512]) for i in range(2)]

    P.dcp('sp', lg_sb[:, :], lg[:, :], (), [('lg',)], 'lg')
    P.dcp('sp', mask_sb[:, :], maskT[:, :], (), [('mask',)], 'mask')
    P.dcp('sp', irow_sb[:, :], irow[:, :], (), [('irow',)], 'irow')
    P.dcp('sp', icol_sb[:, :], icol[:, :], (), [('icol',)], 'icol')
    P.act(lgp[:, :], lg_sb[:, :], AF.Abs, reads=[('lg',)], writes=[('lgp',)])
    P.ts('dve', lgn[:, :], lgp[:, :], -1.0, None, ALU.mult, None, reads=[('lgp',)], writes=[('lgn',)])
    P.act(gq[:, :], irow_sb[:, :], AF.Exp, reads=[('irow',), ('lgn',)], writes=[('gq',)], scale=lgn[:, 0:1])
    P.act(gk[:, :], irow_sb[:, :], AF.Exp, reads=[('irow',), ('lgp',)], writes=[('gk',)], scale=lgp[:, 0:1])
    P.ts('dve', gk[:, :], gk[:, :], 1.0 / 16, None, ALU.mult, None, reads=[('gk',)], writes=[('gk',)])
    P.act(gkc[:, :], icol_sb[:, :], AF.Exp, reads=[('icol',), ('lgp',)], writes=[('gkc',)], scale=lgp[:, 0:1])
    P.ts('dve', gkc[:, :], gkc[:, :], 1.0 / 16, None, ALU.mult, None, reads=[('gkc',)], writes=[('gkc',)])
    P.emit('dve', lambda e: e.memset(c128[:, :], 128.0), writes=[('c128',)])
    P.act(gC[:, :], c128[:, :], AF.Exp, reads=[('c128',), ('lgn',)], writes=[('gC',)], scale=lgn[:, 0:1])
    P.emit('dve', lambda e: e.memset(Tst[:, :, :], 0.0), writes=[('Tst', 0), ('Tst', 1)])

    outs = []
    for sbk in range(L // TH):
        t0 = sbk * TH
        bi = sbk % 2
        for kc in range(2):
            P.dcp('sp', q32[bi][:, kc, :], qT[kc, :, t0:t0 + TH], (), [('q32', bi, kc)], ('q32', bi, kc))
            P.dcp('sp', k32[bi][:, kc, :], kT[kc, :, t0:t0 + TH], (), [('k32', bi, kc)], ('k32', bi, kc))
            P.tt('dve', qb[bi][:, kc, :], q32[bi][:, kc, :], gq[:, :], ALU.mult, reads=[('q32', bi, kc), ('gq',)], writes=[('qb', bi, kc)])
            P.tt('dve', kb[bi][:, kc, :], k32[bi][:, kc, :], gk[:, :], ALU.mult, reads=[('k32', bi, kc), ('gk',)], writes=[('kb', bi, kc)])
        P.dcp('sp', kt32[bi][:, :, :], ktok[t0:t0 + TH, :].rearrange("(c p) e -> p c e", p=128), (), [('kt32', bi)], ('kt32', bi))
        P.dcp('sp', vt32[bi][:, :, :], vtok[t0:t0 + TH, :].rearrange("(c p) e -> p c e", p=128), (), [('vt32', bi)], ('vt32', bi))
        P.ts('dve', ktb[bi][:, :, :], kt32[bi][:, :, :], gkc[:, 0:1], None, ALU.mult, None, reads=[('kt32', bi), ('gkc',)], writes=[('ktb', bi)])
        P.act(vtb[bi][:, :, :], vt32[bi][:, :, :], AF.Copy, reads=[('vt32', bi)], writes=[('vtb', bi)])
        for cc in range(4):
            n = sbk * 4 + cc
            cs = slice(cc * 128, (cc + 1) * 128)
            pS = psS[n % 2]; pSr = ('psS', n % 2)
            pO = psO[n % 2]; pOr = ('psO', n % 2)
            for kc in range(2):
                P.mm(pS[:, 0:128], kb[bi][:, kc, cs], qb[bi][:, kc, cs], kc == 0, kc == 1,
                     reads=[('kb', bi, kc), ('qb', bi, kc)], writes=[pSr])
            sm = Sm[n % 2]; smr = ('Sm', n % 2)
            P.tt('dve', sm[:, :], pS[:, 0:128], mask_sb[:, :], ALU.mult, reads=[pSr, ('mask',)], writes=[smr])
            P.mm(pO[:, 0:256], sm[:, :], vtb[bi][:, cc, :], True, n == 0, reads=[smr, ('vtb', bi)], writes=[pOr])
            if n > 0:
                for kc in range(2):
                    P.mm(pO[:, 0:256], qb[bi][:, kc, cs], Sbf[:, kc, :], False, kc == 1,
                         reads=[('qb', bi, kc), ('Sbf', kc)], writes=[pOr])
            ob = osb[n % 2]; obr = ('osb', n % 2)
            P.act(ob[:, :], pO[:, 0:256], AF.Copy, reads=[pOr], writes=[obr])
            P.dcp('sp', o[n * 128:(n + 1) * 128, :], ob[:, :], [obr], [('out', n)], ('osb', n % 2))
            outs.append(('out', n))
            if n < NCH - 1:
                for kc in range(2):
                    pK = psK[kc]; pKr = ('psK', kc)
                    P.mm(pK[:, 0:256], ktb[bi][:, cc, kc * 128:(kc + 1) * 128], vtb[bi][:, cc, :], True, True,
                         reads=[('ktb', bi), ('vtb', bi)], writes=[pKr])
                    P.stt(Tst[:, kc, :], Tst[:, kc, :], gC[:, 0:1], pK[:, 0:256], ALU.mult, ALU.add,
                          reads=[('Tst', kc), ('gC',), pKr], writes=[('Tst', kc)])
                    P.act(Sbf[:, kc, :], Tst[:, kc, :], AF.Copy, reads=[('Tst', kc), ('gC',)], writes=[('Sbf', kc)], scale=gC[:, 0:1])
    return P.finish(outs)


def run_ret(q_rot, k_rot, v, ret_log_decay):
    nc = build_ret()
    in_maps = []
    ii = np.arange(128)
    irow = np.ascontiguousarray(np.broadcast_to(((np.arange(TH) % 128) + 1).astype(np.float32)[None, :], (128, TH)))
    icol = (ii + 1).astype(np.float32).reshape(128, 1)
    for c in range(NCORES):
        hh, d = c // 2, c % 2
        qq, kk, vv = q_rot[:, hh], k_rot[:, hh], v[:, hh]
        if d == 1:
            qq, kk, vv = qq[::-1], kk[::-1], vv[::-1]
        mk = (ii[None, :] >= ii[:, None]) if d == 0 else (ii[None, :] > ii[:, None])
        in_maps.append({"qT": np.ascontiguousarray(qq.T.reshape(2, 128, L)), "kT": np.ascontiguousarray(kk.T.reshape(2, 128, L)),
                        "ktok": np.ascontiguousarray(kk), "vtok": np.ascontiguousarray(vv),
                        "lg": np.full((128, 1), ret_log_decay[d, hh], np.float32), "maskT": mk.astype(np.float32),
                        "irow": irow, "icol": icol})
    res = run_bass_kernel_spmd(nc, in_maps, core_ids=list(range(NCORES)))
    o_f = np.stack([res.results[2 * hh]["o"] for hh in range(4)], 1)
    o_b = np.stack([res.results[2 * hh + 1]["o"][::-1] for hh in range(4)], 1)
    return o_f, o_b


def build_attn():
    P = Prog()
    TK = T + 256
    NBLK = T // 128
    qT = P.dram_in("qT", [16, 128, T])
    kT = P.dram_in("kT", [4, 128, TK])
    vtok = P.dram_in("vtok", [TK, 512])
    sink = P.dram_in("sink", [128, 16])
    mbias = P.dram_in("mbias", [128, 3, 384])
    ident = P.dram_in("ident", [128, 128])
    oT = P.dram_out("oT", [16, 128, T])
    P.dcp = lambda eng, out, in_, reads, writes, key: P.dma(eng, lambda e: e.dma_start(out=out, in_=in_), reads=reads, writes=writes, key=key)

    qb = P.sb("qb", [128, 16, T], BF16)
    kb = P.sb("kb", [128, 4, TK], BF16)
    vb = P.sb("vb", [128, TK // 128, 512], BF16)
    sink_sb = P.sb("sink_sb", [128, 16], F32)
    mb_sb = P.sb("mb_sb", [128, 3, 384], F32)
    id32 = P.sb("id32", [128, 128], F32)
    idb = P.sb("idb", [128, 128], BF16)
    osb = P.sb("osb", [128, 16, T], F32)
    NR = 2
    Sb = [P.sb(f"Sb{i}", [128, 384], F32) for i in range(NR)]
    Pe = [P.sb(f"Pe{i}", [128, 384], BF16) for i in range(NR)]
    Pn = [P.sb(f"Pn{i}", [128, 384], BF16) for i in range(NR)]
    PT = [P.sb(f"PT{i}", [128, 3, 128], BF16) for i in range(NR)]
    sm = [P.sb(f"sm{i}", [128, 8], F32) for i in range(NR)]
    psS = [P.ps(f"psS{i}", [128, 512]) for i in range(2)]
    psT = [P.ps(f"psT{i}", [128, 3, 128], BF16) for i in range(2)]
    psO = [P.ps(f"psO{i}", [128, 512]) for i in range(2)]

    for h in range(16):
        P.dcp('pool', qb[:, h, :], qT[h, :, :], (), [('qb', h)], ('qb', h))
    for h in range(4):
        P.dcp('pool', kb[:, h, :], kT[h, :, :], (), [('kb', h)], ('kb', h))
    P.dcp('pool', vb[:, :, :], vtok.rearrange("(c p) e -> p c e", p=128), (), [('vb',)], 'vb')
    P.dcp('sp', sink_sb[:, :], sink[:, :], (), [('sink',)], 'sink')
    P.dcp('sp', mb_sb[:, :, :], mbias[:, :, :], (), [('mb',)], 'mb')
    P.dcp('sp', id32[:, :], ident[:, :], (), [('id32',)], 'id32')
    P.act(idb[:, :], id32[:, :], AF.Copy, reads=[('id32',)], writes=[('idb',)])
    SCALE = float(128 ** -0.5)
    it = 0
    for c in range(NBLK):
        mi = 0 if c == 0 else (2 if c == NBLK - 1 else 1)
        qs = slice(c * 128, (c + 1) * 128)
        for hq in range(16):
            kv = hq // 4
            i = it % NR; it += 1
            pS = psS[i % 2]; pSr = ('psS', i % 2)
            P.mm(pS[:, 0:384], qb[:, hq, qs], kb[:, kv, c * 128:c * 128 + 384], True, True,
                 reads=[('qb', hq), ('kb', kv)], writes=[pSr])
            P.stt(Sb[i][:, :], pS[:, 0:384], SCALE, mb_sb[:, mi, :], ALU.mult, ALU.add, reads=[pSr, ('mb',)], writes=[('Sb', i)])
            s_ = sm[i]; smr = ('sm', i)
            P.emit('dve', lambda e, i=i, s_=s_: e.reduce_max(out=s_[:, 0:1], in_=Sb[i][:, :], axis=mybir.AxisListType.X),
                   reads=[('Sb', i)], writes=[smr])
            P.tt('dve', s_[:, 1:2], s_[:, 0:1], sink_sb[:, hq:hq + 1], ALU.max, reads=[smr, ('sink',)], writes=[smr])
            P.ts('dve', s_[:, 2:3], s_[:, 1:2], -1.0, None, ALU.mult, None, reads=[smr], writes=[smr])
            P.emit('act', lambda e, i=i, s_=s_: e.activation(out=Pe[i][:, :], in_=Sb[i][:, :], func=AF.Exp, bias=s_[:, 2:3],
                                                            accum_out=s_[:, 3:4]), reads=[('Sb', i), smr], writes=[('Pe', i), smr])
            P.act(s_[:, 4:5], sink_sb[:, hq:hq + 1], AF.Exp, reads=[('sink',), smr], writes=[smr], bias=s_[:, 2:3])
            P.tt('dve', s_[:, 5:6], s_[:, 3:4], s_[:, 4:5], ALU.add, reads=[smr], writes=[smr])
            P.emit('dve', lambda e, s_=s_: e.reciprocal(out=s_[:, 6:7], in_=s_[:, 5:6]), reads=[smr], writes=[smr])
            P.ts('dve', Pn[i][:, :], Pe[i][:, :], s_[:, 6:7], None, ALU.mult, None, reads=[('Pe', i), smr], writes=[('Pn', i)])
            pT = psT[i % 2]; pTr = ('psT', i % 2)
            for kk in range(3):
                P.emit('pe', lambda e, i=i, kk=kk, pT=pT: e.transpose(out=pT[:, kk, :], in_=Pn[i][:, kk * 128:(kk + 1) * 128], identity=idb[:, :]),
                       reads=[('Pn', i), ('idb',)], writes=[pTr])
            P.act(PT[i][:, :, :], pT[:, :, :], AF.Copy, reads=[pTr], writes=[('PT', i)])
            pO = psO[i % 2]; pOr = ('psO', i % 2)
            for kk in range(3):
                P.mm(pO[:, 0:128], vb[:, c + kk, kv * 128:(kv + 1) * 128], PT[i][:, kk, :], kk == 0, kk == 2,
                     reads=[('vb',), ('PT', i)], writes=[pOr])
            P.emit('dve', lambda e, hq=hq, qs=qs, pO=pO: e.tensor_copy(out=osb[:, hq, qs], in_=pO[:, 0:128]), reads=[pOr], writes=[('osb', hq)])
    outs = []
    for hq in range(16):
        P.dcp('sp', oT[hq, :, :], osb[:, hq, :], [('osb', hq)], [('out', hq)], ('osb', hq))
        outs.append(('out', hq))
    return P.finish(outs)


def attn_masks():
    t = np.arange(128)[:, None]; s = np.arange(384)[None, :]
    inwin = np.abs(t - s + 128) <= 128
    NEG = np.float32(-30000.0)
    mid = np.where(inwin, 0.0, NEG).astype(np.float32)
    first = np.where(inwin & (s >= 128), 0.0, NEG).astype(np.float32)
    last = np.where(inwin & (s < 256), 0.0, NEG).astype(np.float32)
    return mid, first, last


def run_attn(q_rot, k_rot, v, sink):
    nc = build_attn()
    mid, first, last = attn_masks()
    kp = np.zeros((L + 256, 4, 128), np.float32); kp[128:L + 128] = k_rot
    vp = np.zeros((L + 256, 4, 128), np.float32); vp[128:L + 128] = v
    in_maps = []
    for c in range(NCORES):
        sl = slice(c * T, (c + 1) * T)
        hs = slice(c * T, (c + 1) * T + 256)
        mb = np.stack([first if c == 0 else mid, mid, last if c == NCORES - 1 else mid], 1)
        in_maps.append({"qT": np.ascontiguousarray(q_rot[sl].transpose(1, 2, 0)), "kT": np.ascontiguousarray(kp[hs].transpose(1, 2, 0)),
                        "vtok": np.ascontiguousarray(vp[hs].reshape(T + 256, 512)),
                        "sink": np.ascontiguousarray(np.broadcast_to(sink.reshape(1, 16), (128, 16))).astype(np.float32),
                        "mbias": np.ascontiguousarray(mb), "ident": np.eye(128, dtype=np.float32)})
    res = run_bass_kernel_spmd(nc, in_maps, core_ids=list(range(NCORES)))
    return np.concatenate([r["oT"].transpose(2, 0, 1).reshape(T, 2048) for r in res.results], 0)


def build_mid(stop=0):
    P = TokProg()
    xT = P.dram_in("xT", [D, T])
    yfT = P.dram_in("yfT", [1024, T])
    ybT = P.dram_in("ybT", [1024, T])
    ofT = P.dram_in("ofT", [1024, T])
    obT = P.dram_in("obT", [1024, T])
    gT = P.dram_in("gT", [1024, T])
    w_glu = P.dram_in("w_glu", [1024, 1024])
    b_glu = P.dram_in("b_glu", [128, 8])
    w_out = P.dram_in("w_out", [D, D])
    w1 = P.dram_in("w1", [D, 4 * D])
    w2 = P.dram_in("w2", [4 * D, D])
    lnp = P.dram_in("lnp", [128, 4, 16])
    w_in2 = P.dram_in("w_in2", [D, 3072])
    cosF = P.dram_in("cosF", [128, T])
    sinF = P.dram_in("sinF", [128, T])
    perm = P.dram_in("perm", [128, 128])
    x2T = P.dram_out("x2T", [D, T])
    h2T = P.dram_out("h2T", [3072, T])

    X = P.sb("X", [128, 16, TH], F32)
    Rb = P.sb("R", [128, 16, TH], F32)
    Bb = P.sb("Bb", [128, 16, TH], BF16)
    H = P.sb("H", [128, 64, TH], BF16)
    lnp_sb = P.sb("lnp_sb", [128, 4, 16], F32)
    bglu_sb = P.sb("bglu_sb", [128, 8], F32)
    cs = P.sb("cs", [128, 2, T], F32)
    perm32 = P.sb("perm32", [128, 128], F32)
    permb = P.sb("permb", [128, 128], BF16)
    hb16 = [P.sb(f"hb16_{i}", [128, TH], BF16) for i in range(2)]
    P.setup_common()
    xres = lambda kc: ('X', kc)
    rres = lambda kc: ('R', kc)
    bres = lambda kc: ('B', kc)
    hres = lambda kc: ('H', kc)
    v = lambda ap: ap.rearrange("(kc p) t -> p kc t", p=128)
    xTv, yfv, ybv, ofv, obv, gv, x2v, h2v = v(xT), v(yfT), v(ybT), v(ofT), v(obT), v(gT), v(x2T), v(h2T)
    YB = H

    def body():
        P.start()
        P.dcp('sp', lnp_sb[:, :, :], lnp[:, :, :], (), [('lnp',)], 'lnp')
        P.dcp('sp', bglu_sb[:, :], b_glu[:, :], (), [('bglu',)], 'bglu')
        P.dcp('sp', cs[:, 0, :], cosF[:, :], (), [('cs',)], 'cs0')
        P.dcp('sp', cs[:, 1, :], sinF[:, :], (), [('cs',)], 'cs1')
        P.dcp('sp', perm32[:, :], perm[:, :], (), [('perm32',)], 'perm32')
        P.act(permb[:, :], perm32[:, :], AF.Copy, reads=[('perm32',)], writes=[('permb',)])
        outs = []
        hbi = [0]
        for half in range(T // TH):
            t0 = half * TH
            ts_ = slice(t0, t0 + TH)
            for n in range(8):
                P.dcp('sp', Rb[:, n, :], yfv[:, n, ts_], (), [rres(n)], ('R', n))
                P.dcp('sp', Rb[:, 8 + n, :], ybv[:, n, ts_], (), [rres(8 + n)], ('R', 8 + n))
            for n in range(8):
                xs = X[:, n, :]
                t1 = P.tmp[0]; t2 = P.tmp[1]
                P.tt('dve', xs, Rb[:, n, :], Rb[:, 8 + n, :], ALU.add, reads=[rres(n), rres(8 + n)], writes=[xres(n)])
                P.tt('dve', t1[:, :], xs, xs, ALU.mult, reads=[xres(n)], writes=[('tmp', 0)])
                P.ts('dve', t1[:, :], t1[:, :], 0.044715, 1.0, ALU.mult, ALU.add, reads=[('tmp', 0)], writes=[('tmp', 0)])
                P.tt('dve', t1[:, :], t1[:, :], xs, ALU.mult, reads=[('tmp', 0), xres(n)], writes=[('tmp', 0)])
                P.act(t2[:, :], t1[:, :], AF.Sigmoid, reads=[('tmp', 0)], writes=[('tmp', 1)], scale=1.5957691216057308)
                P.tt('dve', xs, xs, t2[:, :], ALU.mult, reads=[xres(n), ('tmp', 1)], writes=[xres(n)])
                P.act(YB[:, n, :], xs, AF.Copy, reads=[xres(n)], writes=[hres(n)])

            def evg(n, ps, pr):
                t2 = P.tmp[1]
                P.act(t2[:, :], ps[:, :], AF.Sigmoid, reads=[pr, ('bglu',)], writes=[('tmp', 1)], bias=bglu_sb[:, n:n + 1])
                P.tt('dve', Bb[:, n, :], X[:, n, :], t2[:, :], ALU.mult, reads=[xres(n), ('tmp', 1)], writes=[bres(n)])
            P.lin(w_glu, 1024, 1024, lambda kc: YB[:, kc, :], hres, evg)
            if stop == 1:
                for kc in range(8):
                    P.dcp('sp', x2v[:, kc, ts_], X[:, kc, :], [xres(kc), bres(kc)], [('out', half, kc)], ('X', kc))
                    outs.append(('out', half, kc))
                return outs
            for n in range(8):
                P.dcp('sp', Rb[:, n, :], ofv[:, n, ts_], (), [rres(n)], ('R', n))
                P.dcp('sp', Rb[:, 8 + n, :], obv[:, n, ts_], (), [rres(8 + n)], ('R', 8 + n))
            for n in range(8):
                P.tt('dve', X[:, 8 + n, :], Rb[:, n, :], Rb[:, 8 + n, :], ALU.add, reads=[rres(n), rres(8 + n)], writes=[xres(8 + n)])
            for n in range(8):
                P.dcp('sp', Rb[:, n, :], gv[:, n, ts_], (), [rres(n)], ('R', n))
            for hh in range(4):
                P.stats([(X[:, 8 + 2 * hh + k, :], xres(8 + 2 * hh + k)) for k in range(2)], 1e-6)
                for k in range(2):
                    n = 2 * hh + k
                    t1 = P.tmp[0]; t2 = P.tmp[1]
                    P.act(t2[:, :], Rb[:, n, :], AF.Silu, reads=[rres(n)], writes=[('tmp', 1)])
                    P.tt('dve', t1[:, :], X[:, 8 + n, :], P.mean[:, :], ALU.subtract, reads=[xres(8 + n), ('mean',)], writes=[('tmp', 0)])
                    P.tt('dve', t1[:, :], t1[:, :], P.rstd[:, :], ALU.mult, reads=[('tmp', 0), ('rstd',)], writes=[('tmp', 0)])
                    P.tt('dve', Bb[:, 8 + n, :], t1[:, :], t2[:, :], ALU.mult, reads=[('tmp', 0), ('tmp', 1)], writes=[bres(8 + n)])
            if stop == 2:
                for kc in range(16):
                    P.dcp('sp', x2v[:, kc, ts_], X[:, kc, :], [xres(kc), bres(kc)], [('out', half, kc)], ('X', kc))
                    outs.append(('out', half, kc))
                return outs
            for kc in range(16):
                P.dcp('sp', X[:, kc, :], xTv[:, kc, ts_], (), [xres(kc)], ('X', kc))

            def ev1(n, ps, pr):
                P.stt(Rb[:, n, :], X[:, n, :], ALPHA, ps[:, :], ALU.mult, ALU.add, reads=[xres(n), pr], writes=[rres(n)])
            P.lin(w_out, D, D, lambda kc: Bb[:, kc, :], bres, ev1)
            P.layernorm(Rb, rres, X, xres, Bb, bres, lnp_sb[:, 0, :], lnp_sb[:, 1, :])

            def ev2(n, ps, pr):
                P.act(H[:, n, :], ps[:, :], AF.Relu, reads=[pr], writes=[hres(n)])
                P.tt('dve', H[:, n, :], H[:, n, :], H[:, n, :], ALU.mult, reads=[hres(n)], writes=[hres(n)])
            P.lin(w1, D, 4 * D, lambda kc: Bb[:, kc, :], bres, ev2)

            def ev3(n, ps, pr):
                P.stt(X[:, n, :], X[:, n, :], ALPHA, ps[:, :], ALU.mult, ALU.add, reads=[xres(n), pr], writes=[xres(n)])
            P.lin(w2, 4 * D, D, lambda kc: H[:, kc, :], hres, ev3)
            P.layernorm(X, xres, Rb, rres, Bb, bres, lnp_sb[:, 2, :], lnp_sb[:, 3, :])
            for kc in range(16):
                P.dcp('sp', x2v[:, kc, ts_], Rb[:, kc, :], [rres(kc)], [('out', half, kc)], ('R', kc))
                outs.append(('out', half, kc))
            if stop == 3:
                continue
            cosv = cs[:, 0, ts_]; sinv = cs[:, 1, ts_]

            def ev4(n, ps, pr):
                ores = ('X', n % 16)
                ob = X[:, n % 16, :]
                if n < 20:
                    i = hbi[0] % 2; hbi[0] += 1
                    P.act(hb16[i][:, :], ps[:, :], AF.Copy, reads=[pr], writes=[('hb16', i), pr])
                    pp = P.pss[6 + i]; ppr = ('ps', 6 + i)
                    P.mm(pp[:, :], permb[:, :], hb16[i][:, :], True, True, reads=[('permb',), ('hb16', i)], writes=[ppr])
                    t1 = P.tmp[0]; t2 = P.tmp[1]
                    P.tt('dve', t1[:, :], ps[:, :], cosv, ALU.mult, reads=[pr, ('cs',)], writes=[('tmp', 0)])
                    P.tt('dve', t2[:, :], pp[:, :], sinv, ALU.mult, reads=[ppr, ('cs',)], writes=[('tmp', 1)])
                    P.tt('dve', ob, t1[:, :], t2[:, :], ALU.add, reads=[('tmp', 0), ('tmp', 1)], writes=[ores])
                else:
                    P.act(ob, ps[:, :], AF.Copy, reads=[pr], writes=[ores])
                P.dcp('sp', h2v[:, n, ts_], ob, [ores], [('outh', half, n)], ('X', n % 16))
                outs.append(('outh', half, n))
            P.lin(w_in2, D, 3072, lambda kc: Bb[:, kc, :], bres, ev4)
        return outs

    P.plan = True
    body()
    P.plan = False
    outs = body()
    return P.finish(outs)


def odd_rope_tables(pos):
    c, s = rope_tables(pos, 16, 500000.0)
    n = len(pos)
    cosF = np.ones((128, n), np.float32); sinF = np.zeros((128, n), np.float32)
    cosF[0:16] = c; cosF[16:32] = c
    sinF[0:16] = -s; sinF[16:32] = s
    perm = np.zeros((128, 128), np.float32)
    for m in range(16):
        perm[m + 16, m] = 1.0
        perm[m, m + 16] = 1.0
    return cosF, sinF, perm


def tm(a, c):
    return np.ascontiguousarray(a[c * T:(c + 1) * T].T)


def run_mid(x_tok, yf, yb, of, ob, gate, inp, stop=0):
    nc = build_mid(stop)
    in_maps = []
    lnp = lnp_layout(inp['ln_g'], inp['ln_b'], 0)
    bglu = np.ascontiguousarray(inp['s5_b_glu'][0].reshape(8, 128).T)
    for c in range(NCORES):
        cosF, sinF, perm = odd_rope_tables(np.arange(c * T, (c + 1) * T))
        in_maps.append({"xT": tm(x_tok, c), "yfT": tm(yf, c), "ybT": tm(yb, c), "ofT": tm(of, c), "obT": tm(ob, c), "gT": tm(gate, c),
                        "w_glu": inp['s5_w_glu'][0], "b_glu": bglu, "w_out": inp['even_w_out'][0], "w1": inp['mlp_w1'][0],
                        "w2": inp['mlp_w2'][0], "lnp": lnp, "w_in2": inp['odd_w_in'][0], "cosF": cosF, "sinF": sinF, "perm": perm})
    res = run_bass_kernel_spmd(nc, in_maps, core_ids=list(range(NCORES)))
    x2 = np.concatenate([r["x2T"].T for r in res.results], 0)
    h2 = np.concatenate([r["h2T"].T for r in res.results], 0)
    return x2, h2


def build_inproj_odd(mode=0):
    P = TokProg()
    xT = P.dram_in("xT", [D, T])
    w_in2 = P.dram_in("w_in2", [D, 3072])
    cosF = P.dram_in("cosF", [128, T])
    sinF = P.dram_in("sinF", [128, T])
    perm = P.dram_in("perm", [128, 128])
    h2T = P.dram_out("h2T", [3072, T])
    Rb = P.sb("R", [128, 16, TH], F32)
    Bb = P.sb("Bb", [128, 16, TH], BF16)
    O = P.sb("O", [128, 24, TH], F32)
    cs = P.sb("cs", [128, 2, T], F32)
    perm32 = P.sb("perm32", [128, 128], F32)
    permb = P.sb("permb", [128, 128], BF16)
    hb16 = [P.sb(f"hb16_{i}", [128, TH], BF16) for i in range(2)]
    P.setup_common()
    rres = lambda kc: ('R', kc)
    bres = lambda kc: ('B', kc)
    xTv = xT.rearrange("(kc p) t -> p kc t", p=128)
    h2v = h2T.rearrange("(kc p) t -> p kc t", p=128)

    def body():
        P.start()
        P.dcp('sp', cs[:, 0, :], cosF[:, :], (), [('cs',)], 'cs0')
        P.dcp('sp', cs[:, 1, :], sinF[:, :], (), [('cs',)], 'cs1')
        P.dcp('sp', perm32[:, :], perm[:, :], (), [('perm32',)], 'perm32')
        P.act(permb[:, :], perm32[:, :], AF.Copy, reads=[('perm32',)], writes=[('permb',)])
        outs = []
        hbi = [0]
        for half in range(T // TH):
            t0 = half * TH
            ts_ = slice(t0, t0 + TH)
            for kc in range(16):
                P.dcp('sp', Rb[:, kc, :], xTv[:, kc, ts_], (), [rres(kc)], ('R', kc))
                P.act(Bb[:, kc, :], Rb[:, kc, :], AF.Copy, reads=[rres(kc)], writes=[bres(kc)])
            cosv = cs[:, 0, ts_]; sinv = cs[:, 1, ts_]

            def ev4(n, ps, pr):
                ores = ('O', n)
                ob = O[:, n, :]
                if n < 20 and mode == 0:
                    i = hbi[0] % 2; hbi[0] += 1
                    P.act(hb16[i][:, :], ps[:, :], AF.Copy, reads=[pr], writes=[('hb16', i), pr])
                    pp = P.pss[6 + i]; ppr = ('ps', 6 + i)
                    P.mm(pp[:, :], permb[:, :], hb16[i][:, :], True, True, reads=[('permb',), ('hb16', i)], writes=[ppr])
                    t1 = P.tmp[0]; t2 = P.tmp[1]
                    P.tt('dve', t1[:, :], ps[:, :], cosv, ALU.mult, reads=[pr, ('cs',)], writes=[('tmp', 0)])
                    P.tt('dve', t2[:, :], pp[:, :], sinv, ALU.mult, reads=[ppr, ('cs',)], writes=[('tmp', 1)])
                    P.tt('dve', ob, t1[:, :], t2[:, :], ALU.add, reads=[('tmp', 0), ('tmp', 1)], writes=[ores])
                else:
                    P.act(ob, ps[:, :], AF.Copy, reads=[pr], writes=[ores])
                P.dcp('sp', h2v[:, n, ts_], ob, [ores], [('outh', half, n)], ('O', n))
                outs.append(('outh', half, n))
            P.lin(w_in2, D, 3072, lambda kc: Bb[:, kc, :], bres, ev4)
        return outs

    P.plan = True
    body()
    P.plan = False
    outs = body()
    return P.finish(outs)


def run_inproj_odd(x_tok, w_in2, mode=0):
    nc = build_inproj_odd(mode)
    in_maps = []
    for c in range(NCORES):
        cosF, sinF, perm = odd_rope_tables(np.arange(c * T, (c + 1) * T))
        in_maps.append({"xT": tm(x_tok, c), "w_in2": w_in2, "cosF": cosF, "sinF": sinF, "perm": perm})
    res = run_bass_kernel_spmd(nc, in_maps, core_ids=list(range(NCORES)))
    return np.concatenate([r["h2T"].T for r in res.results], 0)


def kernel(**inputs):
    inp = {k: np.asarray(v) for k, v in inputs.items()}
    x = np.ascontiguousarray(inp['x'][0], dtype=np.float32)
    h = run_inproj_even(x, inp['even_w_in'][0])
    u = np.ascontiguousarray(h[:, :1024])
    q = h[:, 1024:2048].reshape(L, 4, 256)
    k = h[:, 2048:3072].reshape(L, 4, 256)
    v = h[:, 3072:4096].reshape(L, 4, 256)
    gate = np.ascontiguousarray(h[:, 4096:])
    yf, yb = run_s5(u, inp)
    of, ob = run_ret(q, k, v, inp['ret_log_decay'][0])
    x2, h2 = run_mid(x, yf, yb, of.reshape(L, 1024), ob.reshape(L, 1024), gate, inp)
    q2 = h2[:, :2048].reshape(L, 16, 128)
    k2 = h2[:, 2048:2560].reshape(L, 4, 128)
    v2 = h2[:, 2560:].reshape(L, 4, 128)
    o = run_attn(q2, k2, v2, inp['attn_sink'][0])
    y = run_tail(x2, o, inp['odd_w_out'][0], inp['mlp_w1'][1], inp['mlp_w2'][1], lnp_layout(inp['ln_g'], inp['ln_b'], 1))
    return y.reshape(1, L, D).astype(np.float32)
```

```python
import numpy as np
from contextlib import ExitStack
import concourse.bass as bass
import concourse.mybir as mybir
from concourse.bass_utils import run_bass_kernel_spmd

F32 = mybir.dt.float32
BF16 = mybir.dt.bfloat16
ALU = mybir.AluOpType
AF = mybir.ActivationFunctionType
ENGS = ['pe', 'dve', 'act', 'pool', 'sp']

NCORES = 8
D = 2048
L = 8192
T = L // NCORES
TH = 512
ALPHA = float(4 ** 0.25)
LN_EPS = 1e-5


class Prog:
    def __init__(self):
        self.nc = bass.Bass("TRN2", target_bir_lowering=False)
        self.q = {e: [] for e in ENGS}
        self.cnt = {}
        self.seen = {e: {} for e in ENGS}
        self.lastw = {}
        self.readers = {}
        self.semkeys = []
        self.es = ExitStack()
        self.plan = False
        self.panels = []
        self.pi = 0
        self.issued = 0
        self.psi = 0

    def sb(self, name, shape, dt):
        return self.es.enter_context(self.nc.sbuf_tensor(name, shape, dt))

    def ps(self, name, shape, dt=F32):
        return self.es.enter_context(self.nc.psum_tensor(name, shape, dt))

    def dram_in(self, name, shape, dt=F32):
        return self.nc.dram_tensor(name, list(shape), dt, kind="ExternalInput").ap()

    def dram_out(self, name, shape, dt=F32):
        return self.nc.dram_tensor(name, list(shape), dt, kind="ExternalOutput").ap()

    def _deps(self, eng, reads, writes):
        deps = []
        for r in reads:
            t = self.lastw.get(r)
            if t: deps.append(t)
            if isinstance(r, tuple) and isinstance(r[0], str) and r[0].startswith('ps'):
                deps.extend(x for x in self.readers.get(r, []) if x[2] != eng)
        for w in writes:
            t = self.lastw.get(w)
            if t: deps.append(t)
            deps.extend(self.readers.get(w, []))
        waits = []
        for (sk, val, deng) in deps:
            if deng == eng and eng == 'pe':
                continue
            if self.seen[eng].get(sk, 0) >= val:
                continue
            self.seen[eng][sk] = val
            waits.append((sk, val))
        return waits

    def _bump(self, sk, inc):
        if sk not in self.cnt:
            self.cnt[sk] = 0
            self.semkeys.append(sk)
        self.cnt[sk] += inc
        return self.cnt[sk]

    def _record(self, tok, reads, writes):
        for w in writes:
            self.lastw[w] = tok
            self.readers[w] = []
        for r in reads:
            self.readers.setdefault(r, []).append(tok)

    def emit(self, eng, fn, reads=(), writes=()):
        if self.plan: return
        waits = self._deps(eng, reads, writes)
        sk = 'e_' + eng
        v = self._bump(sk, 1)
        self.q[eng].append((waits, fn, sk, 1))
        self._record((sk, v, eng), reads, writes)

    def dma(self, eng, fn, reads=(), writes=(), key=None):
        if self.plan: return
        waits = self._deps(eng, reads, writes)
        sk = 'd_' + str(key)
        v = self._bump(sk, 16)
        self.q[eng].append((waits, fn, sk, 16))
        self._record((sk, v, 'dma'), reads, writes)

    def barrier(self):
        if self.plan: return
        toks = []
        for sk in self.semkeys:
            toks.append((sk, self.cnt[sk]))
        for e in ENGS:
            waits = []
            for (sk, v) in toks:
                if self.seen[e].get(sk, 0) >= v: continue
                self.seen[e][sk] = v
                waits.append((sk, v))
            if waits:
                self.q[e].append((waits, None, None, 0))

    def tt(self, eng, out, in0, in1, op, reads, writes):
        self.emit(eng, lambda e: e.tensor_tensor(out=out, in0=in0, in1=in1, op=op), reads, writes)

    def ts(self, eng, out, in0, s1, s2, op0, op1, reads, writes):
        if op1 is None:
            self.emit(eng, lambda e: e.tensor_scalar(out=out, in0=in0, scalar1=s1, scalar2=None, op0=op0), reads, writes)
        else:
            self.emit(eng, lambda e: e.tensor_scalar(out=out, in0=in0, scalar1=s1, scalar2=s2, op0=op0, op1=op1), reads, writes)

    def stt(self, out, in0, scalar, in1, op0, op1, reads, writes):
        self.emit('dve', lambda e: e.scalar_tensor_tensor(out=out, in0=in0, scalar=scalar, in1=in1, op0=op0, op1=op1), reads, writes)

    def act(self, out, in_, func, reads, writes, bias=None, scale=None):
        kw = {}
        if bias is not None: kw['bias'] = bias
        if scale is not None: kw['scale'] = scale
        self.emit('act', lambda e: e.activation(out=out, in_=in_, func=func, **kw), reads, writes)

    def mm(self, out, lhsT, rhs, start, stop, reads, writes):
        self.emit('pe', lambda e: e.matmul(out, lhsT=lhsT, rhs=rhs, start=start, stop=stop), reads, writes)

    def finish(self, final_res):
        waits = self._deps('sp', final_res, ())
        self.q['sp'].append((waits, None, None, 0))
        nc = self.nc
        sems = {}
        for i, sk in enumerate(self.semkeys):
            sems[sk] = self.es.enter_context(nc.semaphore(f"s{i}"))
        with nc.Block() as block:
            def run(e, engobj):
                for (waits, fn, sk, inc) in self.q[e]:
                    for (wk, wv) in waits:
                        engobj.wait_ge(sems[wk], wv)
                    if fn is not None:
                        fn(engobj).then_inc(sems[sk], inc)

            @block.tensor
            def _(e): run('pe', e)

            @block.vector
            def _(e): run('dve', e)

            @block.scalar
            def _(e): run('act', e)

            @block.gpsimd
            def _(e): run('pool', e)

            @block.sync
            def _(e): run('sp', e)
        self.es.close()
        return nc


def panel_dims(K, N):
    KC = K // 128
    CW = min(N, 8192 // KC, 512)
    return KC, CW, N // CW


def to_panels(wb, K, N):
    KC, CW, NP = panel_dims(K, N)
    return np.ascontiguousarray(wb.reshape(KC, 128, NP, CW).transpose(2, 1, 0, 3).reshape(NP, 128, KC * CW))


class TokProg(Prog):
    NSLOT = 2
    WELEMS = 8192

    def dcp(self, eng, out, in_, reads, writes, key):
        if eng == 'sp':
            eng = 'pool'
        self.dma(eng, lambda e: e.dma_start(out=out, in_=in_), reads=reads, writes=writes, key=key)

    def wparam(self, name, K, N):
        KC, CW, NP = panel_dims(K, N)
        return self.dram_in(name, [NP, 128, KC * CW], BF16)

    def setup_common(self):
        self.wbuf = [self.sb(f"wbuf{i}", [128, self.WELEMS], BF16) for i in range(self.NSLOT)]
        self.pss = [self.ps(f"ps{i}", [128, 512]) for i in range(8)]
        self.ones32 = self.sb("ones32", [128, 128], F32)
        self.sq = [self.sb(f"sq{i}", [128, TH], F32) for i in range(2)]
        self.tmp = [self.sb(f"tmp{i}", [128, TH], F32) for i in range(2)]
        self.mean = self.sb("mean", [128, TH], F32)
        self.m2 = self.sb("m2", [128, TH], F32)
        self.rstd = self.sb("rstd", [128, TH], F32)
        self.sqi = 0
        self.tmpi = 0

    def start(self):
        self.psi = 0
        self.emit('dve', lambda e: e.memset(self.ones32[:, :], 1.0), writes=[('ones32',)])

    def next_ps(self):
        i = self.psi % 4
        self.psi += 1
        return self.pss[i], ('ps', i)

    def issue_panel(self, i):
        (w_ap, KC, c0, CW) = self.panels[i]
        slot = i % self.NSLOT
        wv = self.wbuf[slot][:, 0:KC * CW]
        src = w_ap[c0 // CW, :, :]
        self.dma('sp', lambda e: e.dma_start(out=wv, in_=src), writes=[('wb', slot)], key=('wb', slot))

    def next_panel(self, w_ap, KC, c0, CW):
        if self.plan:
            self.panels.append((w_ap, KC, c0, CW))
            return None
        i = self.pi
        self.pi += 1
        assert self.panels[i][1:] == (KC, c0, CW)
        while self.issued < min(len(self.panels), i + self.NSLOT):
            self.issue_panel(self.issued)
            self.issued += 1
        slot = i % self.NSLOT
        return self.wbuf[slot][:, 0:KC * CW].rearrange("p (kc n) -> p kc n", kc=KC), slot

    def lin(self, w_ap, K, N, rhs_fn, rhs_res, evac_fn, tw=TH):
        KC, CW, _ = panel_dims(K, N)
        for c0 in range(0, N, CW):
            r = self.next_panel(w_ap, KC, c0, CW)
            if self.plan: continue
            wv, slot = r
            for nn in range(CW // 128):
                n = c0 // 128 + nn
                ps, pr = self.next_ps()
                for kc in range(KC):
                    self.mm(ps[:, 0:tw], wv[:, kc, nn * 128:(nn + 1) * 128], rhs_fn(kc), kc == 0, kc == KC - 1,
                            reads=[('wb', slot), rhs_res(kc)], writes=[pr])
                evac_fn(n, ps, pr)

    def stats(self, chunks, eps):
        nck = len(chunks)
        ps1, pr1 = self.pss[4], ('ps', 4)
        ps2, pr2 = self.pss[5], ('ps', 5)
        for kc, (ap, rs) in enumerate(chunks):
            self.mm(ps1[:, :], self.ones32[:, :], ap, kc == 0, kc == nck - 1, reads=[('ones32',), rs], writes=[pr1])
        for kc, (ap, rs) in enumerate(chunks):
            sq = self.sq[self.sqi % 2]; sqr = ('sq', self.sqi % 2); self.sqi += 1
            self.act(sq[:, :], ap, AF.Square, reads=[rs], writes=[sqr])
            self.mm(ps2[:, :], self.ones32[:, :], sq[:, :], kc == 0, kc == nck - 1, reads=[('ones32',), sqr], writes=[pr2])
        inv = 1.0 / (nck * 128)
        self.act(self.mean[:, :], ps1[:, :], AF.Copy, reads=[pr1], writes=[('mean',)], scale=inv)
        self.tt('dve', self.m2[:, :], self.mean[:, :], self.mean[:, :], ALU.mult, reads=[('mean',)], writes=[('m2',)])
        self.stt(self.m2[:, :], ps2[:, :], inv, self.m2[:, :], ALU.mult, ALU.subtract, reads=[pr2, ('m2',)], writes=[('m2',)])
        self.ts('dve', self.m2[:, :], self.m2[:, :], eps, None, ALU.add, None, reads=[('m2',)], writes=[('m2',)])
        self.act(self.m2[:, :], self.m2[:, :], AF.Sqrt, reads=[('m2',)], writes=[('m2',)])
        self.emit('dve', lambda e: e.reciprocal(out=self.rstd[:, :], in_=self.m2[:, :]), reads=[('m2',)], writes=[('rstd',)])

    def layernorm(self, S, sres, Dst, dres, Bd, bres, g_ap, b_ap, nck=16):
        self.stats([(S[:, kc, :], sres(kc)) for kc in range(nck)], LN_EPS)
        for kc in range(nck):
            tmp = self.tmp[self.tmpi % 2]; tr = ('tmp', self.tmpi % 2); self.tmpi += 1
            self.tt('dve', tmp[:, :], S[:, kc, :], self.mean[:, :], ALU.subtract, reads=[sres(kc), ('mean',)], writes=[tr])
            self.stt(tmp[:, :], tmp[:, :], g_ap[:, kc:kc + 1], self.rstd[:, :], ALU.mult, ALU.mult,
                     reads=[tr, ('rstd',), ('lnp',)], writes=[tr])
            self.act(Dst[:, kc, :], tmp[:, :], AF.Identity, reads=[tr, ('lnp',)], writes=[dres(kc)], bias=b_ap[:, kc:kc + 1])
            if Bd is not None:
                self.act(Bd[:, kc, :], Dst[:, kc, :], AF.Copy, reads=[dres(kc)], writes=[bres(kc)])


def build_tail(layer_has_inproj=False):
    P = TokProg()
    xT = P.dram_in("xT", [D, T])
    aT = P.dram_in("aT", [D, T])
    w_out = P.wparam("w_out", D, D)
    w1 = P.wparam("w1", D, 4 * D)
    w2 = P.wparam("w2", 4 * D, D)
    lnp = P.dram_in("lnp", [128, 4, 16])
    yT = P.dram_out("yT", [D, T])

    X = P.sb("X", [128, 16, TH], F32)
    Rb = P.sb("R", [128, 16, TH], F32)
    Bb = P.sb("Bb", [128, 16, TH], BF16)
    H = P.sb("H", [128, 64, TH], BF16)
    lnp_sb = P.sb("lnp_sb", [128, 4, 16], F32)
    P.setup_common()
    xres = lambda kc: ('X', kc)
    rres = lambda kc: ('R', kc)
    bres = lambda kc: ('B', kc)
    hres = lambda kc: ('H', kc)
    xTv = xT.rearrange("(kc p) t -> p kc t", p=128)
    aTv = aT.rearrange("(kc p) t -> p kc t", p=128)
    yTv = yT.rearrange("(kc p) t -> p kc t", p=128)

    def body():
        P.start()
        P.dcp('sp', lnp_sb[:, :, :], lnp[:, :, :], (), [('lnp',)], 'lnp')
        outs = []
        for half in range(T // TH):
            t0 = half * TH
            for kc in range(16):
                P.dcp('sp', X[:, kc, :], xTv[:, kc, t0:t0 + TH], (), [xres(kc)], ('X', kc))
                P.dcp('sp', Rb[:, kc, :], aTv[:, kc, t0:t0 + TH], (), [rres(kc)], ('R', kc))
                P.act(Bb[:, kc, :], Rb[:, kc, :], AF.Copy, reads=[rres(kc)], writes=[bres(kc)])

            def ev1(n, ps, pr):
                P.stt(Rb[:, n, :], X[:, n, :], ALPHA, ps[:, :], ALU.mult, ALU.add, reads=[xres(n), pr], writes=[rres(n)])
            P.lin(w_out, D, D, lambda kc: Bb[:, kc, :], bres, ev1)
            P.layernorm(Rb, rres, X, xres, Bb, bres, lnp_sb[:, 0, :], lnp_sb[:, 1, :])

            def ev2(n, ps, pr):
                P.act(H[:, n, :], ps[:, :], AF.Relu, reads=[pr], writes=[hres(n)])
                P.tt('dve', H[:, n, :], H[:, n, :], H[:, n, :], ALU.mult, reads=[hres(n)], writes=[hres(n)])
            P.lin(w1, D, 4 * D, lambda kc: Bb[:, kc, :], bres, ev2)

            def ev3(n, ps, pr):
                P.stt(X[:, n, :], X[:, n, :], ALPHA, ps[:, :], ALU.mult, ALU.add, reads=[xres(n), pr], writes=[xres(n)])
            P.lin(w2, 4 * D, D, lambda kc: H[:, kc, :], hres, ev3)
            P.layernorm(X, xres, Rb, rres, None, None, lnp_sb[:, 2, :], lnp_sb[:, 3, :])
            for kc in range(16):
                P.dcp('sp', yTv[:, kc, t0:t0 + TH], Rb[:, kc, :], [rres(kc)], [('out', half, kc)], ('R', kc))
                outs.append(('out', half, kc))
        return outs

    P.plan = True
    body()
    P.plan = False
    outs = body()
    return P.finish(outs)


def lnp_layout(ln_g, ln_b, layer):
    a = np.stack([ln_g[layer, 0], ln_b[layer, 0], ln_g[layer, 1], ln_b[layer, 1]], 0)
    return np.ascontiguousarray(a.reshape(4, 16, 128).transpose(2, 0, 1))


def run_tail(x_tok, a_tok, w_out, w1, w2, lnp):
    nc = build_tail()
    in_maps = []
    for c in range(NCORES):
        sl = slice(c * T, (c + 1) * T)
        in_maps.append({"xT": np.ascontiguousarray(x_tok[sl].T), "aT": np.ascontiguousarray(a_tok[sl].T),
                        "w_out": w_out, "w1": w1, "w2": w2, "lnp": lnp})
    res = run_bass_kernel_spmd(nc, in_maps, core_ids=list(range(NCORES)))
    return np.concatenate([r["yT"].T for r in res.results], 0)


def rope_tables(pos, half, theta):
    inv_freq = (np.float32(1.0) / np.power(np.float32(theta), (np.arange(half, dtype=np.float32) / np.float32(half)))).astype(np.float32)
    ang = (pos.astype(np.float32)[None, :] * inv_freq[:, None]).astype(np.float32)
    return np.cos(ang.astype(np.float64)).astype(np.float32), np.sin(ang.astype(np.float64)).astype(np.float32)


def build_inproj_even():
    P = TokProg()
    NOUT = 5120
    xT = P.dram_in("xT", [D, T])
    w_in = P.wparam("w_in", D, NOUT)
    cosT = P.dram_in("cosT", [128, T])
    sinT = P.dram_in("sinT", [128, T])
    hT = P.dram_out("hT", [NOUT, T])
    Rb = P.sb("R", [128, 16, TH], F32)
    Bb = P.sb("Bb", [128, 16, TH], BF16)
    O = P.sb("O", [128, 40, TH], F32)
    cs = P.sb("cs", [128, 2, T], F32)
    P.setup_common()
    rres = lambda kc: ('R', kc)
    bres = lambda kc: ('B', kc)
    xTv = xT.rearrange("(kc p) t -> p kc t", p=128)
    hTv = hT.rearrange("(kc p) t -> p kc t", p=128)

    def body():
        P.start()
        P.dcp('sp', cs[:, 0, :], cosT[:, :], (), [('cs',)], 'cs0')
        P.dcp('sp', cs[:, 1, :], sinT[:, :], (), [('cs',)], 'cs1')
        outs = []
        for half in range(T // TH):
            t0 = half * TH
            for kc in range(16):
                P.dcp('sp', Rb[:, kc, :], xTv[:, kc, t0:t0 + TH], (), [rres(kc)], ('R', kc))
                P.act(Bb[:, kc, :], Rb[:, kc, :], AF.Copy, reads=[rres(kc)], writes=[bres(kc)])
            pend = {}
            cosv = cs[:, 0, t0:t0 + TH]
            sinv = cs[:, 1, t0:t0 + TH]

            def ev(n, ps, pr):
                ores = ('O', n)
                if 8 <= n < 24:
                    if n % 2 == 0:
                        pend['a'] = (ps, pr)
                        return
                    pa, pra = pend['a']
                    t1 = P.tmp[0]; t2 = P.tmp[1]
                    P.tt('dve', t1[:, :], pa[:, :], cosv, ALU.mult, reads=[pra, ('cs',)], writes=[('tmp', 0)])
                    P.tt('dve', t2[:, :], ps[:, :], sinv, ALU.mult, reads=[pr, ('cs',)], writes=[('tmp', 1)])
                    P.tt('dve', O[:, n - 1, :], t1[:, :], t2[:, :], ALU.subtract, reads=[('tmp', 0), ('tmp', 1)], writes=[('O', n - 1)])
                    P.tt('dve', t1[:, :], ps[:, :], cosv, ALU.mult, reads=[pr, ('cs',)], writes=[('tmp', 0)])
                    P.tt('dve', t2[:, :], pa[:, :], sinv, ALU.mult, reads=[pra, ('cs',)], writes=[('tmp', 1)])
                    P.tt('dve', O[:, n, :], t1[:, :], t2[:, :], ALU.add, reads=[('tmp', 0), ('tmp', 1)], writes=[('O', n)])
                    for m in (n - 1, n):
                        P.dcp('sp', hTv[:, m, t0:t0 + TH], O[:, m, :], [('O', m)], [('out', half, m)], ('O', m))
                        outs.append(('out', half, m))
                else:
                    P.act(O[:, n, :], ps[:, :], AF.Copy, reads=[pr], writes=[ores])
                    P.dcp('sp', hTv[:, n, t0:t0 + TH], O[:, n, :], [ores], [('out', half, n)], ('O', n))
                    outs.append(('out', half, n))
            P.lin(w_in, D, NOUT, lambda kc: Bb[:, kc, :], bres, ev)
        return outs

    P.plan = True
    body()
    P.plan = False
    outs = body()
    return P.finish(outs)


def run_inproj_even(x_tok, w_in):
    nc = build_inproj_even()
    in_maps = []
    for c in range(NCORES):
        sl = slice(c * T, (c + 1) * T)
        cosT, sinT = rope_tables(np.arange(c * T, (c + 1) * T), 128, 10000.0)
        in_maps.append({"xT": np.ascontiguousarray(x_tok[sl].T), "w_in": w_in, "cosT": cosT, "sinT": sinT})
    res = run_bass_kernel_spmd(nc, in_maps, core_ids=list(range(NCORES)))
    return np.concatenate([r["hT"].T for r in res.results], 0)


NB5 = L // TH


def s5_disc(P, pre, F):
    names = ['step', 'lr', 'ex', 'mag', 'th', 's', 'c', 't1', 't2', 's2', 'c2', 'ar', 'ai', 'nr', 'den', 'zr', 'zi', 't3']
    tl = {n: P.sb(f"{pre}_{n}", [128, F], F32) for n in names}
    r = lambda n: (pre, n)
    A = lambda n: tl[n][:, :]

    def run(lamre, lamim, lstep):
        P.act(A('step'), lstep, AF.Exp, reads=[(pre, 'in')], writes=[r('step')])
        P.ts('dve', A('lr'), lamre, -1e-4, None, ALU.min, None, reads=[(pre, 'in')], writes=[r('lr')])
        P.tt('dve', A('ex'), A('lr'), A('step'), ALU.mult, reads=[r('lr'), r('step')], writes=[r('ex')])
        P.act(A('mag'), A('ex'), AF.Exp, reads=[r('ex')], writes=[r('mag')])
        P.tt('dve', A('th'), lamim, A('step'), ALU.mult, reads=[(pre, 'in'), r('step')], writes=[r('th')])
        P.act(A('s'), A('th'), AF.Sin, reads=[r('th')], writes=[r('s')], scale=1.0 / 16)
        P.act(A('c'), A('th'), AF.Sin, reads=[r('th'), ('halfpi',)], writes=[r('c')], scale=1.0 / 16, bias=P.halfpi[:, 0:1])
        cs, ss = 'c', 's'
        for it in range(4):
            cn, sn = ('c2', 's2') if it % 2 == 0 else ('c', 's')
            P.tt('dve', A('t1'), A(cs), A(cs), ALU.mult, reads=[r(cs)], writes=[r('t1')])
            P.tt('dve', A('t2'), A(ss), A(ss), ALU.mult, reads=[r(ss)], writes=[r('t2')])
            P.stt(A(sn), A(cs), 2.0, A(ss), ALU.mult, ALU.mult, reads=[r(cs), r(ss)], writes=[r(sn)])
            P.tt('dve', A(cn), A('t1'), A('t2'), ALU.subtract, reads=[r('t1'), r('t2')], writes=[r(cn)])
            cs, ss = cn, sn
        assert cs == 'c'
        P.tt('dve', A('ar'), A('mag'), A('c'), ALU.mult, reads=[r('mag'), r('c')], writes=[r('ar')])
        P.tt('dve', A('ai'), A('mag'), A('s'), ALU.mult, reads=[r('mag'), r('s')], writes=[r('ai')])
        P.ts('dve', A('nr'), A('ar'), -1.0, None, ALU.add, None, reads=[r('ar')], writes=[r('nr')])
        P.tt('dve', A('t1'), A('lr'), A('lr'), ALU.mult, reads=[r('lr')], writes=[r('t1')])
        P.tt('dve', A('t2'), lamim, lamim, ALU.mult, reads=[(pre, 'in')], writes=[r('t2')])
        P.tt('dve', A('den'), A('t1'), A('t2'), ALU.add, reads=[r('t1'), r('t2')], writes=[r('den')])
        P.emit('dve', lambda e: e.reciprocal(out=A('den'), in_=A('den')), reads=[r('den')], writes=[r('den')])
        P.tt('dve', A('t1'), A('nr'), A('lr'), ALU.mult, reads=[r('nr'), r('lr')], writes=[r('t1')])
        P.tt('dve', A('t2'), A('ai'), lamim, ALU.mult, reads=[r('ai'), (pre, 'in')], writes=[r('t2')])
        P.tt('dve', A('t3'), A('t1'), A('t2'), ALU.add, reads=[r('t1'), r('t2')], writes=[r('t3')])
        P.tt('dve', A('zr'), A('t3'), A('den'), ALU.mult, reads=[r('t3'), r('den')], writes=[r('zr')])
        P.tt('dve', A('t1'), A('ai'), A('lr'), ALU.mult, reads=[r('ai'), r('lr')], writes=[r('t1')])
        P.tt('dve', A('t2'), A('nr'), lamim, ALU.mult, reads=[r('nr'), (pre, 'in')], writes=[r('t2')])
        P.tt('dve', A('t3'), A('t1'), A('t2'), ALU.subtract, reads=[r('t1'), r('t2')], writes=[r('t3')])
        P.tt('dve', A('zi'), A('t3'), A('den'), ALU.mult, reads=[r('t3'), r('den')], writes=[r('zi')])
    return tl, run


def build_s5():
    P = Prog()
    uT = P.dram_in("uT", [2, 128, L])
    prow = P.dram_in("prow", [128, 3, 1024])
    pcol = P.dram_in("pcol", [128, 3, 8])
    BT = P.dram_in("BT", [128, 2, 1024])
    CT = P.dram_in("CT", [128, 2, 8, 128])
    dsk = P.dram_in("dsk", [128, 1])
    yT = P.dram_out("yT", [2, 128, L])

    u32 = [[P.sb(f"u32_{d}_{i}", [128, TH], F32) for i in range(2)] for d in range(2)]
    ub = [[P.sb(f"ub{d}_{i}", [128, TH], BF16) for i in range(2)] for d in range(2)]
    prow_sb = P.sb("prow_sb", [128, 3, 1024], F32)
    pcol_sb = P.sb("pcol_sb", [128, 3, 8], F32)
    BT_sb = P.sb("BT_sb", [128, 2, 1024], F32)
    CT_sb = P.sb("CT_sb", [128, 2, 8, 128], F32)
    dsk_sb = P.sb("dsk_sb", [128, 1], F32)
    P.halfpi = P.sb("halfpi", [128, 1], F32)
    Bbar = P.sb("Bbar", [128, 2, 1024], BF16)
    Cb = P.sb("Cb", [128, 2, 8, 128], BF16)
    tw = [P.sb(f"tw{i}", [128, 512], F32) for i in range(2)]
    tabc = P.sb("tabc", [128, 8, TH], F32)
    tabs = P.sb("tabs", [128, 8, TH], F32)
    magT = P.sb("magT", [128, 8, TH], F32)
    cur = [P.sb(f"cur{i}", [128, 2, 8], F32) for i in range(2)]
    curt = P.sb("curt", [128, 2, 8], F32)
    ttmp = P.sb("ttmp", [128, TH], F32)
    init = P.sb("init", [128, 2, 8], F32)
    ini_t = P.sb("ini_t", [128, 2, 8], F32)
    NW = 2
    m = [[P.sb(f"m{i}_{k}", [128, TH], F32) for k in range(4)] for i in range(NW)]
    v = [[P.sb(f"v{i}_{k}", [128, TH], F32) for k in range(2)] for i in range(NW)]
    w = [[P.sb(f"w{i}_{k}", [128, TH], F32) for k in range(2)] for i in range(NW)]
    pm = [[P.sb(f"pm{i}_{k}", [128, TH], F32) for k in range(4)] for i in range(NW)]
    hb = [[P.sb(f"hb{i}_{k}", [128, TH], BF16) for k in range(2)] for i in range(NW)]
    yo = [P.sb(f"yo{i}", [128, TH], F32) for i in range(2)]
    psb = [P.ps(f"psb{i}", [128, 512]) for i in range(4)]
    psy = [P.ps(f"psy{i}", [128, 512]) for i in range(2)]

    rowt, rowrun = s5_disc(P, 'row', 512)
    colt, colrun = s5_disc(P, 'col', 8)

    P.dcp = lambda eng, out, in_, reads, writes, key: P.dma(eng, lambda e: e.dma_start(out=out, in_=in_), reads=reads, writes=writes, key=key)
    P.dcp('sp', prow_sb[:, :, :], prow[:, :, :], (), [('row', 'in')], 'prow')
    P.dcp('sp', pcol_sb[:, :, :], pcol[:, :, :], (), [('col', 'in')], 'pcol')
    P.dcp('sp', BT_sb[:, :, :], BT[:, :, :], (), [('BT',)], 'BT')
    P.dcp('sp', CT_sb[:, :, :, :], CT[:, :, :, :], (), [('CT',)], 'CT')
    P.dcp('sp', dsk_sb[:, :], dsk[:, :], (), [('dsk',)], 'dsk')
    P.emit('dve', lambda e: e.memset(P.halfpi[:, :], float(np.pi / 2)), writes=[('halfpi',)])
    colrun(pcol_sb[:, 0, :], pcol_sb[:, 1, :], pcol_sb[:, 2, :])
    for d in range(2):
        ds_ = slice(d * 512, (d + 1) * 512)
        rowrun(prow_sb[:, 0, ds_], prow_sb[:, 1, ds_], prow_sb[:, 2, ds_])
        zr, zi = rowt['zr'][:, :], rowt['zi'][:, :]
        zres = [('row', 'zr'), ('row', 'zi'), ('BT',)]
        P.tt('dve', tw[0][:, :], zr, BT_sb[:, 0, ds_], ALU.mult, reads=zres, writes=[('tw', 0)])
        P.tt('dve', tw[1][:, :], zi, BT_sb[:, 1, ds_], ALU.mult, reads=zres, writes=[('tw', 1)])
        P.tt('dve', Bbar[:, 0, ds_], tw[0][:, :], tw[1][:, :], ALU.subtract, reads=[('tw', 0), ('tw', 1)], writes=[('Bbar',)])
        P.tt('dve', tw[0][:, :], zr, BT_sb[:, 1, ds_], ALU.mult, reads=zres + [('Bbar',)], writes=[('tw', 0)])
        P.tt('dve', tw[1][:, :], zi, BT_sb[:, 0, ds_], ALU.mult, reads=zres + [('Bbar',)], writes=[('tw', 1)])
        P.tt('dve', Bbar[:, 1, ds_], tw[0][:, :], tw[1][:, :], ALU.add, reads=[('tw', 0), ('tw', 1)], writes=[('Bbar',)])
    P.act(Cb[:, 0, :, :], CT_sb[:, 0, :, :], AF.Copy, reads=[('CT',)], writes=[('Cb',)])
    P.act(Cb[:, 1, :, :], CT_sb[:, 1, :, :], AF.Copy, reads=[('CT',)], writes=[('Cb',)], scale=-1.0)
    for dj in range(8):
        P.emit('dve', lambda e, dj=dj: e.memset(magT[:, dj, :], 1.0), writes=[('magT',)])
        P.ts('dve', magT[:, dj, :], magT[:, dj, :], colt['mag'][:, dj:dj + 1], None, ALU.mult, None,
             reads=[('magT',), ('col', 'mag')], writes=[('magT',)])
    P.emit('dve', lambda e: e.memset(tabc[:, :, 0:1], 1.0), writes=[('tab',)])
    P.emit('dve', lambda e: e.memset(tabs[:, :, 0:1], 0.0), writes=[('tab',)])
    P.emit('dve', lambda e: e.tensor_copy(out=cur[0][:, 0, :], in_=colt['c'][:, :]), reads=[('col', 'c')], writes=[('cur', 0)])
    P.emit('dve', lambda e: e.tensor_copy(out=cur[0][:, 1, :], in_=colt['s'][:, :]), reads=[('col', 's')], writes=[('cur', 0)])
    ci = 0
    for k in range(9):
        n = 1 << k
        cc = cur[ci]
        for dj in range(8):
            cs_c = cc[:, 0, dj:dj + 1]; cs_s = cc[:, 1, dj:dj + 1]
            rd = [('tab',), ('cur', ci)]
            P.ts('dve', ttmp[:, 0:n], tabs[:, dj, 0:n], cs_s, None, ALU.mult, None, reads=rd, writes=[('ttmp',)])
            P.stt(tabc[:, dj, n:2 * n], tabc[:, dj, 0:n], cs_c, ttmp[:, 0:n], ALU.mult, ALU.subtract, reads=rd + [('ttmp',)], writes=[('tab',)])
            P.ts('dve', ttmp[:, 0:n], tabs[:, dj, 0:n], cs_c, None, ALU.mult, None, reads=rd, writes=[('ttmp',)])
            P.stt(tabs[:, dj, n:2 * n], tabc[:, dj, 0:n], cs_s, ttmp[:, 0:n], ALU.mult, ALU.add, reads=rd + [('ttmp',)], writes=[('tab',)])
        cn = cur[1 - ci]
        P.tt('dve', curt[:, 0, :], cc[:, 0, :], cc[:, 0, :], ALU.mult, reads=[('cur', ci)], writes=[('curt',)])
        P.tt('dve', curt[:, 1, :], cc[:, 1, :], cc[:, 1, :], ALU.mult, reads=[('cur', ci)], writes=[('curt',)])
        P.tt('dve', cn[:, 0, :], curt[:, 0, :], curt[:, 1, :], ALU.subtract, reads=[('curt',)], writes=[('cur', 1 - ci)])
        P.stt(cn[:, 1, :], cc[:, 0, :], 2.0, cc[:, 1, :], ALU.mult, ALU.mult, reads=[('cur', ci)], writes=[('cur', 1 - ci)])
        ci = 1 - ci
    Rc = cur[ci]
    Rres = ('cur', ci)
    P.emit('dve', lambda e: e.memset(init[:, :, :], 0.0), writes=[('init',)])

    outs = []
    wi_ = 0
    pbi = 0
    for b in range(NB5):
        t0 = b * TH
        ui = b % 2
        for d in range(2):
            P.dcp('sp', u32[d][ui][:, :], uT[d, :, t0:t0 + TH], (), [('u32', d, ui)], ('u32', d, ui))
            P.act(ub[d][ui][:, :], u32[d][ui][:, :], AF.Copy, reads=[('u32', d, ui)], writes=[('ub', d, ui)])
        for d in range(2):
            py = psy[d]; pyr = ('psy', d)
            for j in range(4):
                dj = d * 4 + j
                i = wi_ % NW; wi_ += 1
                pr_ = psb[(pbi * 2) % 4]; pi_ = psb[(pbi * 2 + 1) % 4]
                prr = ('psb', (pbi * 2) % 4); pir = ('psb', (pbi * 2 + 1) % 4); pbi += 1
                col0 = d * 512 + j * 128
                P.mm(pr_[:, :], Bbar[:, 0, col0:col0 + 128], ub[d][ui][:, :], True, True, reads=[('Bbar',), ('ub', d, ui)], writes=[prr])
                P.mm(pi_[:, :], Bbar[:, 1, col0:col0 + 128], ub[d][ui][:, :], True, True, reads=[('Bbar',), ('ub', d, ui)], writes=[pir])
                tc = tabc[:, dj, :]; tsn = tabs[:, dj, :]
                mr = lambda k: ('m', i, k)
                P.tt('dve', m[i][0][:, :], pr_[:, :], tc, ALU.mult, reads=[prr, ('tab',)], writes=[mr(0)])
                P.tt('dve', m[i][1][:, :], pi_[:, :], tsn, ALU.mult, reads=[pir, ('tab',)], writes=[mr(1)])
                P.tt('dve', m[i][2][:, :], pi_[:, :], tc, ALU.mult, reads=[pir, ('tab',)], writes=[mr(2)])
                P.tt('dve', m[i][3][:, :], pr_[:, :], tsn, ALU.mult, reads=[prr, ('tab',)], writes=[mr(3)])
                P.tt('dve', v[i][0][:, :], m[i][0][:, :], m[i][1][:, :], ALU.add, reads=[mr(0), mr(1)], writes=[('v', i, 0)])
                P.tt('dve', v[i][1][:, :], m[i][2][:, :], m[i][3][:, :], ALU.subtract, reads=[mr(2), mr(3)], writes=[('v', i, 1)])
                for k in range(2):
                    P.emit('dve', lambda e, i=i, k=k, dj=dj: e.tensor_tensor_scan(
                        out=w[i][k][:, :], data0=magT[:, dj, :], data1=v[i][k][:, :], initial=init[:, k, dj:dj + 1],
                        op0=ALU.mult, op1=ALU.add), reads=[('magT',), ('v', i, k), ('init',)], writes=[('w', i, k)])
                wl_r = w[i][0][:, TH - 1:TH]; wl_i = w[i][1][:, TH - 1:TH]
                wres = [('w', i, 0), ('w', i, 1), Rres]
                P.tt('dve', ini_t[:, 0, dj:dj + 1], wl_i, Rc[:, 1, dj:dj + 1], ALU.mult, reads=wres, writes=[('ini_t',)])
                P.tt('dve', ini_t[:, 1, dj:dj + 1], wl_i, Rc[:, 0, dj:dj + 1], ALU.mult, reads=wres, writes=[('ini_t',)])
                P.stt(init[:, 0, dj:dj + 1], wl_r, Rc[:, 0, dj:dj + 1], ini_t[:, 0, dj:dj + 1], ALU.mult, ALU.subtract,
                      reads=wres + [('ini_t',)], writes=[('init',)])
                P.stt(init[:, 1, dj:dj + 1], wl_r, Rc[:, 1, dj:dj + 1], ini_t[:, 1, dj:dj + 1], ALU.mult, ALU.add,
                      reads=wres + [('ini_t',)], writes=[('init',)])
                pmr = lambda k: ('pm', i, k)
                P.tt('dve', pm[i][0][:, :], w[i][0][:, :], tc, ALU.mult, reads=[('w', i, 0), ('tab',)], writes=[pmr(0)])
                P.tt('dve', pm[i][1][:, :], w[i][1][:, :], tsn, ALU.mult, reads=[('w', i, 1), ('tab',)], writes=[pmr(1)])
                P.tt('pool', pm[i][2][:, :], w[i][0][:, :], tsn, ALU.mult, reads=[('w', i, 0), ('tab',)], writes=[pmr(2)])
                P.tt('pool', pm[i][3][:, :], w[i][1][:, :], tc, ALU.mult, reads=[('w', i, 1), ('tab',)], writes=[pmr(3)])
                P.tt('dve', hb[i][0][:, :], pm[i][0][:, :], pm[i][1][:, :], ALU.subtract, reads=[pmr(0), pmr(1)], writes=[('hb', i, 0)])
                P.tt('dve', hb[i][1][:, :], pm[i][2][:, :], pm[i][3][:, :], ALU.add, reads=[pmr(2), pmr(3)], writes=[('hb', i, 1)])
                P.mm(py[:, :], Cb[:, 0, dj, :], hb[i][0][:, :], j == 0, False, reads=[('Cb',), ('hb', i, 0)], writes=[pyr])
                P.mm(py[:, :], Cb[:, 1, dj, :], hb[i][1][:, :], False, j == 3, reads=[('Cb',), ('hb', i, 1)], writes=[pyr])
            yr = ('yo', d)
            if d == 0:
                P.stt(yo[d][:, :], u32[0][ui][:, :], dsk_sb[:, 0:1], py[:, :], ALU.mult, ALU.add,
                      reads=[('u32', 0, ui), ('dsk',), pyr], writes=[yr])
            else:
                P.act(yo[d][:, :], py[:, :], AF.Copy, reads=[pyr], writes=[yr])
            P.dcp('sp', yT[d, :, t0:t0 + TH], yo[d][:, :], [yr], [('out', b, d)], ('yo', d))
            outs.append(('out', b, d))
    return P.finish(outs)


def s5_host_layout(inp, c):
    g0 = 8 * c
    lamre = inp['s5_lambda_re'][0][:, g0:g0 + 8, :]
    lamim = inp['s5_lambda_im'][0][:, g0:g0 + 8, :]
    lstep = np.broadcast_to(inp['s5_log_step'][0][:, g0:g0 + 8, None], (2, 8, 64))
    st = np.stack([lamre, lamim, lstep], 0).reshape(3, 2, 512)
    prow = np.ascontiguousarray(np.broadcast_to(st.reshape(1, 3, 1024), (128, 3, 1024))).astype(np.float32)
    pcol = np.ascontiguousarray(st.reshape(3, 2, 4, 128).transpose(3, 0, 1, 2).reshape(128, 3, 8)).astype(np.float32)
    BT = np.zeros((128, 2, 2, 512), np.float32)
    CT = np.zeros((128, 2, 2, 4, 128), np.float32)
    for ri, (bk, ck) in enumerate([('s5_b_re', 's5_c_re'), ('s5_b_im', 's5_c_im')]):
        Bm = inp[bk][0][:, g0:g0 + 8]
        Cm = inp[ck][0][:, g0:g0 + 8]
        for d in range(2):
            for g in range(8):
                BT[g * 16:(g + 1) * 16, ri, d, g * 64:(g + 1) * 64] = Bm[d, g].T
                j, gg = g // 2, g % 2
                CT[gg * 64:(gg + 1) * 64, ri, d, j, g * 16:(g + 1) * 16] = Cm[d, g].T
    dsk = inp['s5_d'][0][g0:g0 + 8].reshape(128, 1).astype(np.float32)
    return {"prow": prow, "pcol": pcol, "BT": BT.reshape(128, 2, 1024), "CT": CT.reshape(128, 2, 8, 128), "dsk": np.ascontiguousarray(dsk)}


def run_s5(u_tok, inp):
    nc = build_s5()
    in_maps = []
    for c in range(NCORES):
        uc = u_tok[:, c * 128:(c + 1) * 128].T
        m = s5_host_layout(inp, c)
        m["uT"] = np.ascontiguousarray(np.stack([uc, uc[:, ::-1]], 0))
        in_maps.append(m)
    res = run_bass_kernel_spmd(nc, in_maps, core_ids=list(range(NCORES)))
    yf = np.concatenate([r["yT"][0].T for r in res.results], 1)
    yb = np.concatenate([r["yT"][1][:, ::-1].T for r in res.results], 1)
    return yf, yb


def build_ret():
    P = Prog()
    NCH = L // 128
    qT = P.dram_in("qT", [2, 128, L])
    kT = P.dram_in("kT", [2, 128, L])
    ktok = P.dram_in("ktok", [L, 256])
    vtok = P.dram_in("vtok", [L, 256])
    lg = P.dram_in("lg", [128, 1])
    maskT = P.dram_in("maskT", [128, 128])
    irow = P.dram_in("irow", [128, TH])
    icol = P.dram_in("icol", [128, 1])
    o = P.dram_out("o", [L, 256])
    P.dcp = lambda eng, out, in_, reads, writes, key: P.dma(eng, lambda e: e.dma_start(out=out, in_=in_), reads=reads, writes=writes, key=key)

    lg_sb = P.sb("lg_sb", [128, 1], F32)
    lgn = P.sb("lgn", [128, 1], F32)
    lgp = P.sb("lgp", [128, 1], F32)
    mask_sb = P.sb("mask_sb", [128, 128], F32)
    irow_sb = P.sb("irow_sb", [128, TH], F32)
    icol_sb = P.sb("icol_sb", [128, 1], F32)
    gq = P.sb("gq", [128, TH], F32)
    gk = P.sb("gk", [128, TH], F32)
    gkc = P.sb("gkc", [128, 1], F32)
    gC = P.sb("gC", [128, 1], F32)
    c128 = P.sb("c128", [128, 1], F32)
    q32 = [P.sb(f"q32_{i}", [128, 2, TH], F32) for i in range(2)]
    k32 = [P.sb(f"k32_{i}", [128, 2, TH], F32) for i in range(2)]
    qb = [P.sb(f"qb_{i}", [128, 2, TH], BF16) for i in range(2)]
    kb = [P.sb(f"kb_{i}", [128, 2, TH], BF16) for i in range(2)]
    kt32 = [P.sb(f"kt32_{i}", [128, 4, 256], F32) for i in range(2)]
    vt32 = [P.sb(f"vt32_{i}", [128, 4, 256], F32) for i in range(2)]
    ktb = [P.sb(f"ktb_{i}", [128, 4, 256], BF16) for i in range(2)]
    vtb = [P.sb(f"vtb_{i}", [128, 4, 256], BF16) for i in range(2)]
    Sm = [P.sb(f"Sm{i}", [128, 128], BF16) for i in range(2)]
    Tst = P.sb("Tst", [128, 2, 256], F32)
    Sbf = P.sb("Sbf", [128, 2, 256], BF16)
    osb = [P.sb(f"osb{i}", [128, 256], F32) for i in range(2)]
    psS = [P.ps(f"psS{i}", [128, 512]) for i in range(2)]
    psO = [P.ps(f"psO{i}", [128, 512]) for i in range(2)]
    psK = [P.ps(f"psK{i}", [128, 512]) for i in range(2)]

    P.dcp('sp', lg_sb[:, :], lg[:, :], (), [('lg',)], 'lg')
    P.dcp('sp', mask_sb[:, :], maskT[:, :], (), [('mask',)], 'mask')
    P.dcp('sp', irow_sb[:, :], irow[:, :], (), [('irow',)], 'irow')
    P.dcp('sp', icol_sb[:, :], icol[:, :], (), [('icol',)], 'icol')
    P.act(lgp[:, :], lg_sb[:, :], AF.Abs, reads=[('lg',)], writes=[('lgp',)])
    P.ts('dve', lgn[:, :], lgp[:, :], -1.0, None, ALU.mult, None, reads=[('lgp',)], writes=[('lgn',)])
    P.act(gq[:, :], irow_sb[:, :], AF.Exp, reads=[('irow',), ('lgn',)], writes=[('gq',)], scale=lgn[:, 0:1])
    P.act(gk[:, :], irow_sb[:, :], AF.Exp, reads=[('irow',), ('lgp',)], writes=[('gk',)], scale=lgp[:, 0:1])
    P.ts('dve', gk[:, :], gk[:, :], 1.0 / 16, None, ALU.mult, None, reads=[('gk',)], writes=[('gk',)])
    P.act(gkc[:, :], icol_sb[:, :], AF.Exp, reads=[('icol',), ('lgp',)], writes=[('gkc',)], scale=lgp[:, 0:1])
    P.ts('dve', gkc[:, :], gkc[:, :], 1.0 / 16, None, ALU.mult, None, reads=[('gkc',)], writes=[('gkc',)])
    P.emit('dve', lambda e: e.memset(c128[:, :], 128.0), writes=[('c128',)])
    P.act(gC[:, :], c128[:, :], AF.Exp, reads=[('c128',), ('lgn',)], writes=[('gC',)], scale=lgn[:, 0:1])
    P.emit('dve', lambda e: e.memset(Tst[:, :, :], 0.0), writes=[('Tst', 0), ('Tst', 1)])

    outs = []
    for sbk in range(L // TH):
        t0 = sbk * TH
        bi = sbk % 2
        for kc in range(2):
            P.dcp('sp', q32[bi][:, kc, :], qT[kc, :, t0:t0 + TH], (), [('q32', bi, kc)], ('q32', bi, kc))
            P.dcp('sp', k32[bi][:, kc, :], kT[kc, :, t0:t0 + TH], (), [('k32', bi, kc)], ('k32', bi, kc))
            P.tt('dve', qb[bi][:, kc, :], q32[bi][:, kc, :], gq[:, :], ALU.mult, reads=[('q32', bi, kc), ('gq',)], writes=[('qb', bi, kc)])
            P.tt('dve', kb[bi][:, kc, :], k32[bi][:, kc, :], gk[:, :], ALU.mult, reads=[('k32', bi, kc), ('gk',)], writes=[('kb', bi, kc)])
        P.dcp('sp', kt32[bi][:, :, :], ktok[t0:t0 + TH, :].rearrange("(c p) e -> p c e", p=128), (), [('kt32', bi)], ('kt32', bi))
        P.dcp('sp', vt32[bi][:, :, :], vtok[t0:t0 + TH, :].rearrange("(c p) e -> p c e", p=128), (), [('vt32', bi)], ('vt32', bi))
        P.ts('dve', ktb[bi][:, :, :], kt32[bi][:, :, :], gkc[:, 0:1], None, ALU.mult, None, reads=[('kt32', bi), ('gkc',)], writes=[('ktb', bi)])
        P.act(vtb[bi][:, :, :], vt32[bi][:, :, :], AF.Copy, reads=[('vt32', bi)], writes=[('vtb', bi)])
        for cc in range(4):
            n = sbk * 4 + cc
            cs = slice(cc * 128, (cc + 1) * 128)
            pS = psS[n % 2]; pSr = ('psS', n % 2)
            pO = psO[n % 2]; pOr = ('psO', n % 2)
            for kc in range(2):
                P.mm(pS[:, 0:128], kb[bi][:, kc, cs], qb[bi][:, kc, cs], kc == 0, kc == 1,
                     reads=[('kb', bi, kc), ('qb', bi, kc)], writes=[pSr])
            sm = Sm[n % 2]; smr = ('Sm', n % 2)
            P.tt('dve', sm[:, :], pS[:, 0:128], mask_sb[:, :], ALU.mult, reads=[pSr, ('mask',)], writes=[smr])
            P.mm(pO[:, 0:256], sm[:, :], vtb[bi][:, cc, :], True, n == 0, reads=[smr, ('vtb', bi)], writes=[pOr])
            if n > 0:
                for kc in range(2):
                    P.mm(pO[:, 0:256], qb[bi][:, kc, cs], Sbf[:, kc, :], False, kc == 1,
                         reads=[('qb', bi, kc), ('Sbf', kc)], writes=[pOr])
            ob = osb[n % 2]; obr = ('osb', n % 2)
            P.act(ob[:, :], pO[:, 0:256], AF.Copy, reads=[pOr], writes=[obr])
            P.dcp('sp', o[n * 128:(n + 1) * 128, :], ob[:, :], [obr], [('out', n)], ('osb', n % 2))
            outs.append(('out', n))
            if n < NCH - 1:
                for kc in range(2):
                    pK = psK[kc]; pKr = ('psK', kc)
                    P.mm(pK[:, 0:256], ktb[bi][:, cc, kc * 128:(kc + 1) * 128], vtb[bi][:, cc, :], True, True,
                         reads=[('ktb', bi), ('vtb', bi)], writes=[pKr])
                    P.stt(Tst[:, kc, :], Tst[:, kc, :], gC[:, 0:1], pK[:, 0:256], ALU.mult, ALU.add,
                          reads=[('Tst', kc), ('gC',), pKr], writes=[('Tst', kc)])
                    P.act(Sbf[:, kc, :], Tst[:, kc, :], AF.Copy, reads=[('Tst', kc), ('gC',)], writes=[('Sbf', kc)], scale=gC[:, 0:1])
    return P.finish(outs)


def run_ret(q_rot, k_rot, v, ret_log_decay):
    nc = build_ret()
    in_maps = []
    ii = np.arange(128)
    irow = np.ascontiguousarray(np.broadcast_to(((np.arange(TH) % 128) + 1).astype(np.float32)[None, :], (128, TH)))
    icol = (ii + 1).astype(np.float32).reshape(128, 1)
    for c in range(NCORES):
        hh, d = c // 2, c % 2
        qq, kk, vv = q_rot[:, hh], k_rot[:, hh], v[:, hh]
        if d == 1:
            qq, kk, vv = qq[::-1], kk[::-1], vv[::-1]
        mk = (ii[None, :] >= ii[:, None]) if d == 0 else (ii[None, :] > ii[:, None])
        in_maps.append({"qT": np.ascontiguousarray(qq.T.reshape(2, 128, L)), "kT": np.ascontiguousarray(kk.T.reshape(2, 128, L)),
                        "ktok": np.ascontiguousarray(kk), "vtok": np.ascontiguousarray(vv),
                        "lg": np.full((128, 1), ret_log_decay[d, hh], np.float32), "maskT": mk.astype(np.float32),
                        "irow": irow, "icol": icol})
    res = run_bass_kernel_spmd(nc, in_maps, core_ids=list(range(NCORES)))
    o_f = np.stack([res.results[2 * hh]["o"] for hh in range(4)], 1)
    o_b = np.stack([res.results[2 * hh + 1]["o"][::-1] for hh in range(4)], 1)
    return o_f, o_b


def build_attn():
    P = Prog()
    TK = T + 256
    NBLK = T // 128
    qT = P.dram_in("qT", [16, 128, T])
    kT = P.dram_in("kT", [4, 128, TK])
    vtok = P.dram_in("vtok", [TK, 512])
    sink = P.dram_in("sink", [128, 16])
    mbias = P.dram_in("mbias", [128, 3, 384])
    ident = P.dram_in("ident", [128, 128])
    oT = P.dram_out("oT", [16, 128, T])
    P.dcp = lambda eng, out, in_, reads, writes, key: P.dma(eng, lambda e: e.dma_start(out=out, in_=in_), reads=reads, writes=writes, key=key)

    qb = P.sb("qb", [128, 16, T], BF16)
    kb = P.sb("kb", [128, 4, TK], BF16)
    vb = P.sb("vb", [128, TK // 128, 512], BF16)
    sink_sb = P.sb("sink_sb", [128, 16], F32)
    mb_sb = P.sb("mb_sb", [128, 3, 384], F32)
    id32 = P.sb("id32", [128, 128], F32)
    idb = P.sb("idb", [128, 128], BF16)
    osb = P.sb("osb", [128, 16, T], F32)
    NR = 2
    Sb = [P.sb(f"Sb{i}", [128, 384], F32) for i in range(NR)]
    Pe = [P.sb(f"Pe{i}", [128, 384], BF16) for i in range(NR)]
    Pn = [P.sb(f"Pn{i}", [128, 384], BF16) for i in range(NR)]
    PT = [P.sb(f"PT{i}", [128, 3, 128], BF16) for i in range(NR)]
    sm = [P.sb(f"sm{i}", [128, 8], F32) for i in range(NR)]
    psS = [P.ps(f"psS{i}", [128, 512]) for i in range(2)]
    psT = [P.ps(f"psT{i}", [128, 3, 128], BF16) for i in range(2)]
    psO = [P.ps(f"psO{i}", [128, 512]) for i in range(2)]

    for h in range(16):
        P.dcp('pool', qb[:, h, :], qT[h, :, :], (), [('qb', h)], ('qb', h))
    for h in range(4):
        P.dcp('pool', kb[:, h, :], kT[h, :, :], (), [('kb', h)], ('kb', h))
    P.dcp('pool', vb[:, :, :], vtok.rearrange("(c p) e -> p c e", p=128), (), [('vb',)], 'vb')
    P.dcp('sp', sink_sb[:, :], sink[:, :], (), [('sink',)], 'sink')
    P.dcp('sp', mb_sb[:, :, :], mbias[:, :, :], (), [('mb',)], 'mb')
    P.dcp('sp', id32[:, :], ident[:, :], (), [('id32',)], 'id32')
    P.act(idb[:, :], id32[:, :], AF.Copy, reads=[('id32',)], writes=[('idb',)])
    SCALE = float(128 ** -0.5)
    it = 0
    for c in range(NBLK):
        mi = 0 if c == 0 else (2 if c == NBLK - 1 else 1)
        qs = slice(c * 128, (c + 1) * 128)
        for hq in range(16):
            kv = hq // 4
            i = it % NR; it += 1
            pS = psS[i % 2]; pSr = ('psS', i % 2)
            P.mm(pS[:, 0:384], qb[:, hq, qs], kb[:, kv, c * 128:c * 128 + 384], True, True,
                 reads=[('qb', hq), ('kb', kv)], writes=[pSr])
            P.stt(Sb[i][:, :], pS[:, 0:384], SCALE, mb_sb[:, mi, :], ALU.mult, ALU.add, reads=[pSr, ('mb',)], writes=[('Sb', i)])
            s_ = sm[i]; smr = ('sm', i)
            P.emit('dve', lambda e, i=i, s_=s_: e.reduce_max(out=s_[:, 0:1], in_=Sb[i][:, :], axis=mybir.AxisListType.X),
                   reads=[('Sb', i)], writes=[smr])
            P.tt('dve', s_[:, 1:2], s_[:, 0:1], sink_sb[:, hq:hq + 1], ALU.max, reads=[smr, ('sink',)], writes=[smr])
            P.ts('dve', s_[:, 2:3], s_[:, 1:2], -1.0, None, ALU.mult, None, reads=[smr], writes=[smr])
            P.emit('act', lambda e, i=i, s_=s_: e.activation(out=Pe[i][:, :], in_=Sb[i][:, :], func=AF.Exp, bias=s_[:, 2:3],
                                                            accum_out=s_[:, 3:4]), reads=[('Sb', i), smr], writes=[('Pe', i), smr])
            P.act(s_[:, 4:5], sink_sb[:, hq:hq + 1], AF.Exp, reads=[('sink',), smr], writes=[smr], bias=s_[:, 2:3])
            P.tt('dve', s_[:, 5:6], s_[:, 3:4], s_[:, 4:5], ALU.add, reads=[smr], writes=[smr])
            P.emit('dve', lambda e, s_=s_: e.reciprocal(out=s_[:, 6:7], in_=s_[:, 5:6]), reads=[smr], writes=[smr])
            P.ts('dve', Pn[i][:, :], Pe[i][:, :], s_[:, 6:7], None, ALU.mult, None, reads=[('Pe', i), smr], writes=[('Pn', i)])
            pT = psT[i % 2]; pTr = ('psT', i % 2)
            for kk in range(3):
                P.emit('pe', lambda e, i=i, kk=kk, pT=pT: e.transpose(out=pT[:, kk, :], in_=Pn[i][:, kk * 128:(kk + 1) * 128], identity=idb[:, :]),
                       reads=[('Pn', i), ('idb',)], writes=[pTr])
            P.act(PT[i][:, :, :], pT[:, :, :], AF.Copy, reads=[pTr], writes=[('PT', i)])
            pO = psO[i % 2]; pOr = ('psO', i % 2)
            for kk in range(3):
                P.mm(pO[:, 0:128], vb[:, c + kk, kv * 128:(kv + 1) * 128], PT[i][:, kk, :], kk == 0, kk == 2,
                     reads=[('vb',), ('PT', i)], writes=[pOr])
            P.emit('dve', lambda e, hq=hq, qs=qs, pO=pO: e.tensor_copy(out=osb[:, hq, qs], in_=pO[:, 0:128]), reads=[pOr], writes=[('osb', hq)])
    outs = []
    for hq in range(16):
        P.dcp('sp', oT[hq, :, :], osb[:, hq, :], [('osb', hq)], [('out', hq)], ('osb', hq))
        outs.append(('out', hq))
    return P.finish(outs)


def attn_masks():
    t = np.arange(128)[:, None]; s = np.arange(384)[None, :]
    inwin = np.abs(t - s + 128) <= 128
    NEG = np.float32(-30000.0)
    mid = np.where(inwin, 0.0, NEG).astype(np.float32)
    first = np.where(inwin & (s >= 128), 0.0, NEG).astype(np.float32)
    last = np.where(inwin & (s < 256), 0.0, NEG).astype(np.float32)
    return mid, first, last


def run_attn(q_rot, k_rot, v, sink):
    nc = build_attn()
    mid, first, last = attn_masks()
    kp = np.zeros((L + 256, 4, 128), np.float32); kp[128:L + 128] = k_rot
    vp = np.zeros((L + 256, 4, 128), np.float32); vp[128:L + 128] = v
    in_maps = []
    for c in range(NCORES):
        sl = slice(c * T, (c + 1) * T)
        hs = slice(c * T, (c + 1) * T + 256)
        mb = np.stack([first if c == 0 else mid, mid, last if c == NCORES - 1 else mid], 1)
        in_maps.append({"qT": np.ascontiguousarray(q_rot[sl].transpose(1, 2, 0)), "kT": np.ascontiguousarray(kp[hs].transpose(1, 2, 0)),
                        "vtok": np.ascontiguousarray(vp[hs].reshape(T + 256, 512)),
                        "sink": np.ascontiguousarray(np.broadcast_to(sink.reshape(1, 16), (128, 16))).astype(np.float32),
                        "mbias": np.ascontiguousarray(mb), "ident": np.eye(128, dtype=np.float32)})
    res = run_bass_kernel_spmd(nc, in_maps, core_ids=list(range(NCORES)))
    return np.concatenate([r["oT"].transpose(2, 0, 1).reshape(T, 2048) for r in res.results], 0)


def build_mid(stop=0):
    P = TokProg()
    xT = P.dram_in("xT", [D, T])
    yfT = P.dram_in("yfT", [1024, T])
    ybT = P.dram_in("ybT", [1024, T])
    ofT = P.dram_in("ofT", [1024, T])
    obT = P.dram_in("obT", [1024, T])
    gT = P.dram_in("gT", [1024, T])
    w_glu = P.wparam("w_glu", 1024, 1024)
    b_glu = P.dram_in("b_glu", [128, 8])
    w_out = P.wparam("w_out", D, D)
    w1 = P.wparam("w1", D, 4 * D)
    w2 = P.wparam("w2", 4 * D, D)
    lnp = P.dram_in("lnp", [128, 4, 16])
    w_in2 = P.wparam("w_in2", D, 3072)
    cosF = P.dram_in("cosF", [128, T])
    sinF = P.dram_in("sinF", [128, T])
    perm = P.dram_in("perm", [128, 128])
    x2T = P.dram_out("x2T", [D, T])
    h2T = P.dram_out("h2T", [3072, T])

    X = P.sb("X", [128, 16, TH], F32)
    Rb = P.sb("R", [128, 16, TH], F32)
    Bb = P.sb("Bb", [128, 16, TH], BF16)
    H = P.sb("H", [128, 64, TH], BF16)
    lnp_sb = P.sb("lnp_sb", [128, 4, 16], F32)
    bglu_sb = P.sb("bglu_sb", [128, 8], F32)
    cs = P.sb("cs", [128, 2, T], F32)
    perm32 = P.sb("perm32", [128, 128], F32)
    permb = P.sb("permb", [128, 128], BF16)
    hb16 = [P.sb(f"hb16_{i}", [128, TH], BF16) for i in range(2)]
    P.setup_common()
    xres = lambda kc: ('X', kc)
    rres = lambda kc: ('R', kc)
    bres = lambda kc: ('B', kc)
    hres = lambda kc: ('H', kc)
    v = lambda ap: ap.rearrange("(kc p) t -> p kc t", p=128)
    xTv, yfv, ybv, ofv, obv, gv, x2v, h2v = v(xT), v(yfT), v(ybT), v(ofT), v(obT), v(gT), v(x2T), v(h2T)
    YB = H

    def body():
        P.start()
        P.dcp('sp', lnp_sb[:, :, :], lnp[:, :, :], (), [('lnp',)], 'lnp')
        P.dcp('sp', bglu_sb[:, :], b_glu[:, :], (), [('bglu',)], 'bglu')
        P.dcp('sp', cs[:, 0, :], cosF[:, :], (), [('cs',)], 'cs0')
        P.dcp('sp', cs[:, 1, :], sinF[:, :], (), [('cs',)], 'cs1')
        P.dcp('sp', perm32[:, :], perm[:, :], (), [('perm32',)], 'perm32')
        P.act(permb[:, :], perm32[:, :], AF.Copy, reads=[('perm32',)], writes=[('permb',)])
        outs = []
        hbi = [0]
        for half in range(T // TH):
            t0 = half * TH
            ts_ = slice(t0, t0 + TH)
            for n in range(8):
                P.dcp('sp', Rb[:, n, :], yfv[:, n, ts_], (), [rres(n)], ('R', n))
                P.dcp('sp', Rb[:, 8 + n, :], ybv[:, n, ts_], (), [rres(8 + n)], ('R', 8 + n))
            for n in range(8):
                xs = X[:, n, :]
                t1 = P.tmp[0]; t2 = P.tmp[1]
                P.tt('dve', xs, Rb[:, n, :], Rb[:, 8 + n, :], ALU.add, reads=[rres(n), rres(8 + n)], writes=[xres(n)])
                P.tt('dve', t1[:, :], xs, xs, ALU.mult, reads=[xres(n)], writes=[('tmp', 0)])
                P.ts('dve', t1[:, :], t1[:, :], 0.044715, 1.0, ALU.mult, ALU.add, reads=[('tmp', 0)], writes=[('tmp', 0)])
                P.tt('dve', t1[:, :], t1[:, :], xs, ALU.mult, reads=[('tmp', 0), xres(n)], writes=[('tmp', 0)])
                P.act(t2[:, :], t1[:, :], AF.Sigmoid, reads=[('tmp', 0)], writes=[('tmp', 1)], scale=1.5957691216057308)
                P.tt('dve', xs, xs, t2[:, :], ALU.mult, reads=[xres(n), ('tmp', 1)], writes=[xres(n)])
                P.act(YB[:, n, :], xs, AF.Copy, reads=[xres(n)], writes=[hres(n)])

            def evg(n, ps, pr):
                t2 = P.tmp[1]
                P.act(t2[:, :], ps[:, :], AF.Sigmoid, reads=[pr, ('bglu',)], writes=[('tmp', 1)], bias=bglu_sb[:, n:n + 1])
                P.tt('dve', Bb[:, n, :], X[:, n, :], t2[:, :], ALU.mult, reads=[xres(n), ('tmp', 1)], writes=[bres(n)])
            P.lin(w_glu, 1024, 1024, lambda kc: YB[:, kc, :], hres, evg)
            if stop == 1:
                for kc in range(8):
                    P.dcp('sp', x2v[:, kc, ts_], X[:, kc, :], [xres(kc), bres(kc)], [('out', half, kc)], ('X', kc))
                    outs.append(('out', half, kc))
                return outs
            for n in range(8):
                P.dcp('sp', Rb[:, n, :], ofv[:, n, ts_], (), [rres(n)], ('R', n))
                P.dcp('sp', Rb[:, 8 + n, :], obv[:, n, ts_], (), [rres(8 + n)], ('R', 8 + n))
            for n in range(8):
                P.tt('dve', X[:, 8 + n, :], Rb[:, n, :], Rb[:, 8 + n, :], ALU.add, reads=[rres(n), rres(8 + n)], writes=[xres(8 + n)])
            for n in range(8):
                P.dcp('sp', Rb[:, n, :], gv[:, n, ts_], (), [rres(n)], ('R', n))
            for hh in range(4):
                P.stats([(X[:, 8 + 2 * hh + k, :], xres(8 + 2 * hh + k)) for k in range(2)], 1e-6)
                for k in range(2):
                    n = 2 * hh + k
                    t1 = P.tmp[0]; t2 = P.tmp[1]
                    P.act(t2[:, :], Rb[:, n, :], AF.Silu, reads=[rres(n)], writes=[('tmp', 1)])
                    P.tt('dve', t1[:, :], X[:, 8 + n, :], P.mean[:, :], ALU.subtract, reads=[xres(8 + n), ('mean',)], writes=[('tmp', 0)])
                    P.tt('dve', t1[:, :], t1[:, :], P.rstd[:, :], ALU.mult, reads=[('tmp', 0), ('rstd',)], writes=[('tmp', 0)])
                    P.tt('dve', Bb[:, 8 + n, :], t1[:, :], t2[:, :], ALU.mult, reads=[('tmp', 0), ('tmp', 1)], writes=[bres(8 + n)])
            if stop == 2:
                for kc in range(16):
                    P.dcp('sp', x2v[:, kc, ts_], X[:, kc, :], [xres(kc), bres(kc)], [('out', half, kc)], ('X', kc))
                    outs.append(('out', half, kc))
                return outs
            for kc in range(16):
                P.dcp('sp', X[:, kc, :], xTv[:, kc, ts_], (), [xres(kc)], ('X', kc))

            def ev1(n, ps, pr):
                P.stt(Rb[:, n, :], X[:, n, :], ALPHA, ps[:, :], ALU.mult, ALU.add, reads=[xres(n), pr], writes=[rres(n)])
            P.lin(w_out, D, D, lambda kc: Bb[:, kc, :], bres, ev1)
            P.layernorm(Rb, rres, X, xres, Bb, bres, lnp_sb[:, 0, :], lnp_sb[:, 1, :])

            def ev2(n, ps, pr):
                P.act(H[:, n, :], ps[:, :], AF.Relu, reads=[pr], writes=[hres(n)])
                P.tt('dve', H[:, n, :], H[:, n, :], H[:, n, :], ALU.mult, reads=[hres(n)], writes=[hres(n)])
            P.lin(w1, D, 4 * D, lambda kc: Bb[:, kc, :], bres, ev2)

            def ev3(n, ps, pr):
                P.stt(X[:, n, :], X[:, n, :], ALPHA, ps[:, :], ALU.mult, ALU.add, reads=[xres(n), pr], writes=[xres(n)])
            P.lin(w2, 4 * D, D, lambda kc: H[:, kc, :], hres, ev3)
            P.layernorm(X, xres, Rb, rres, Bb, bres, lnp_sb[:, 2, :], lnp_sb[:, 3, :])
            for kc in range(16):
                P.dcp('sp', x2v[:, kc, ts_], Rb[:, kc, :], [rres(kc)], [('out', half, kc)], ('R', kc))
                outs.append(('out', half, kc))
            if stop == 3:
                continue
            cosv = cs[:, 0, ts_]; sinv = cs[:, 1, ts_]

            def ev4(n, ps, pr):
                ores = ('X', n % 16)
                ob = X[:, n % 16, :]
                if n < 20:
                    i = hbi[0] % 2; hbi[0] += 1
                    P.act(hb16[i][:, :], ps[:, :], AF.Copy, reads=[pr], writes=[('hb16', i), pr])
                    pp = P.pss[6 + i]; ppr = ('ps', 6 + i)
                    P.mm(pp[:, :], permb[:, :], hb16[i][:, :], True, True, reads=[('permb',), ('hb16', i)], writes=[ppr])
                    t1 = P.tmp[0]; t2 = P.tmp[1]
                    P.tt('dve', t1[:, :], ps[:, :], cosv, ALU.mult, reads=[pr, ('cs',)], writes=[('tmp', 0)])
                    P.tt('dve', t2[:, :], pp[:, :], sinv, ALU.mult, reads=[ppr, ('cs',)], writes=[('tmp', 1)])
                    P.tt('dve', ob, t1[:, :], t2[:, :], ALU.add, reads=[('tmp', 0), ('tmp', 1)], writes=[ores])
                else:
                    P.act(ob, ps[:, :], AF.Copy, reads=[pr], writes=[ores])
                P.dcp('sp', h2v[:, n, ts_], ob, [ores], [('outh', half, n)], ('X', n % 16))
                outs.append(('outh', half, n))
            P.lin(w_in2, D, 3072, lambda kc: Bb[:, kc, :], bres, ev4)
        return outs

    P.plan = True
    body()
    P.plan = False
    outs = body()
    return P.finish(outs)


def odd_rope_tables(pos):
    c, s = rope_tables(pos, 16, 500000.0)
    n = len(pos)
    cosF = np.ones((128, n), np.float32); sinF = np.zeros((128, n), np.float32)
    cosF[0:16] = c; cosF[16:32] = c
    sinF[0:16] = -s; sinF[16:32] = s
    perm = np.zeros((128, 128), np.float32)
    for m in range(16):
        perm[m + 16, m] = 1.0
        perm[m, m + 16] = 1.0
    return cosF, sinF, perm


def tm(a, c):
    return np.ascontiguousarray(a[c * T:(c + 1) * T].T)


def run_mid(x_tok, yf, yb, of, ob, gate, inp, W, stop=0):
    nc = build_mid(stop)
    in_maps = []
    lnp = lnp_layout(inp['ln_g'], inp['ln_b'], 0)
    bglu = np.ascontiguousarray(inp['s5_b_glu'][0].reshape(8, 128).T)
    for c in range(NCORES):
        cosF, sinF, perm = odd_rope_tables(np.arange(c * T, (c + 1) * T))
        in_maps.append({"xT": tm(x_tok, c), "yfT": tm(yf, c), "ybT": tm(yb, c), "ofT": tm(of, c), "obT": tm(ob, c), "gT": tm(gate, c),
                        "w_glu": W[('s5_w_glu', 0)], "b_glu": bglu, "w_out": W[('even_w_out', 0)], "w1": W[('mlp_w1', 0)],
                        "w2": W[('mlp_w2', 0)], "lnp": lnp, "w_in2": W[('odd_w_in', 0)], "cosF": cosF, "sinF": sinF, "perm": perm})
    res = run_bass_kernel_spmd(nc, in_maps, core_ids=list(range(NCORES)))
    x2 = np.concatenate([r["x2T"].T for r in res.results], 0)
    h2 = np.concatenate([r["h2T"].T for r in res.results], 0)
    return x2, h2


def build_inproj_odd(mode=0):
    P = TokProg()
    xT = P.dram_in("xT", [D, T])
    w_in2 = P.wparam("w_in2", D, 3072)
    cosF = P.dram_in("cosF", [128, T])
    sinF = P.dram_in("sinF", [128, T])
    perm = P.dram_in("perm", [128, 128])
    h2T = P.dram_out("h2T", [3072, T])
    Rb = P.sb("R", [128, 16, TH], F32)
    Bb = P.sb("Bb", [128, 16, TH], BF16)
    O = P.sb("O", [128, 24, TH], F32)
    cs = P.sb("cs", [128, 2, T], F32)
    perm32 = P.sb("perm32", [128, 128], F32)
    permb = P.sb("permb", [128, 128], BF16)
    hb16 = [P.sb(f"hb16_{i}", [128, TH], BF16) for i in range(2)]
    P.setup_common()
    rres = lambda kc: ('R', kc)
    bres = lambda kc: ('B', kc)
    xTv = xT.rearrange("(kc p) t -> p kc t", p=128)
    h2v = h2T.rearrange("(kc p) t -> p kc t", p=128)

    def body():
        P.start()
        P.dcp('sp', cs[:, 0, :], cosF[:, :], (), [('cs',)], 'cs0')
        P.dcp('sp', cs[:, 1, :], sinF[:, :], (), [('cs',)], 'cs1')
        P.dcp('sp', perm32[:, :], perm[:, :], (), [('perm32',)], 'perm32')
        P.act(permb[:, :], perm32[:, :], AF.Copy, reads=[('perm32',)], writes=[('permb',)])
        outs = []
        hbi = [0]
        for half in range(T // TH):
            t0 = half * TH
            ts_ = slice(t0, t0 + TH)
            for kc in range(16):
                P.dcp('sp', Rb[:, kc, :], xTv[:, kc, ts_], (), [rres(kc)], ('R', kc))
                P.act(Bb[:, kc, :], Rb[:, kc, :], AF.Copy, reads=[rres(kc)], writes=[bres(kc)])
            cosv = cs[:, 0, ts_]; sinv = cs[:, 1, ts_]

            def ev4(n, ps, pr):
                ores = ('O', n)
                ob = O[:, n, :]
                if n < 20 and mode == 0:
                    i = hbi[0] % 2; hbi[0] += 1
                    P.act(hb16[i][:, :], ps[:, :], AF.Copy, reads=[pr], writes=[('hb16', i), pr])
                    pp = P.pss[6 + i]; ppr = ('ps', 6 + i)
                    P.mm(pp[:, :], permb[:, :], hb16[i][:, :], True, True, reads=[('permb',), ('hb16', i)], writes=[ppr])
                    t1 = P.tmp[0]; t2 = P.tmp[1]
                    P.tt('dve', t1[:, :], ps[:, :], cosv, ALU.mult, reads=[pr, ('cs',)], writes=[('tmp', 0)])
                    P.tt('dve', t2[:, :], pp[:, :], sinv, ALU.mult, reads=[ppr, ('cs',)], writes=[('tmp', 1)])
                    P.tt('dve', ob, t1[:, :], t2[:, :], ALU.add, reads=[('tmp', 0), ('tmp', 1)], writes=[ores])
                else:
                    P.act(ob, ps[:, :], AF.Copy, reads=[pr], writes=[ores])
                P.dcp('sp', h2v[:, n, ts_], ob, [ores], [('outh', half, n)], ('O', n))
                outs.append(('outh', half, n))
            P.lin(w_in2, D, 3072, lambda kc: Bb[:, kc, :], bres, ev4)
        return outs

    P.plan = True
    body()
    P.plan = False
    outs = body()
    return P.finish(outs)


def run_inproj_odd(x_tok, w_in2, mode=0):
    nc = build_inproj_odd(mode)
    in_maps = []
    for c in range(NCORES):
        cosF, sinF, perm = odd_rope_tables(np.arange(c * T, (c + 1) * T))
        in_maps.append({"xT": tm(x_tok, c), "w_in2": w_in2, "cosF": cosF, "sinF": sinF, "perm": perm})
    res = run_bass_kernel_spmd(nc, in_maps, core_ids=list(range(NCORES)))
    return np.concatenate([r["h2T"].T for r in res.results], 0)


WNAMES = [('even_w_in', 0, D, 5120), ('even_w_out', 0, D, D), ('s5_w_glu', 0, 1024, 1024), ('mlp_w1', 0, D, 4 * D), ('mlp_w2', 0, 4 * D, D),
          ('odd_w_in', 0, D, 3072), ('odd_w_out', 0, D, D), ('mlp_w1', 1, D, 4 * D), ('mlp_w2', 1, 4 * D, D)]
CAST_ROWS = sum(k * n for (_, _, k, n) in WNAMES) // 2048 // NCORES


def build_cast():
    P = Prog()
    w = P.dram_in("w", [CAST_ROWS, 2048])
    o = P.dram_out("o", [CAST_ROWS, 2048], BF16)
    outs = []
    RB = 64
    for i in range(CAST_ROWS // RB):
        P.dma('pool', lambda e, i=i: e.dma_start(out=o[i * RB:(i + 1) * RB, :], in_=w[i * RB:(i + 1) * RB, :]),
              writes=[('o', i)], key=('o', i % 8))
        outs.append(('o', i))
    return P.finish(outs)


def cast_weights(inp):
    flat = np.concatenate([np.asarray(inp[nm][li], dtype=np.float32).reshape(-1) for (nm, li, _, _) in WNAMES]).reshape(NCORES, CAST_ROWS, 2048)
    nc = build_cast()
    res = run_bass_kernel_spmd(nc, [{"w": flat[c]} for c in range(NCORES)], core_ids=list(range(NCORES)))
    ob = np.concatenate([r["o"].reshape(-1) for r in res.results])
    out = {}
    off = 0
    for (nm, li, k, n) in WNAMES:
        out[(nm, li)] = to_panels(ob[off:off + k * n].reshape(k, n), k, n)
        off += k * n
    return out


def kernel(**inputs):
    inp = {k: np.asarray(v) for k, v in inputs.items()}
    x = np.ascontiguousarray(inp['x'][0], dtype=np.float32)
    W = cast_weights(inp)
    h = run_inproj_even(x, W[('even_w_in', 0)])
    u = np.ascontiguousarray(h[:, :1024])
    q = h[:, 1024:2048].reshape(L, 4, 256)
    k = h[:, 2048:3072].reshape(L, 4, 256)
    v = h[:, 3072:4096].reshape(L, 4, 256)
    gate = np.ascontiguousarray(h[:, 4096:])
    yf, yb = run_s5(u, inp)
    of, ob = run_ret(q, k, v, inp['ret_log_decay'][0])
    x2, h2 = run_mid(x, yf, yb, of.reshape(L, 1024), ob.reshape(L, 1024), gate, inp, W)
    q2 = h2[:, :2048].reshape(L, 16, 128)
    k2 = h2[:, 2048:2560].reshape(L, 4, 128)
    v2 = h2[:, 2560:].reshape(L, 4, 128)
    o = run_attn(q2, k2, v2, inp['attn_sink'][0])
    y = run_tail(x2, o, W[('odd_w_out', 0)], W[('mlp_w1', 1)], W[('mlp_w2', 1)], lnp_layout(inp['ln_g'], inp['ln_b'], 1))
    return y.reshape(1, L, D).astype(np.float32)
```

```python
import numpy as np
from contextlib import ExitStack
import concourse.bass as bass
import concourse.mybir as mybir
from concourse.bass_utils import run_bass_kernel_spmd

F32 = mybir.dt.float32
BF16 = mybir.dt.bfloat16
ALU = mybir.AluOpType
AF = mybir.ActivationFunctionType
ENGS = ['pe', 'dve', 'act', 'pool', 'sp']

NCORES = 8
D = 2048
L = 8192
T = L // NCORES
TH = 512
ALPHA = float(4 ** 0.25)
LN_EPS = 1e-5


class Prog:
    def __init__(self):
        self.nc = bass.Bass("TRN2", target_bir_lowering=False)
        self.q = {e: [] for e in ENGS}
        self.cnt = {}
        self.seen = {e: {} for e in ENGS}
        self.lastw = {}
        self.readers = {}
        self.semkeys = []
        self.es = ExitStack()
        self.plan = False
        self.panels = []
        self.pi = 0
        self.issued = 0
        self.psi = 0

    def sb(self, name, shape, dt):
        return self.es.enter_context(self.nc.sbuf_tensor(name, shape, dt))

    def ps(self, name, shape, dt=F32):
        return self.es.enter_context(self.nc.psum_tensor(name, shape, dt))

    def dram_in(self, name, shape, dt=F32):
        return self.nc.dram_tensor(name, list(shape), dt, kind="ExternalInput").ap()

    def dram_out(self, name, shape, dt=F32):
        return self.nc.dram_tensor(name, list(shape), dt, kind="ExternalOutput").ap()

    def _deps(self, eng, reads, writes):
        deps = []
        for r in reads:
            t = self.lastw.get(r)
            if t: deps.append(t)
            if isinstance(r, tuple) and isinstance(r[0], str) and r[0].startswith('ps'):
                deps.extend(x for x in self.readers.get(r, []) if x[2] != eng)
        for w in writes:
            t = self.lastw.get(w)
            if t: deps.append(t)
            deps.extend(self.readers.get(w, []))
        waits = []
        for (sk, val, deng) in deps:
            if deng == eng and eng == 'pe':
                continue
            if self.seen[eng].get(sk, 0) >= val:
                continue
            self.seen[eng][sk] = val
            waits.append((sk, val))
        return waits

    def _bump(self, sk, inc):
        if sk not in self.cnt:
            self.cnt[sk] = 0
            self.semkeys.append(sk)
        self.cnt[sk] += inc
        return self.cnt[sk]

    def _record(self, tok, reads, writes):
        for w in writes:
            self.lastw[w] = tok
            self.readers[w] = []
        for r in reads:
            self.readers.setdefault(r, []).append(tok)

    def emit(self, eng, fn, reads=(), writes=()):
        if self.plan: return
        waits = self._deps(eng, reads, writes)
        sk = 'e_' + eng
        v = self._bump(sk, 1)
        self.q[eng].append((waits, fn, sk, 1))
        self._record((sk, v, eng), reads, writes)

    def dma(self, eng, fn, reads=(), writes=(), key=None):
        if self.plan: return
        waits = self._deps(eng, reads, writes)
        sk = 'd_' + str(key)
        v = self._bump(sk, 16)
        self.q[eng].append((waits, fn, sk, 16))
        self._record((sk, v, 'dma'), reads, writes)

    def barrier(self):
        if self.plan: return
        toks = []
        for sk in self.semkeys:
            toks.append((sk, self.cnt[sk]))
        for e in ENGS:
            waits = []
            for (sk, v) in toks:
                if self.seen[e].get(sk, 0) >= v: continue
                self.seen[e][sk] = v
                waits.append((sk, v))
            if waits:
                self.q[e].append((waits, None, None, 0))

    def tt(self, eng, out, in0, in1, op, reads, writes):
        self.emit(eng, lambda e: e.tensor_tensor(out=out, in0=in0, in1=in1, op=op), reads, writes)

    def ts(self, eng, out, in0, s1, s2, op0, op1, reads, writes):
        if op1 is None:
            self.emit(eng, lambda e: e.tensor_scalar(out=out, in0=in0, scalar1=s1, scalar2=None, op0=op0), reads, writes)
        else:
            self.emit(eng, lambda e: e.tensor_scalar(out=out, in0=in0, scalar1=s1, scalar2=s2, op0=op0, op1=op1), reads, writes)

    def stt(self, out, in0, scalar, in1, op0, op1, reads, writes):
        self.emit('dve', lambda e: e.scalar_tensor_tensor(out=out, in0=in0, scalar=scalar, in1=in1, op0=op0, op1=op1), reads, writes)

    def act(self, out, in_, func, reads, writes, bias=None, scale=None):
        kw = {}
        if bias is not None: kw['bias'] = bias
        if scale is not None: kw['scale'] = scale
        self.emit('act', lambda e: e.activation(out=out, in_=in_, func=func, **kw), reads, writes)

    def mm(self, out, lhsT, rhs, start, stop, reads, writes):
        self.emit('pe', lambda e: e.matmul(out, lhsT=lhsT, rhs=rhs, start=start, stop=stop), reads, writes)

    def finish(self, final_res):
        waits = self._deps('sp', final_res, ())
        self.q['sp'].append((waits, None, None, 0))
        nc = self.nc
        sems = {}
        for i, sk in enumerate(self.semkeys):
            sems[sk] = self.es.enter_context(nc.semaphore(f"s{i}"))
        with nc.Block() as block:
            def run(e, engobj):
                for (waits, fn, sk, inc) in self.q[e]:
                    for (wk, wv) in waits:
                        engobj.wait_ge(sems[wk], wv)
                    if fn is not None:
                        fn(engobj).then_inc(sems[sk], inc)

            @block.tensor
            def _(e): run('pe', e)

            @block.vector
            def _(e): run('dve', e)

            @block.scalar
            def _(e): run('act', e)

            @block.gpsimd
            def _(e): run('pool', e)

            @block.sync
            def _(e): run('sp', e)
        self.es.close()
        return nc


def panel_dims(K, N):
    KC = K // 128
    CW = min(N, 8192 // KC, 512)
    return KC, CW, N // CW


def to_panels(wb, K, N):
    KC, CW, NP = panel_dims(K, N)
    return np.ascontiguousarray(wb.reshape(KC, 128, NP, CW).transpose(2, 1, 0, 3).reshape(NP, 128, KC * CW))


class TokProg(Prog):
    NSLOT = 2
    WELEMS = 8192

    def dcp(self, eng, out, in_, reads, writes, key):
        if eng == 'sp':
            eng = 'pool'
        self.dma(eng, lambda e: e.dma_start(out=out, in_=in_), reads=reads, writes=writes, key=key)

    def wparam(self, name, K, N):
        KC, CW, NP = panel_dims(K, N)
        return self.dram_in(name, [NP, 128, KC * CW], BF16)

    def setup_common(self):
        self.wbuf = [self.sb(f"wbuf{i}", [128, self.WELEMS], BF16) for i in range(self.NSLOT)]
        self.pss = [self.ps(f"ps{i}", [128, 512]) for i in range(8)]
        self.ones32 = self.sb("ones32", [128, 128], F32)
        self.sq = [self.sb(f"sq{i}", [128, TH], F32) for i in range(2)]
        self.tmp = [self.sb(f"tmp{i}", [128, TH], F32) for i in range(2)]
        self.mean = self.sb("mean", [128, TH], F32)
        self.m2 = self.sb("m2", [128, TH], F32)
        self.rstd = self.sb("rstd", [128, TH], F32)
        self.sqi = 0
        self.tmpi = 0

    def start(self):
        self.psi = 0
        self.emit('dve', lambda e: e.memset(self.ones32[:, :], 1.0), writes=[('ones32',)])

    def next_ps(self):
        i = self.psi % 4
        self.psi += 1
        return self.pss[i], ('ps', i)

    def issue_panel(self, i):
        (w_ap, KC, c0, CW) = self.panels[i]
        slot = i % self.NSLOT
        wv = self.wbuf[slot][:, 0:KC * CW]
        src = w_ap[c0 // CW, :, :]
        self.dma('sp', lambda e: e.dma_start(out=wv, in_=src), writes=[('wb', slot)], key=('wb', slot))

    def next_panel(self, w_ap, KC, c0, CW):
        if self.plan:
            self.panels.append((w_ap, KC, c0, CW))
            return None
        i = self.pi
        self.pi += 1
        assert self.panels[i][1:] == (KC, c0, CW)
        while self.issued < min(len(self.panels), i + self.NSLOT):
            self.issue_panel(self.issued)
            self.issued += 1
        slot = i % self.NSLOT
        return self.wbuf[slot][:, 0:KC * CW].rearrange("p (kc n) -> p kc n", kc=KC), slot

    def lin(self, w_ap, K, N, rhs_fn, rhs_res, evac_fn, tw=TH):
        KC, CW, _ = panel_dims(K, N)
        for c0 in range(0, N, CW):
            r = self.next_panel(w_ap, KC, c0, CW)
            if self.plan: continue
            wv, slot = r
            for nn in range(CW // 128):
                n = c0 // 128 + nn
                ps, pr = self.next_ps()
                for kc in range(KC):
                    self.mm(ps[:, 0:tw], wv[:, kc, nn * 128:(nn + 1) * 128], rhs_fn(kc), kc == 0, kc == KC - 1,
                            reads=[('wb', slot), rhs_res(kc)], writes=[pr])
                evac_fn(n, ps, pr)

    def stats(self, chunks, eps):
        nck = len(chunks)
        ps1, pr1 = self.pss[4], ('ps', 4)
        ps2, pr2 = self.pss[5], ('ps', 5)
        for kc, (ap, rs) in enumerate(chunks):
            self.mm(ps1[:, :], self.ones32[:, :], ap, kc == 0, kc == nck - 1, reads=[('ones32',), rs], writes=[pr1])
        for kc, (ap, rs) in enumerate(chunks):
            sq = self.sq[self.sqi % 2]; sqr = ('sq', self.sqi % 2); self.sqi += 1
            self.act(sq[:, :], ap, AF.Square, reads=[rs], writes=[sqr])
            self.mm(ps2[:, :], self.ones32[:, :], sq[:, :], kc == 0, kc == nck - 1, reads=[('ones32',), sqr], writes=[pr2])
        inv = 1.0 / (nck * 128)
        self.act(self.mean[:, :], ps1[:, :], AF.Copy, reads=[pr1], writes=[('mean',)], scale=inv)
        self.tt('dve', self.m2[:, :], self.mean[:, :], self.mean[:, :], ALU.mult, reads=[('mean',)], writes=[('m2',)])
        self.stt(self.m2[:, :], ps2[:, :], inv, self.m2[:, :], ALU.mult, ALU.subtract, reads=[pr2, ('m2',)], writes=[('m2',)])
        self.ts('dve', self.m2[:, :], self.m2[:, :], eps, None, ALU.add, None, reads=[('m2',)], writes=[('m2',)])
        self.act(self.m2[:, :], self.m2[:, :], AF.Sqrt, reads=[('m2',)], writes=[('m2',)])
        self.emit('dve', lambda e: e.reciprocal(out=self.rstd[:, :], in_=self.m2[:, :]), reads=[('m2',)], writes=[('rstd',)])

    def layernorm(self, S, sres, Dst, dres, Bd, bres, g_ap, b_ap, nck=16):
        self.stats([(S[:, kc, :], sres(kc)) for kc in range(nck)], LN_EPS)
        for kc in range(nck):
            tmp = self.tmp[self.tmpi % 2]; tr = ('tmp', self.tmpi % 2); self.tmpi += 1
            self.tt('dve', tmp[:, :], S[:, kc, :], self.mean[:, :], ALU.subtract, reads=[sres(kc), ('mean',)], writes=[tr])
            self.stt(tmp[:, :], tmp[:, :], g_ap[:, kc:kc + 1], self.rstd[:, :], ALU.mult, ALU.mult,
                     reads=[tr, ('rstd',), ('lnp',)], writes=[tr])
            self.act(Dst[:, kc, :], tmp[:, :], AF.Identity, reads=[tr, ('lnp',)], writes=[dres(kc)], bias=b_ap[:, kc:kc + 1])
            if Bd is not None:
                self.act(Bd[:, kc, :], Dst[:, kc, :], AF.Copy, reads=[dres(kc)], writes=[bres(kc)])


def build_tail(layer_has_inproj=False):
    P = TokProg()
    xT = P.dram_in("xT", [D, T])
    aT = P.dram_in("aT", [D, T])
    w_out = P.wparam("w_out", D, D)
    w1 = P.wparam("w1", D, 4 * D)
    w2 = P.wparam("w2", 4 * D, D)
    lnp = P.dram_in("lnp", [128, 4, 16])
    yT = P.dram_out("yT", [D, T])

    X = P.sb("X", [128, 16, TH], F32)
    Rb = P.sb("R", [128, 16, TH], F32)
    Bb = P.sb("Bb", [128, 16, TH], BF16)
    H = P.sb("H", [128, 64, TH], BF16)
    lnp_sb = P.sb("lnp_sb", [128, 4, 16], F32)
    P.setup_common()
    xres = lambda kc: ('X', kc)
    rres = lambda kc: ('R', kc)
    bres = lambda kc: ('B', kc)
    hres = lambda kc: ('H', kc)
    xTv = xT.rearrange("(kc p) t -> p kc t", p=128)
    aTv = aT.rearrange("(kc p) t -> p kc t", p=128)
    yTv = yT.rearrange("(kc p) t -> p kc t", p=128)

    def body():
        P.start()
        P.dcp('sp', lnp_sb[:, :, :], lnp[:, :, :], (), [('lnp',)], 'lnp')
        outs = []
        for half in range(T // TH):
            t0 = half * TH
            for kc in range(16):
                P.dcp('sp', X[:, kc, :], xTv[:, kc, t0:t0 + TH], (), [xres(kc)], ('X', kc))
                P.dcp('sp', Rb[:, kc, :], aTv[:, kc, t0:t0 + TH], (), [rres(kc)], ('R', kc))
                P.act(Bb[:, kc, :], Rb[:, kc, :], AF.Copy, reads=[rres(kc)], writes=[bres(kc)])

            def ev1(n, ps, pr):
                P.stt(Rb[:, n, :], X[:, n, :], ALPHA, ps[:, :], ALU.mult, ALU.add, reads=[xres(n), pr], writes=[rres(n)])
            P.lin(w_out, D, D, lambda kc: Bb[:, kc, :], bres, ev1)
            P.layernorm(Rb, rres, X, xres, Bb, bres, lnp_sb[:, 0, :], lnp_sb[:, 1, :])

            def ev2(n, ps, pr):
                P.act(H[:, n, :], ps[:, :], AF.Relu, reads=[pr], writes=[hres(n)])
                P.tt('dve', H[:, n, :], H[:, n, :], H[:, n, :], ALU.mult, reads=[hres(n)], writes=[hres(n)])
            P.lin(w1, D, 4 * D, lambda kc: Bb[:, kc, :], bres, ev2)

            def ev3(n, ps, pr):
                P.stt(X[:, n, :], X[:, n, :], ALPHA, ps[:, :], ALU.mult, ALU.add, reads=[xres(n), pr], writes=[xres(n)])
            P.lin(w2, 4 * D, D, lambda kc: H[:, kc, :], hres, ev3)
            P.layernorm(X, xres, Rb, rres, None, None, lnp_sb[:, 2, :], lnp_sb[:, 3, :])
            for kc in range(16):
                P.dcp('sp', yTv[:, kc, t0:t0 + TH], Rb[:, kc, :], [rres(kc)], [('out', half, kc)], ('R', kc))
                outs.append(('out', half, kc))
        return outs

    P.plan = True
    body()
    P.plan = False
    outs = body()
    return P.finish(outs)


def lnp_layout(ln_g, ln_b, layer):
    a = np.stack([ln_g[layer, 0], ln_b[layer, 0], ln_g[layer, 1], ln_b[layer, 1]], 0)
    return np.ascontiguousarray(a.reshape(4, 16, 128).transpose(2, 0, 1))


def run_tail(x_tok, a_tok, w_out, w1, w2, lnp):
    nc = build_tail()
    in_maps = []
    for c in range(NCORES):
        sl = slice(c * T, (c + 1) * T)
        in_maps.append({"xT": np.ascontiguousarray(x_tok[sl].T), "aT": np.ascontiguousarray(a_tok[sl].T),
                        "w_out": w_out, "w1": w1, "w2": w2, "lnp": lnp})
    res = run_bass_kernel_spmd(nc, in_maps, core_ids=list(range(NCORES)))
    return np.concatenate([r["yT"].T for r in res.results], 0)


def rope_tables(pos, half, theta):
    inv_freq = (np.float32(1.0) / np.power(np.float32(theta), (np.arange(half, dtype=np.float32) / np.float32(half)))).astype(np.float32)
    ang = (pos.astype(np.float32)[None, :] * inv_freq[:, None]).astype(np.float32)
    return np.cos(ang.astype(np.float64)).astype(np.float32), np.sin(ang.astype(np.float64)).astype(np.float32)


def build_inproj_even():
    P = TokProg()
    NOUT = 5120
    xT = P.dram_in("xT", [D, T])
    w_in = P.wparam("w_in", D, NOUT)
    cosT = P.dram_in("cosT", [128, T])
    sinT = P.dram_in("sinT", [128, T])
    hT = P.dram_out("hT", [NOUT, T])
    Rb = P.sb("R", [128, 16, TH], F32)
    Bb = P.sb("Bb", [128, 16, TH], BF16)
    O = P.sb("O", [128, 40, TH], F32)
    cs = P.sb("cs", [128, 2, T], F32)
    P.setup_common()
    rres = lambda kc: ('R', kc)
    bres = lambda kc: ('B', kc)
    xTv = xT.rearrange("(kc p) t -> p kc t", p=128)
    hTv = hT.rearrange("(kc p) t -> p kc t", p=128)

    def body():
        P.start()
        P.dcp('sp', cs[:, 0, :], cosT[:, :], (), [('cs',)], 'cs0')
        P.dcp('sp', cs[:, 1, :], sinT[:, :], (), [('cs',)], 'cs1')
        outs = []
        for half in range(T // TH):
            t0 = half * TH
            for kc in range(16):
                P.dcp('sp', Rb[:, kc, :], xTv[:, kc, t0:t0 + TH], (), [rres(kc)], ('R', kc))
                P.act(Bb[:, kc, :], Rb[:, kc, :], AF.Copy, reads=[rres(kc)], writes=[bres(kc)])
            pend = {}
            cosv = cs[:, 0, t0:t0 + TH]
            sinv = cs[:, 1, t0:t0 + TH]

            def ev(n, ps, pr):
                ores = ('O', n)
                if 8 <= n < 24:
                    if n % 2 == 0:
                        pend['a'] = (ps, pr)
                        return
                    pa, pra = pend['a']
                    t1 = P.tmp[0]; t2 = P.tmp[1]
                    P.tt('dve', t1[:, :], pa[:, :], cosv, ALU.mult, reads=[pra, ('cs',)], writes=[('tmp', 0)])
                    P.tt('dve', t2[:, :], ps[:, :], sinv, ALU.mult, reads=[pr, ('cs',)], writes=[('tmp', 1)])
                    P.tt('dve', O[:, n - 1, :], t1[:, :], t2[:, :], ALU.subtract, reads=[('tmp', 0), ('tmp', 1)], writes=[('O', n - 1)])
                    P.tt('dve', t1[:, :], ps[:, :], cosv, ALU.mult, reads=[pr, ('cs',)], writes=[('tmp', 0)])
                    P.tt('dve', t2[:, :], pa[:, :], sinv, ALU.mult, reads=[pra, ('cs',)], writes=[('tmp', 1)])
                    P.tt('dve', O[:, n, :], t1[:, :], t2[:, :], ALU.add, reads=[('tmp', 0), ('tmp', 1)], writes=[('O', n)])
                    for m in (n - 1, n):
                        P.dcp('sp', hTv[:, m, t0:t0 + TH], O[:, m, :], [('O', m)], [('out', half, m)], ('O', m))
                        outs.append(('out', half, m))
                else:
                    P.act(O[:, n, :], ps[:, :], AF.Copy, reads=[pr], writes=[ores])
                    P.dcp('sp', hTv[:, n, t0:t0 + TH], O[:, n, :], [ores], [('out', half, n)], ('O', n))
                    outs.append(('out', half, n))
            P.lin(w_in, D, NOUT, lambda kc: Bb[:, kc, :], bres, ev)
        return outs

    P.plan = True
    body()
    P.plan = False
    outs = body()
    return P.finish(outs)


def run_inproj_even(x_tok, w_in):
    nc = build_inproj_even()
    in_maps = []
    for c in range(NCORES):
        sl = slice(c * T, (c + 1) * T)
        cosT, sinT = rope_tables(np.arange(c * T, (c + 1) * T), 128, 10000.0)
        in_maps.append({"xT": np.ascontiguousarray(x_tok[sl].T), "w_in": w_in, "cosT": cosT, "sinT": sinT})
    res = run_bass_kernel_spmd(nc, in_maps, core_ids=list(range(NCORES)))
    return np.concatenate([r["hT"].T for r in res.results], 0)


NB5 = L // TH


def s5_disc(P, pre, F):
    names = ['step', 'lr', 'ex', 'mag', 'th', 's', 'c', 't1', 't2', 's2', 'c2', 'ar', 'ai', 'nr', 'den', 'zr', 'zi', 't3']
    tl = {n: P.sb(f"{pre}_{n}", [128, F], F32) for n in names}
    r = lambda n: (pre, n)
    A = lambda n: tl[n][:, :]

    def run(lamre, lamim, lstep):
        P.act(A('step'), lstep, AF.Exp, reads=[(pre, 'in')], writes=[r('step')])
        P.ts('dve', A('lr'), lamre, -1e-4, None, ALU.min, None, reads=[(pre, 'in')], writes=[r('lr')])
        P.tt('dve', A('ex'), A('lr'), A('step'), ALU.mult, reads=[r('lr'), r('step')], writes=[r('ex')])
        P.act(A('mag'), A('ex'), AF.Exp, reads=[r('ex')], writes=[r('mag')])
        P.tt('dve', A('th'), lamim, A('step'), ALU.mult, reads=[(pre, 'in'), r('step')], writes=[r('th')])
        P.act(A('s'), A('th'), AF.Sin, reads=[r('th')], writes=[r('s')], scale=1.0 / 16)
        P.act(A('c'), A('th'), AF.Sin, reads=[r('th'), ('halfpi',)], writes=[r('c')], scale=1.0 / 16, bias=P.halfpi[:, 0:1])
        cs, ss = 'c', 's'
        for it in range(4):
            cn, sn = ('c2', 's2') if it % 2 == 0 else ('c', 's')
            P.tt('dve', A('t1'), A(cs), A(cs), ALU.mult, reads=[r(cs)], writes=[r('t1')])
            P.tt('dve', A('t2'), A(ss), A(ss), ALU.mult, reads=[r(ss)], writes=[r('t2')])
            P.stt(A(sn), A(cs), 2.0, A(ss), ALU.mult, ALU.mult, reads=[r(cs), r(ss)], writes=[r(sn)])
            P.tt('dve', A(cn), A('t1'), A('t2'), ALU.subtract, reads=[r('t1'), r('t2')], writes=[r(cn)])
            cs, ss = cn, sn
        assert cs == 'c'
        P.tt('dve', A('ar'), A('mag'), A('c'), ALU.mult, reads=[r('mag'), r('c')], writes=[r('ar')])
        P.tt('dve', A('ai'), A('mag'), A('s'), ALU.mult, reads=[r('mag'), r('s')], writes=[r('ai')])
        P.ts('dve', A('nr'), A('ar'), -1.0, None, ALU.add, None, reads=[r('ar')], writes=[r('nr')])
        P.tt('dve', A('t1'), A('lr'), A('lr'), ALU.mult, reads=[r('lr')], writes=[r('t1')])
        P.tt('dve', A('t2'), lamim, lamim, ALU.mult, reads=[(pre, 'in')], writes=[r('t2')])
        P.tt('dve', A('den'), A('t1'), A('t2'), ALU.add, reads=[r('t1'), r('t2')], writes=[r('den')])
        P.emit('dve', lambda e: e.reciprocal(out=A('den'), in_=A('den')), reads=[r('den')], writes=[r('den')])
        P.tt('dve', A('t1'), A('nr'), A('lr'), ALU.mult, reads=[r('nr'), r('lr')], writes=[r('t1')])
        P.tt('dve', A('t2'), A('ai'), lamim, ALU.mult, reads=[r('ai'), (pre, 'in')], writes=[r('t2')])
        P.tt('dve', A('t3'), A('t1'), A('t2'), ALU.add, reads=[r('t1'), r('t2')], writes=[r('t3')])
        P.tt('dve', A('zr'), A('t3'), A('den'), ALU.mult, reads=[r('t3'), r('den')], writes=[r('zr')])
        P.tt('dve', A('t1'), A('ai'), A('lr'), ALU.mult, reads=[r('ai'), r('lr')], writes=[r('t1')])
        P.tt('dve', A('t2'), A('nr'), lamim, ALU.mult, reads=[r('nr'), (pre, 'in')], writes=[r('t2')])
        P.tt('dve', A('t3'), A('t1'), A('t2'), ALU.subtract, reads=[r('t1'), r('t2')], writes=[r('t3')])
        P.tt('dve', A('zi'), A('t3'), A('den'), ALU.mult, reads=[r('t3'), r('den')], writes=[r('zi')])
    return tl, run


def build_s5():
    P = Prog()
    uT = P.dram_in("uT", [2, 128, L])
    prow = P.dram_in("prow", [128, 3, 1024])
    pcol = P.dram_in("pcol", [128, 3, 8])
    BT = P.dram_in("BT", [128, 2, 1024])
    CT = P.dram_in("CT", [128, 2, 8, 128])
    dsk = P.dram_in("dsk", [128, 1])
    yT = P.dram_out("yT", [2, 128, L])

    u32 = [[P.sb(f"u32_{d}_{i}", [128, TH], F32) for i in range(2)] for d in range(2)]
    ub = [[P.sb(f"ub{d}_{i}", [128, TH], BF16) for i in range(2)] for d in range(2)]
    prow_sb = P.sb("prow_sb", [128, 3, 1024], F32)
    pcol_sb = P.sb("pcol_sb", [128, 3, 8], F32)
    BT_sb = P.sb("BT_sb", [128, 2, 1024], F32)
    CT_sb = P.sb("CT_sb", [128, 2, 8, 128], F32)
    dsk_sb = P.sb("dsk_sb", [128, 1], F32)
    P.halfpi = P.sb("halfpi", [128, 1], F32)
    Bbar = P.sb("Bbar", [128, 2, 1024], BF16)
    Cb = P.sb("Cb", [128, 2, 8, 128], BF16)
    tw = [P.sb(f"tw{i}", [128, 512], F32) for i in range(2)]
    tabc = P.sb("tabc", [128, 8, TH], F32)
    tabs = P.sb("tabs", [128, 8, TH], F32)
    magT = P.sb("magT", [128, 8, TH], F32)
    cur = [P.sb(f"cur{i}", [128, 2, 8], F32) for i in range(2)]
    curt = P.sb("curt", [128, 2, 8], F32)
    ttmp = P.sb("ttmp", [128, TH], F32)
    init = P.sb("init", [128, 2, 8], F32)
    ini_t = P.sb("ini_t", [128, 2, 8], F32)
    NW = 2
    m = [[P.sb(f"m{i}_{k}", [128, TH], F32) for k in range(4)] for i in range(NW)]
    v = [[P.sb(f"v{i}_{k}", [128, TH], F32) for k in range(2)] for i in range(NW)]
    pm = [[P.sb(f"pm{i}_{k}", [128, TH], F32) for k in range(4)] for i in range(NW)]
    hb = [[P.sb(f"hb{i}_{k}", [128, TH], BF16) for k in range(2)] for i in range(NW)]
    yo = [P.sb(f"yo{i}", [128, TH], F32) for i in range(2)]
    psb = [P.ps(f"psb{i}", [128, 512]) for i in range(4)]
    psy = [P.ps(f"psy{i}", [128, 512]) for i in range(2)]
    psw = [P.ps(f"psw{i}", [128, 512]) for i in range(2)]

    rowt, rowrun = s5_disc(P, 'row', 512)
    colt, colrun = s5_disc(P, 'col', 8)

    P.dcp = lambda eng, out, in_, reads, writes, key: P.dma(eng, lambda e: e.dma_start(out=out, in_=in_), reads=reads, writes=writes, key=key)
    P.dcp('sp', prow_sb[:, :, :], prow[:, :, :], (), [('row', 'in')], 'prow')
    P.dcp('sp', pcol_sb[:, :, :], pcol[:, :, :], (), [('col', 'in')], 'pcol')
    P.dcp('sp', BT_sb[:, :, :], BT[:, :, :], (), [('BT',)], 'BT')
    P.dcp('sp', CT_sb[:, :, :, :], CT[:, :, :, :], (), [('CT',)], 'CT')
    P.dcp('sp', dsk_sb[:, :], dsk[:, :], (), [('dsk',)], 'dsk')
    P.emit('dve', lambda e: e.memset(P.halfpi[:, :], float(np.pi / 2)), writes=[('halfpi',)])
    colrun(pcol_sb[:, 0, :], pcol_sb[:, 1, :], pcol_sb[:, 2, :])
    for d in range(2):
        ds_ = slice(d * 512, (d + 1) * 512)
        rowrun(prow_sb[:, 0, ds_], prow_sb[:, 1, ds_], prow_sb[:, 2, ds_])
        zr, zi = rowt['zr'][:, :], rowt['zi'][:, :]
        zres = [('row', 'zr'), ('row', 'zi'), ('BT',)]
        P.tt('dve', tw[0][:, :], zr, BT_sb[:, 0, ds_], ALU.mult, reads=zres, writes=[('tw', 0)])
        P.tt('dve', tw[1][:, :], zi, BT_sb[:, 1, ds_], ALU.mult, reads=zres, writes=[('tw', 1)])
        P.tt('dve', Bbar[:, 0, ds_], tw[0][:, :], tw[1][:, :], ALU.subtract, reads=[('tw', 0), ('tw', 1)], writes=[('Bbar',)])
        P.tt('dve', tw[0][:, :], zr, BT_sb[:, 1, ds_], ALU.mult, reads=zres + [('Bbar',)], writes=[('tw', 0)])
        P.tt('dve', tw[1][:, :], zi, BT_sb[:, 0, ds_], ALU.mult, reads=zres + [('Bbar',)], writes=[('tw', 1)])
        P.tt('dve', Bbar[:, 1, ds_], tw[0][:, :], tw[1][:, :], ALU.add, reads=[('tw', 0), ('tw', 1)], writes=[('Bbar',)])
    P.act(Cb[:, 0, :, :], CT_sb[:, 0, :, :], AF.Copy, reads=[('CT',)], writes=[('Cb',)])
    P.act(Cb[:, 1, :, :], CT_sb[:, 1, :, :], AF.Copy, reads=[('CT',)], writes=[('Cb',)], scale=-1.0)
    for dj in range(8):
        P.emit('dve', lambda e, dj=dj: e.memset(magT[:, dj, :], 1.0), writes=[('magT',)])
        P.ts('dve', magT[:, dj, :], magT[:, dj, :], colt['mag'][:, dj:dj + 1], None, ALU.mult, None,
             reads=[('magT',), ('col', 'mag')], writes=[('magT',)])
    P.emit('dve', lambda e: e.memset(tabc[:, :, 0:1], 1.0), writes=[('tab',)])
    P.emit('dve', lambda e: e.memset(tabs[:, :, 0:1], 0.0), writes=[('tab',)])
    P.emit('dve', lambda e: e.tensor_copy(out=cur[0][:, 0, :], in_=colt['c'][:, :]), reads=[('col', 'c')], writes=[('cur', 0)])
    P.emit('dve', lambda e: e.tensor_copy(out=cur[0][:, 1, :], in_=colt['s'][:, :]), reads=[('col', 's')], writes=[('cur', 0)])
    ci = 0
    for k in range(9):
        n = 1 << k
        cc = cur[ci]
        for dj in range(8):
            cs_c = cc[:, 0, dj:dj + 1]; cs_s = cc[:, 1, dj:dj + 1]
            rd = [('tab',), ('cur', ci)]
            P.ts('dve', ttmp[:, 0:n], tabs[:, dj, 0:n], cs_s, None, ALU.mult, None, reads=rd, writes=[('ttmp',)])
            P.stt(tabc[:, dj, n:2 * n], tabc[:, dj, 0:n], cs_c, ttmp[:, 0:n], ALU.mult, ALU.subtract, reads=rd + [('ttmp',)], writes=[('tab',)])
            P.ts('dve', ttmp[:, 0:n], tabs[:, dj, 0:n], cs_c, None, ALU.mult, None, reads=rd, writes=[('ttmp',)])
            P.stt(tabs[:, dj, n:2 * n], tabc[:, dj, 0:n], cs_s, ttmp[:, 0:n], ALU.mult, ALU.add, reads=rd + [('ttmp',)], writes=[('tab',)])
        cn = cur[1 - ci]
        P.tt('dve', curt[:, 0, :], cc[:, 0, :], cc[:, 0, :], ALU.mult, reads=[('cur', ci)], writes=[('curt',)])
        P.tt('dve', curt[:, 1, :], cc[:, 1, :], cc[:, 1, :], ALU.mult, reads=[('cur', ci)], writes=[('curt',)])
        P.tt('dve', cn[:, 0, :], curt[:, 0, :], curt[:, 1, :], ALU.subtract, reads=[('curt',)], writes=[('cur', 1 - ci)])
        P.stt(cn[:, 1, :], cc[:, 0, :], 2.0, cc[:, 1, :], ALU.mult, ALU.mult, reads=[('cur', ci)], writes=[('cur', 1 - ci)])
        ci = 1 - ci
    Rc = cur[ci]
    Rres = ('cur', ci)
    P.emit('dve', lambda e: e.memset(init[:, :, :], 0.0), writes=[('init',)])
    Rneg = P.sb("Rneg", [128, 8], F32)
    P.ts('dve', Rneg[:, :], Rc[:, 1, :], -1.0, None, ALU.mult, None, reads=[Rres], writes=[('Rneg',)])

    outs = []
    st = {'wi': 0, 'pbi': 0}

    def stage1(b, d, j):
        t0 = b * TH
        ui = b % 2
        if d == 0 and j == 0:
            for dd in range(2):
                P.dcp('sp', u32[dd][ui][:, :], uT[dd, :, t0:t0 + TH], (), [('u32', dd, ui)], ('u32', dd, ui))
                P.act(ub[dd][ui][:, :], u32[dd][ui][:, :], AF.Copy, reads=[('u32', dd, ui)], writes=[('ub', dd, ui)])
        dj = d * 4 + j
        i = st['wi'] % NW; st['wi'] += 1
        pbi = st['pbi']; st['pbi'] += 1
        pr_ = psb[(pbi * 2) % 4]; pi_ = psb[(pbi * 2 + 1) % 4]
        prr = ('psb', (pbi * 2) % 4); pir = ('psb', (pbi * 2 + 1) % 4)
        col0 = d * 512 + j * 128
        P.mm(pr_[:, :], Bbar[:, 0, col0:col0 + 128], ub[d][ui][:, :], True, True, reads=[('Bbar',), ('ub', d, ui)], writes=[prr])
        P.mm(pi_[:, :], Bbar[:, 1, col0:col0 + 128], ub[d][ui][:, :], True, True, reads=[('Bbar',), ('ub', d, ui)], writes=[pir])
        tc = tabc[:, dj, :]; tsn = tabs[:, dj, :]
        mr = lambda k: ('m', i, k)
        P.tt('dve', m[i][0][:, :], pr_[:, :], tc, ALU.mult, reads=[prr, ('tab',)], writes=[mr(0)])
        P.tt('dve', m[i][1][:, :], pi_[:, :], tsn, ALU.mult, reads=[pir, ('tab',)], writes=[mr(1)])
        P.tt('dve', m[i][2][:, :], pi_[:, :], tc, ALU.mult, reads=[pir, ('tab',)], writes=[mr(2)])
        P.tt('dve', m[i][3][:, :], pr_[:, :], tsn, ALU.mult, reads=[prr, ('tab',)], writes=[mr(3)])
        P.tt('pool', v[i][0][:, :], m[i][0][:, :], m[i][1][:, :], ALU.add, reads=[mr(0), mr(1)], writes=[('v', i, 0)])
        P.tt('pool', v[i][1][:, :], m[i][2][:, :], m[i][3][:, :], ALU.subtract, reads=[mr(2), mr(3)], writes=[('v', i, 1)])
        return (b, d, j, i)

    def stage2(ctx):
        (b, d, j, i) = ctx
        t0 = b * TH
        ui = b % 2
        dj = d * 4 + j
        py = psy[d]; pyr = ('psy', d)
        tc = tabc[:, dj, :]; tsn = tabs[:, dj, :]
        for k in range(2):
            P.emit('dve', lambda e, i=i, k=k, dj=dj: e.tensor_tensor_scan(
                out=psw[k][:, :], data0=magT[:, dj, :], data1=v[i][k][:, :], initial=init[:, k, dj:dj + 1],
                op0=ALU.mult, op1=ALU.add), reads=[('magT',), ('v', i, k), ('init',)], writes=[('psw', k)])
        pmr = lambda k: ('pm', i, k)
        P.tt('dve', pm[i][0][:, :], psw[0][:, :], tc, ALU.mult, reads=[('psw', 0), ('tab',)], writes=[pmr(0)])
        P.tt('dve', pm[i][1][:, :], psw[1][:, :], tsn, ALU.mult, reads=[('psw', 1), ('tab',)], writes=[pmr(1)])
        P.tt('dve', pm[i][2][:, :], psw[0][:, :], tsn, ALU.mult, reads=[('psw', 0), ('tab',)], writes=[pmr(2)])
        P.tt('dve', pm[i][3][:, :], psw[1][:, :], tc, ALU.mult, reads=[('psw', 1), ('tab',)], writes=[pmr(3)])
        wl_r = psw[0][:, TH - 1:TH]; wl_i = psw[1][:, TH - 1:TH]
        wres = [('psw', 0), ('psw', 1), Rres, ('Rneg',)]
        P.act(ini_t[:, 0, dj:dj + 1], wl_i, AF.Copy, reads=wres, writes=[('ini_t',)], scale=Rneg[:, dj:dj + 1])
        P.act(ini_t[:, 1, dj:dj + 1], wl_i, AF.Copy, reads=wres, writes=[('ini_t',)], scale=Rc[:, 0, dj:dj + 1])
        P.act(init[:, 0, dj:dj + 1], wl_r, AF.Identity, reads=wres + [('ini_t',)], writes=[('init',)],
              scale=Rc[:, 0, dj:dj + 1], bias=ini_t[:, 0, dj:dj + 1])
        P.act(init[:, 1, dj:dj + 1], wl_r, AF.Identity, reads=wres + [('ini_t',)], writes=[('init',)],
              scale=Rc[:, 1, dj:dj + 1], bias=ini_t[:, 1, dj:dj + 1])
        P.tt('pool', hb[i][0][:, :], pm[i][0][:, :], pm[i][1][:, :], ALU.subtract, reads=[pmr(0), pmr(1)], writes=[('hb', i, 0)])
        P.tt('pool', hb[i][1][:, :], pm[i][2][:, :], pm[i][3][:, :], ALU.add, reads=[pmr(2), pmr(3)], writes=[('hb', i, 1)])
        P.mm(py[:, :], Cb[:, 0, dj, :], hb[i][0][:, :], j == 0, False, reads=[('Cb',), ('hb', i, 0)], writes=[pyr])
        P.mm(py[:, :], Cb[:, 1, dj, :], hb[i][1][:, :], False, j == 3, reads=[('Cb',), ('hb', i, 1)], writes=[pyr])
        if j == 3:
            yr = ('yo', d)
            if d == 0:
                P.stt(yo[d][:, :], u32[0][ui][:, :], dsk_sb[:, 0:1], py[:, :], ALU.mult, ALU.add,
                      reads=[('u32', 0, ui), ('dsk',), pyr], writes=[yr])
            else:
                P.act(yo[d][:, :], py[:, :], AF.Copy, reads=[pyr], writes=[yr])
            P.dcp('sp', yT[d, :, t0:t0 + TH], yo[d][:, :], [yr], [('out', b, d)], ('yo', d))
            outs.append(('out', b, d))

    pending = None
    for b in range(NB5):
        for d in range(2):
            for j in range(4):
                ctx = stage1(b, d, j)
                if pending is not None:
                    stage2(pending)
                pending = ctx
    stage2(pending)
    return P.finish(outs)


def s5_host_layout(inp, c):
    g0 = 8 * c
    lamre = inp['s5_lambda_re'][0][:, g0:g0 + 8, :]
    lamim = inp['s5_lambda_im'][0][:, g0:g0 + 8, :]
    lstep = np.broadcast_to(inp['s5_log_step'][0][:, g0:g0 + 8, None], (2, 8, 64))
    st = np.stack([lamre, lamim, lstep], 0).reshape(3, 2, 512)
    prow = np.ascontiguousarray(np.broadcast_to(st.reshape(1, 3, 1024), (128, 3, 1024))).astype(np.float32)
    pcol = np.ascontiguousarray(st.reshape(3, 2, 4, 128).transpose(3, 0, 1, 2).reshape(128, 3, 8)).astype(np.float32)
    BT = np.zeros((128, 2, 2, 512), np.float32)
    CT = np.zeros((128, 2, 2, 4, 128), np.float32)
    for ri, (bk, ck) in enumerate([('s5_b_re', 's5_c_re'), ('s5_b_im', 's5_c_im')]):
        Bm = inp[bk][0][:, g0:g0 + 8]
        Cm = inp[ck][0][:, g0:g0 + 8]
        for d in range(2):
            for g in range(8):
                BT[g * 16:(g + 1) * 16, ri, d, g * 64:(g + 1) * 64] = Bm[d, g].T
                j, gg = g // 2, g % 2
                CT[gg * 64:(gg + 1) * 64, ri, d, j, g * 16:(g + 1) * 16] = Cm[d, g].T
    dsk = inp['s5_d'][0][g0:g0 + 8].reshape(128, 1).astype(np.float32)
    return {"prow": prow, "pcol": pcol, "BT": BT.reshape(128, 2, 1024), "CT": CT.reshape(128, 2, 8, 128), "dsk": np.ascontiguousarray(dsk)}


def run_s5(u_tok, inp):
    nc = build_s5()
    in_maps = []
    for c in range(NCORES):
        uc = u_tok[:, c * 128:(c + 1) * 128].T
        m = s5_host_layout(inp, c)
        m["uT"] = np.ascontiguousarray(np.stack([uc, uc[:, ::-1]], 0))
        in_maps.append(m)
    res = run_bass_kernel_spmd(nc, in_maps, core_ids=list(range(NCORES)))
    yf = np.concatenate([r["yT"][0].T for r in res.results], 1)
    yb = np.concatenate([r["yT"][1][:, ::-1].T for r in res.results], 1)
    return yf, yb


def build_ret():
    P = Prog()
    NCH = L // 128
    qT = P.dram_in("qT", [2, 128, L])
    kT = P.dram_in("kT", [2, 128, L])
    ktok = P.dram_in("ktok", [L, 256])
    vtok = P.dram_in("vtok", [L, 256])
    lg = P.dram_in("lg", [128, 1])
    maskT = P.dram_in("maskT", [128, 128])
    irow = P.dram_in("irow", [128, TH])
    icol = P.dram_in("icol", [128, 1])
    o = P.dram_out("o", [L, 256])
    P.dcp = lambda eng, out, in_, reads, writes, key: P.dma(eng, lambda e: e.dma_start(out=out, in_=in_), reads=reads, writes=writes, key=key)

    lg_sb = P.sb("lg_sb", [128, 1], F32)
    lgn = P.sb("lgn", [128, 1], F32)
    lgp = P.sb("lgp", [128, 1], F32)
    mask_sb = P.sb("mask_sb", [128, 128], F32)
    irow_sb = P.sb("irow_sb", [128, TH], F32)
    icol_sb = P.sb("icol_sb", [128, 1], F32)
    gq = P.sb("gq", [128, TH], F32)
    gk = P.sb("gk", [128, TH], F32)
    gkc = P.sb("gkc", [128, 1], F32)
    gC = P.sb("gC", [128, 1], F32)
    c128 = P.sb("c128", [128, 1], F32)
    q32 = [P.sb(f"q32_{i}", [128, 2, TH], F32) for i in range(2)]
    k32 = [P.sb(f"k32_{i}", [128, 2, TH], F32) for i in range(2)]
    qb = [P.sb(f"qb_{i}", [128, 2, TH], BF16) for i in range(2)]
    kb = [P.sb(f"kb_{i}", [128, 2, TH], BF16) for i in range(2)]
    kt32 = [P.sb(f"kt32_{i}", [128, 4, 256], F32) for i in range(2)]
    vt32 = [P.sb(f"vt32_{i}", [128, 4, 256], F32) for i in range(2)]
    ktb = [P.sb(f"ktb_{i}", [128, 4, 256], BF16) for i in range(2)]
    vtb = [P.sb(f"vtb_{i}", [128, 4, 256], BF16) for i in range(2)]
    Sm = [P.sb(f"Sm{i}", [128, 128], BF16) for i in range(2)]
    Tst = P.sb("Tst", [128, 2, 256], F32)
    Sbf = P.sb("Sbf", [128, 2, 256], BF16)
    osb = [P.sb(f"osb{i}", [128, 256], F32) for i in range(2)]
    psS = [P.ps(f"psS{i}", [128, 512]) for i in range(2)]
    psO = [P.ps(f"psO{i}", [128, 512]) for i in range(2)]
    psK = [P.ps(f"psK{i}", [128, 512]) for i in range(2)]

    P.dcp('sp', lg_sb[:, :], lg[:, :], (), [('lg',)], 'lg')
    P.dcp('sp', mask_sb[:, :], maskT[:, :], (), [('mask',)], 'mask')
    P.dcp('sp', irow_sb[:, :], irow[:, :], (), [('irow',)], 'irow')
    P.dcp('sp', icol_sb[:, :], icol[:, :], (), [('icol',)], 'icol')
    P.act(lgp[:, :], lg_sb[:, :], AF.Abs, reads=[('lg',)], writes=[('lgp',)])
    P.ts('dve', lgn[:, :], lgp[:, :], -1.0, None, ALU.mult, None, reads=[('lgp',)], writes=[('lgn',)])
    P.act(gq[:, :], irow_sb[:, :], AF.Exp, reads=[('irow',), ('lgn',)], writes=[('gq',)], scale=lgn[:, 0:1])
    P.act(gk[:, :], irow_sb[:, :], AF.Exp, reads=[('irow',), ('lgp',)], writes=[('gk',)], scale=lgp[:, 0:1])
    P.ts('dve', gk[:, :], gk[:, :], 1.0 / 16, None, ALU.mult, None, reads=[('gk',)], writes=[('gk',)])
    P.act(gkc[:, :], icol_sb[:, :], AF.Exp, reads=[('icol',), ('lgp',)], writes=[('gkc',)], scale=lgp[:, 0:1])
    P.ts('dve', gkc[:, :], gkc[:, :], 1.0 / 16, None, ALU.mult, None, reads=[('gkc',)], writes=[('gkc',)])
    P.emit('dve', lambda e: e.memset(c128[:, :], 128.0), writes=[('c128',)])
    P.act(gC[:, :], c128[:, :], AF.Exp, reads=[('c128',), ('lgn',)], writes=[('gC',)], scale=lgn[:, 0:1])
    P.emit('dve', lambda e: e.memset(Tst[:, :, :], 0.0), writes=[('Tst', 0), ('Tst', 1)])

    outs = []
    for sbk in range(L // TH):
        t0 = sbk * TH
        bi = sbk % 2
        for kc in range(2):
            P.dcp('sp', q32[bi][:, kc, :], qT[kc, :, t0:t0 + TH], (), [('q32', bi, kc)], ('q32', bi, kc))
            P.dcp('sp', k32[bi][:, kc, :], kT[kc, :, t0:t0 + TH], (), [('k32', bi, kc)], ('k32', bi, kc))
            P.tt('dve', qb[bi][:, kc, :], q32[bi][:, kc, :], gq[:, :], ALU.mult, reads=[('q32', bi, kc), ('gq',)], writes=[('qb', bi, kc)])
            P.tt('dve', kb[bi][:, kc, :], k32[bi][:, kc, :], gk[:, :], ALU.mult, reads=[('k32', bi, kc), ('gk',)], writes=[('kb', bi, kc)])
        P.dcp('sp', kt32[bi][:, :, :], ktok[t0:t0 + TH, :].rearrange("(c p) e -> p c e", p=128), (), [('kt32', bi)], ('kt32', bi))
        P.dcp('sp', vt32[bi][:, :, :], vtok[t0:t0 + TH, :].rearrange("(c p) e -> p c e", p=128), (), [('vt32', bi)], ('vt32', bi))
        P.ts('dve', ktb[bi][:, :, :], kt32[bi][:, :, :], gkc[:, 0:1], None, ALU.mult, None, reads=[('kt32', bi), ('gkc',)], writes=[('ktb', bi)])
        P.act(vtb[bi][:, :, :], vt32[bi][:, :, :], AF.Copy, reads=[('vt32', bi)], writes=[('vtb', bi)])
        for cc in range(4):
            n = sbk * 4 + cc
            cs = slice(cc * 128, (cc + 1) * 128)
            pS = psS[n % 2]; pSr = ('psS', n % 2)
            pO = psO[n % 2]; pOr = ('psO', n % 2)
            for kc in range(2):
                P.mm(pS[:, 0:128], kb[bi][:, kc, cs], qb[bi][:, kc, cs], kc == 0, kc == 1,
                     reads=[('kb', bi, kc), ('qb', bi, kc)], writes=[pSr])
            sm = Sm[n % 2]; smr = ('Sm', n % 2)
            P.tt('dve', sm[:, :], pS[:, 0:128], mask_sb[:, :], ALU.mult, reads=[pSr, ('mask',)], writes=[smr])
            P.mm(pO[:, 0:256], sm[:, :], vtb[bi][:, cc, :], True, n == 0, reads=[smr, ('vtb', bi)], writes=[pOr])
            if n > 0:
                for kc in range(2):
                    P.mm(pO[:, 0:256], qb[bi][:, kc, cs], Sbf[:, kc, :], False, kc == 1,
                         reads=[('qb', bi, kc), ('Sbf', kc)], writes=[pOr])
            ob = osb[n % 2]; obr = ('osb', n % 2)
            P.act(ob[:, :], pO[:, 0:256], AF.Copy, reads=[pOr], writes=[obr])
            P.dcp('sp', o[n * 128:(n + 1) * 128, :], ob[:, :], [obr], [('out', n)], ('osb', n % 2))
            outs.append(('out', n))
            if n < NCH - 1:
                for kc in range(2):
                    pK = psK[kc]; pKr = ('psK', kc)
                    P.mm(pK[:, 0:256], ktb[bi][:, cc, kc * 128:(kc + 1) * 128], vtb[bi][:, cc, :], True, True,
                         reads=[('ktb', bi), ('vtb', bi)], writes=[pKr])
                    P.stt(Tst[:, kc, :], Tst[:, kc, :], gC[:, 0:1], pK[:, 0:256], ALU.mult, ALU.add,
                          reads=[('Tst', kc), ('gC',), pKr], writes=[('Tst', kc)])
                    P.act(Sbf[:, kc, :], Tst[:, kc, :], AF.Copy, reads=[('Tst', kc), ('gC',)], writes=[('Sbf', kc)], scale=gC[:, 0:1])
    return P.finish(outs)


def run_ret(q_rot, k_rot, v, ret_log_decay):
    nc = build_ret()
    in_maps = []
    ii = np.arange(128)
    irow = np.ascontiguousarray(np.broadcast_to(((np.arange(TH) % 128) + 1).astype(np.float32)[None, :], (128, TH)))
    icol = (ii + 1).astype(np.float32).reshape(128, 1)
    for c in range(NCORES):
        hh, d = c // 2, c % 2
        qq, kk, vv = q_rot[:, hh], k_rot[:, hh], v[:, hh]
        if d == 1:
            qq, kk, vv = qq[::-1], kk[::-1], vv[::-1]
        mk = (ii[None, :] >= ii[:, None]) if d == 0 else (ii[None, :] > ii[:, None])
        in_maps.append({"qT": np.ascontiguousarray(qq.T.reshape(2, 128, L)), "kT": np.ascontiguousarray(kk.T.reshape(2, 128, L)),
                        "ktok": np.ascontiguousarray(kk), "vtok": np.ascontiguousarray(vv),
                        "lg": np.full((128, 1), ret_log_decay[d, hh], np.float32), "maskT": mk.astype(np.float32),
                        "irow": irow, "icol": icol})
    res = run_bass_kernel_spmd(nc, in_maps, core_ids=list(range(NCORES)))
    o_f = np.stack([res.results[2 * hh]["o"] for hh in range(4)], 1)
    o_b = np.stack([res.results[2 * hh + 1]["o"][::-1] for hh in range(4)], 1)
    return o_f, o_b


def build_attn():
    P = Prog()
    TK = T + 256
    NBLK = T // 128
    qT = P.dram_in("qT", [16, 128, T])
    kT = P.dram_in("kT", [4, 128, TK])
    vtok = P.dram_in("vtok", [TK, 512])
    sink = P.dram_in("sink", [128, 16])
    mbias = P.dram_in("mbias", [128, 3, 384])
    ident = P.dram_in("ident", [128, 128])
    oT = P.dram_out("oT", [16, 128, T])
    P.dcp = lambda eng, out, in_, reads, writes, key: P.dma(eng, lambda e: e.dma_start(out=out, in_=in_), reads=reads, writes=writes, key=key)

    qb = P.sb("qb", [128, 16, T], BF16)
    kb = P.sb("kb", [128, 4, TK], BF16)
    vb = P.sb("vb", [128, TK // 128, 512], BF16)
    sink_sb = P.sb("sink_sb", [128, 16], F32)
    mb_sb = P.sb("mb_sb", [128, 3, 384], F32)
    id32 = P.sb("id32", [128, 128], F32)
    idb = P.sb("idb", [128, 128], BF16)
    osb = P.sb("osb", [128, 16, T], F32)
    NR = 4
    Sb = [P.sb(f"Sb{i}", [128, 384], F32) for i in range(NR)]
    Pe = [P.sb(f"Pe{i}", [128, 384], BF16) for i in range(NR)]
    Pn = [P.sb(f"Pn{i}", [128, 384], BF16) for i in range(NR)]
    PT = [P.sb(f"PT{i}", [128, 3, 128], BF16) for i in range(NR)]
    sm = [P.sb(f"sm{i}", [128, 8], F32) for i in range(NR)]
    psS = [P.ps(f"psS{i}", [128, 512]) for i in range(2)]
    psT = [P.ps(f"psT{i}", [128, 3, 128], BF16) for i in range(2)]
    psO = [P.ps(f"psO{i}", [128, 512]) for i in range(2)]

    for h in range(16):
        P.dcp('pool', qb[:, h, :], qT[h, :, :], (), [('qb', h)], ('qb', h))
    for h in range(4):
        P.dcp('pool', kb[:, h, :], kT[h, :, :], (), [('kb', h)], ('kb', h))
    P.dcp('pool', vb[:, :, :], vtok.rearrange("(c p) e -> p c e", p=128), (), [('vb',)], 'vb')
    P.dcp('sp', sink_sb[:, :], sink[:, :], (), [('sink',)], 'sink')
    P.dcp('sp', mb_sb[:, :, :], mbias[:, :, :], (), [('mb',)], 'mb')
    P.dcp('sp', id32[:, :], ident[:, :], (), [('id32',)], 'id32')
    P.act(idb[:, :], id32[:, :], AF.Copy, reads=[('id32',)], writes=[('idb',)])
    SCALE = float(128 ** -0.5)
    items = [(c, hq) for c in range(NBLK) for hq in range(16)]

    def phA(n):
        c, hq = items[n]
        mi = 0 if c == 0 else (2 if c == NBLK - 1 else 1)
        qs = slice(c * 128, (c + 1) * 128)
        kv = hq // 4
        i = n % NR
        pS = psS[n % 2]; pSr = ('psS', n % 2)
        P.mm(pS[:, 0:384], qb[:, hq, qs], kb[:, kv, c * 128:c * 128 + 384], True, True,
             reads=[('qb', hq), ('kb', kv)], writes=[pSr])
        P.stt(Sb[i][:, :], pS[:, 0:384], SCALE, mb_sb[:, mi, :], ALU.mult, ALU.add, reads=[pSr, ('mb',)], writes=[('Sb', i)])
        s_ = sm[i]; smr = ('sm', i)
        P.emit('dve', lambda e, i=i, s_=s_: e.reduce_max(out=s_[:, 0:1], in_=Sb[i][:, :], axis=mybir.AxisListType.X),
               reads=[('Sb', i)], writes=[smr])
        P.tt('dve', s_[:, 1:2], s_[:, 0:1], sink_sb[:, hq:hq + 1], ALU.max, reads=[smr, ('sink',)], writes=[smr])
        P.ts('dve', s_[:, 2:3], s_[:, 1:2], -1.0, None, ALU.mult, None, reads=[smr], writes=[smr])
        P.emit('act', lambda e, i=i, s_=s_: e.activation(out=Pe[i][:, :], in_=Sb[i][:, :], func=AF.Exp, bias=s_[:, 2:3],
                                                        accum_out=s_[:, 3:4]), reads=[('Sb', i), smr], writes=[('Pe', i), smr])
        P.act(s_[:, 4:5], sink_sb[:, hq:hq + 1], AF.Exp, reads=[('sink',), smr], writes=[smr], bias=s_[:, 2:3])

    def phB(n):
        i = n % NR
        s_ = sm[i]; smr = ('sm', i)
        P.tt('dve', s_[:, 5:6], s_[:, 3:4], s_[:, 4:5], ALU.add, reads=[smr], writes=[smr])
        P.emit('dve', lambda e, s_=s_: e.reciprocal(out=s_[:, 6:7], in_=s_[:, 5:6]), reads=[smr], writes=[smr])
        P.ts('dve', Pn[i][:, :], Pe[i][:, :], s_[:, 6:7], None, ALU.mult, None, reads=[('Pe', i), smr], writes=[('Pn', i)])
        pT = psT[n % 2]; pTr = ('psT', n % 2)
        for kk in range(3):
            P.emit('pe', lambda e, i=i, kk=kk, pT=pT: e.transpose(out=pT[:, kk, :], in_=Pn[i][:, kk * 128:(kk + 1) * 128], identity=idb[:, :]),
                   reads=[('Pn', i), ('idb',)], writes=[pTr])
        P.act(PT[i][:, :, :], pT[:, :, :], AF.Copy, reads=[pTr], writes=[('PT', i)])

    def phC(n):
        c, hq = items[n]
        qs = slice(c * 128, (c + 1) * 128)
        kv = hq // 4
        i = n % NR
        pO = psO[n % 2]; pOr = ('psO', n % 2)
        for kk in range(3):
            P.mm(pO[:, 0:128], vb[:, c + kk, kv * 128:(kv + 1) * 128], PT[i][:, kk, :], kk == 0, kk == 2,
                 reads=[('vb',), ('PT', i)], writes=[pOr])
        P.emit('dve', lambda e, hq=hq, qs=qs, pO=pO: e.tensor_copy(out=osb[:, hq, qs], in_=pO[:, 0:128]), reads=[pOr], writes=[('osb', hq)])

    NI = len(items)
    for n in range(NI + 2):
        if n < NI: phA(n)
        if 0 <= n - 1 < NI: phB(n - 1)
        if 0 <= n - 2 < NI: phC(n - 2)
    outs = []
    for hq in range(16):
        P.dcp('sp', oT[hq, :, :], osb[:, hq, :], [('osb', hq)], [('out', hq)], ('osb', hq))
        outs.append(('out', hq))
    return P.finish(outs)


def attn_masks():
    t = np.arange(128)[:, None]; s = np.arange(384)[None, :]
    inwin = np.abs(t - s + 128) <= 128
    NEG = np.float32(-30000.0)
    mid = np.where(inwin, 0.0, NEG).astype(np.float32)
    first = np.where(inwin & (s >= 128), 0.0, NEG).astype(np.float32)
    last = np.where(inwin & (s < 256), 0.0, NEG).astype(np.float32)
    return mid, first, last


def run_attn(q_rot, k_rot, v, sink):
    nc = build_attn()
    mid, first, last = attn_masks()
    kp = np.zeros((L + 256, 4, 128), np.float32); kp[128:L + 128] = k_rot
    vp = np.zeros((L + 256, 4, 128), np.float32); vp[128:L + 128] = v
    in_maps = []
    for c in range(NCORES):
        sl = slice(c * T, (c + 1) * T)
        hs = slice(c * T, (c + 1) * T + 256)
        mb = np.stack([first if c == 0 else mid, mid, last if c == NCORES - 1 else mid], 1)
        in_maps.append({"qT": np.ascontiguousarray(q_rot[sl].transpose(1, 2, 0)), "kT": np.ascontiguousarray(kp[hs].transpose(1, 2, 0)),
                        "vtok": np.ascontiguousarray(vp[hs].reshape(T + 256, 512)),
                        "sink": np.ascontiguousarray(np.broadcast_to(sink.reshape(1, 16), (128, 16))).astype(np.float32),
                        "mbias": np.ascontiguousarray(mb), "ident": np.eye(128, dtype=np.float32)})
    res = run_bass_kernel_spmd(nc, in_maps, core_ids=list(range(NCORES)))
    return np.concatenate([r["oT"].transpose(2, 0, 1).reshape(T, 2048) for r in res.results], 0)


def build_mid(stop=0):
    P = TokProg()
    xT = P.dram_in("xT", [D, T])
    yfT = P.dram_in("yfT", [1024, T])
    ybT = P.dram_in("ybT", [1024, T])
    ofT = P.dram_in("ofT", [1024, T])
    obT = P.dram_in("obT", [1024, T])
    gT = P.dram_in("gT", [1024, T])
    w_glu = P.wparam("w_glu", 1024, 1024)
    b_glu = P.dram_in("b_glu", [128, 8])
    w_out = P.wparam("w_out", D, D)
    w1 = P.wparam("w1", D, 4 * D)
    w2 = P.wparam("w2", 4 * D, D)
    lnp = P.dram_in("lnp", [128, 4, 16])
    w_in2 = P.wparam("w_in2", D, 3072)
    cosF = P.dram_in("cosF", [128, T])
    sinF = P.dram_in("sinF", [128, T])
    perm = P.dram_in("perm", [128, 128])
    x2T = P.dram_out("x2T", [D, T])
    h2T = P.dram_out("h2T", [3072, T])

    X = P.sb("X", [128, 16, TH], F32)
    Rb = P.sb("R", [128, 16, TH], F32)
    Bb = P.sb("Bb", [128, 16, TH], BF16)
    H = P.sb("H", [128, 64, TH], BF16)
    lnp_sb = P.sb("lnp_sb", [128, 4, 16], F32)
    bglu_sb = P.sb("bglu_sb", [128, 8], F32)
    cs = P.sb("cs", [128, 2, T], F32)
    perm32 = P.sb("perm32", [128, 128], F32)
    permb = P.sb("permb", [128, 128], BF16)
    hb16 = [P.sb(f"hb16_{i}", [128, TH], BF16) for i in range(2)]
    P.setup_common()
    xres = lambda kc: ('X', kc)
    rres = lambda kc: ('R', kc)
    bres = lambda kc: ('B', kc)
    hres = lambda kc: ('H', kc)
    v = lambda ap: ap.rearrange("(kc p) t -> p kc t", p=128)
    xTv, yfv, ybv, ofv, obv, gv, x2v, h2v = v(xT), v(yfT), v(ybT), v(ofT), v(obT), v(gT), v(x2T), v(h2T)
    YB = H

    def body():
        P.start()
        P.dcp('sp', lnp_sb[:, :, :], lnp[:, :, :], (), [('lnp',)], 'lnp')
        P.dcp('sp', bglu_sb[:, :], b_glu[:, :], (), [('bglu',)], 'bglu')
        P.dcp('sp', cs[:, 0, :], cosF[:, :], (), [('cs',)], 'cs0')
        P.dcp('sp', cs[:, 1, :], sinF[:, :], (), [('cs',)], 'cs1')
        P.dcp('sp', perm32[:, :], perm[:, :], (), [('perm32',)], 'perm32')
        P.act(permb[:, :], perm32[:, :], AF.Copy, reads=[('perm32',)], writes=[('permb',)])
        outs = []
        hbi = [0]
        for half in range(T // TH):
            t0 = half * TH
            ts_ = slice(t0, t0 + TH)
            for n in range(8):
                P.dcp('sp', Rb[:, n, :], yfv[:, n, ts_], (), [rres(n)], ('R', n))
                P.dcp('sp', Rb[:, 8 + n, :], ybv[:, n, ts_], (), [rres(8 + n)], ('R', 8 + n))
            for n in range(8):
                xs = X[:, n, :]
                t1 = P.tmp[0]; t2 = P.tmp[1]
                P.tt('dve', xs, Rb[:, n, :], Rb[:, 8 + n, :], ALU.add, reads=[rres(n), rres(8 + n)], writes=[xres(n)])
                P.tt('dve', t1[:, :], xs, xs, ALU.mult, reads=[xres(n)], writes=[('tmp', 0)])
                P.ts('dve', t1[:, :], t1[:, :], 0.044715, 1.0, ALU.mult, ALU.add, reads=[('tmp', 0)], writes=[('tmp', 0)])
                P.tt('dve', t1[:, :], t1[:, :], xs, ALU.mult, reads=[('tmp', 0), xres(n)], writes=[('tmp', 0)])
                P.act(t2[:, :], t1[:, :], AF.Sigmoid, reads=[('tmp', 0)], writes=[('tmp', 1)], scale=1.5957691216057308)
                P.tt('dve', xs, xs, t2[:, :], ALU.mult, reads=[xres(n), ('tmp', 1)], writes=[xres(n)])
                P.act(YB[:, n, :], xs, AF.Copy, reads=[xres(n)], writes=[hres(n)])

            def evg(n, ps, pr):
                t2 = P.tmp[1]
                P.act(t2[:, :], ps[:, :], AF.Sigmoid, reads=[pr, ('bglu',)], writes=[('tmp', 1)], bias=bglu_sb[:, n:n + 1])
                P.tt('dve', Bb[:, n, :], X[:, n, :], t2[:, :], ALU.mult, reads=[xres(n), ('tmp', 1)], writes=[bres(n)])
            P.lin(w_glu, 1024, 1024, lambda kc: YB[:, kc, :], hres, evg)
            if stop == 1:
                for kc in range(8):
                    P.dcp('sp', x2v[:, kc, ts_], X[:, kc, :], [xres(kc), bres(kc)], [('out', half, kc)], ('X', kc))
                    outs.append(('out', half, kc))
                return outs
            for n in range(8):
                P.dcp('sp', Rb[:, n, :], ofv[:, n, ts_], (), [rres(n)], ('R', n))
                P.dcp('sp', Rb[:, 8 + n, :], obv[:, n, ts_], (), [rres(8 + n)], ('R', 8 + n))
            for n in range(8):
                P.tt('dve', X[:, 8 + n, :], Rb[:, n, :], Rb[:, 8 + n, :], ALU.add, reads=[rres(n), rres(8 + n)], writes=[xres(8 + n)])
            for n in range(8):
                P.dcp('sp', Rb[:, n, :], gv[:, n, ts_], (), [rres(n)], ('R', n))
            for hh in range(4):
                P.stats([(X[:, 8 + 2 * hh + k, :], xres(8 + 2 * hh + k)) for k in range(2)], 1e-6)
                for k in range(2):
                    n = 2 * hh + k
                    t1 = P.tmp[0]; t2 = P.tmp[1]
                    P.act(t2[:, :], Rb[:, n, :], AF.Silu, reads=[rres(n)], writes=[('tmp', 1)])
                    P.tt('dve', t1[:, :], X[:, 8 + n, :], P.mean[:, :], ALU.subtract, reads=[xres(8 + n), ('mean',)], writes=[('tmp', 0)])
                    P.tt('dve', t1[:, :], t1[:, :], P.rstd[:, :], ALU.mult, reads=[('tmp', 0), ('rstd',)], writes=[('tmp', 0)])
                    P.tt('dve', Bb[:, 8 + n, :], t1[:, :], t2[:, :], ALU.mult, reads=[('tmp', 0), ('tmp', 1)], writes=[bres(8 + n)])
            if stop == 2:
                for kc in range(16):
                    P.dcp('sp', x2v[:, kc, ts_], X[:, kc, :], [xres(kc), bres(kc)], [('out', half, kc)], ('X', kc))
                    outs.append(('out', half, kc))
                return outs
            for kc in range(16):
                P.dcp('sp', X[:, kc, :], xTv[:, kc, ts_], (), [xres(kc)], ('X', kc))

            def ev1(n, ps, pr):
                P.stt(Rb[:, n, :], X[:, n, :], ALPHA, ps[:, :], ALU.mult, ALU.add, reads=[xres(n), pr], writes=[rres(n)])
            P.lin(w_out, D, D, lambda kc: Bb[:, kc, :], bres, ev1)
            P.layernorm(Rb, rres, X, xres, Bb, bres, lnp_sb[:, 0, :], lnp_sb[:, 1, :])

            def ev2(n, ps, pr):
                P.act(H[:, n, :], ps[:, :], AF.Relu, reads=[pr], writes=[hres(n)])
                P.tt('dve', H[:, n, :], H[:, n, :], H[:, n, :], ALU.mult, reads=[hres(n)], writes=[hres(n)])
            P.lin(w1, D, 4 * D, lambda kc: Bb[:, kc, :], bres, ev2)

            def ev3(n, ps, pr):
                P.stt(X[:, n, :], X[:, n, :], ALPHA, ps[:, :], ALU.mult, ALU.add, reads=[xres(n), pr], writes=[xres(n)])
            P.lin(w2, 4 * D, D, lambda kc: H[:, kc, :], hres, ev3)
            P.layernorm(X, xres, Rb, rres, Bb, bres, lnp_sb[:, 2, :], lnp_sb[:, 3, :])
            for kc in range(16):
                P.dcp('sp', x2v[:, kc, ts_], Rb[:, kc, :], [rres(kc)], [('out', half, kc)], ('R', kc))
                outs.append(('out', half, kc))
            if stop == 3:
                continue
            cosv = cs[:, 0, ts_]; sinv = cs[:, 1, ts_]

            def ev4(n, ps, pr):
                ores = ('X', n % 16)
                ob = X[:, n % 16, :]
                if n < 20:
                    i = hbi[0] % 2; hbi[0] += 1
                    P.act(hb16[i][:, :], ps[:, :], AF.Copy, reads=[pr], writes=[('hb16', i), pr])
                    pp = P.pss[6 + i]; ppr = ('ps', 6 + i)
                    P.mm(pp[:, :], permb[:, :], hb16[i][:, :], True, True, reads=[('permb',), ('hb16', i)], writes=[ppr])
                    t1 = P.tmp[0]; t2 = P.tmp[1]
                    P.tt('dve', t1[:, :], ps[:, :], cosv, ALU.mult, reads=[pr, ('cs',)], writes=[('tmp', 0)])
                    P.tt('dve', t2[:, :], pp[:, :], sinv, ALU.mult, reads=[ppr, ('cs',)], writes=[('tmp', 1)])
                    P.tt('dve', ob, t1[:, :], t2[:, :], ALU.add, reads=[('tmp', 0), ('tmp', 1)], writes=[ores])
                else:
                    P.act(ob, ps[:, :], AF.Copy, reads=[pr], writes=[ores])
                P.dcp('sp', h2v[:, n, ts_], ob, [ores], [('outh', half, n)], ('X', n % 16))
                outs.append(('outh', half, n))
            P.lin(w_in2, D, 3072, lambda kc: Bb[:, kc, :], bres, ev4)
        return outs

    P.plan = True
    body()
    P.plan = False
    outs = body()
    return P.finish(outs)


def odd_rope_tables(pos):
    c, s = rope_tables(pos, 16, 500000.0)
    n = len(pos)
    cosF = np.ones((128, n), np.float32); sinF = np.zeros((128, n), np.float32)
    cosF[0:16] = c; cosF[16:32] = c
    sinF[0:16] = -s; sinF[16:32] = s
    perm = np.zeros((128, 128), np.float32)
    for m in range(16):
        perm[m + 16, m] = 1.0
        perm[m, m + 16] = 1.0
    return cosF, sinF, perm


def tm(a, c):
    return np.ascontiguousarray(a[c * T:(c + 1) * T].T)


def run_mid(x_tok, yf, yb, of, ob, gate, inp, W, stop=0):
    nc = build_mid(stop)
    in_maps = []
    lnp = lnp_layout(inp['ln_g'], inp['ln_b'], 0)
    bglu = np.ascontiguousarray(inp['s5_b_glu'][0].reshape(8, 128).T)
    for c in range(NCORES):
        cosF, sinF, perm = odd_rope_tables(np.arange(c * T, (c + 1) * T))
        in_maps.append({"xT": tm(x_tok, c), "yfT": tm(yf, c), "ybT": tm(yb, c), "ofT": tm(of, c), "obT": tm(ob, c), "gT": tm(gate, c),
                        "w_glu": W[('s5_w_glu', 0)], "b_glu": bglu, "w_out": W[('even_w_out', 0)], "w1": W[('mlp_w1', 0)],
                        "w2": W[('mlp_w2', 0)], "lnp": lnp, "w_in2": W[('odd_w_in', 0)], "cosF": cosF, "sinF": sinF, "perm": perm})
    res = run_bass_kernel_spmd(nc, in_maps, core_ids=list(range(NCORES)))
    x2 = np.concatenate([r["x2T"].T for r in res.results], 0)
    h2 = np.concatenate([r["h2T"].T for r in res.results], 0)
    return x2, h2


def build_inproj_odd(mode=0):
    P = TokProg()
    xT = P.dram_in("xT", [D, T])
    w_in2 = P.wparam("w_in2", D, 3072)
    cosF = P.dram_in("cosF", [128, T])
    sinF = P.dram_in("sinF", [128, T])
    perm = P.dram_in("perm", [128, 128])
    h2T = P.dram_out("h2T", [3072, T])
    Rb = P.sb("R", [128, 16, TH], F32)
    Bb = P.sb("Bb", [128, 16, TH], BF16)
    O = P.sb("O", [128, 24, TH], F32)
    cs = P.sb("cs", [128, 2, T], F32)
    perm32 = P.sb("perm32", [128, 128], F32)
    permb = P.sb("permb", [128, 128], BF16)
    hb16 = [P.sb(f"hb16_{i}", [128, TH], BF16) for i in range(2)]
    P.setup_common()
    rres = lambda kc: ('R', kc)
    bres = lambda kc: ('B', kc)
    xTv = xT.rearrange("(kc p) t -> p kc t", p=128)
    h2v = h2T.rearrange("(kc p) t -> p kc t", p=128)

    def body():
        P.start()
        P.dcp('sp', cs[:, 0, :], cosF[:, :], (), [('cs',)], 'cs0')
        P.dcp('sp', cs[:, 1, :], sinF[:, :], (), [('cs',)], 'cs1')
        P.dcp('sp', perm32[:, :], perm[:, :], (), [('perm32',)], 'perm32')
        P.act(permb[:, :], perm32[:, :], AF.Copy, reads=[('perm32',)], writes=[('permb',)])
        outs = []
        hbi = [0]
        for half in range(T // TH):
            t0 = half * TH
            ts_ = slice(t0, t0 + TH)
            for kc in range(16):
                P.dcp('sp', Rb[:, kc, :], xTv[:, kc, ts_], (), [rres(kc)], ('R', kc))
                P.act(Bb[:, kc, :], Rb[:, kc, :], AF.Copy, reads=[rres(kc)], writes=[bres(kc)])
            cosv = cs[:, 0, ts_]; sinv = cs[:, 1, ts_]

            def ev4(n, ps, pr):
                ores = ('O', n)
                ob = O[:, n, :]
                if n < 20 and mode == 0:
                    i = hbi[0] % 2; hbi[0] += 1
                    P.act(hb16[i][:, :], ps[:, :], AF.Copy, reads=[pr], writes=[('hb16', i), pr])
                    pp = P.pss[6 + i]; ppr = ('ps', 6 + i)
                    P.mm(pp[:, :], permb[:, :], hb16[i][:, :], True, True, reads=[('permb',), ('hb16', i)], writes=[ppr])
                    t1 = P.tmp[0]; t2 = P.tmp[1]
                    P.tt('dve', t1[:, :], ps[:, :], cosv, ALU.mult, reads=[pr, ('cs',)], writes=[('tmp', 0)])
                    P.tt('dve', t2[:, :], pp[:, :], sinv, ALU.mult, reads=[ppr, ('cs',)], writes=[('tmp', 1)])
                    P.tt('dve', ob, t1[:, :], t2[:, :], ALU.add, reads=[('tmp', 0), ('tmp', 1)], writes=[ores])
                else:
                    P.act(ob, ps[:, :], AF.Copy, reads=[pr], writes=[ores])
                P.dcp('sp', h2v[:, n, ts_], ob, [ores], [('outh', half, n)], ('O', n))
                outs.append(('outh', half, n))
            P.lin(w_in2, D, 3072, lambda kc: Bb[:, kc, :], bres, ev4)
        return outs

    P.plan = True
    body()
    P.plan = False
    outs = body()
    return P.finish(outs)


def run_inproj_odd(x_tok, w_in2, mode=0):
    nc = build_inproj_odd(mode)
    in_maps = []
    for c in range(NCORES):
        cosF, sinF, perm = odd_rope_tables(np.arange(c * T, (c + 1) * T))
        in_maps.append({"xT": tm(x_tok, c), "w_in2": w_in2, "cosF": cosF, "sinF": sinF, "perm": perm})
    res = run_bass_kernel_spmd(nc, in_maps, core_ids=list(range(NCORES)))
    return np.concatenate([r["h2T"].T for r in res.results], 0)


WNAMES = [('even_w_in', 0, D, 5120), ('even_w_out', 0, D, D), ('s5_w_glu', 0, 1024, 1024), ('mlp_w1', 0, D, 4 * D), ('mlp_w2', 0, 4 * D, D),
          ('odd_w_in', 0, D, 3072), ('odd_w_out', 0, D, D), ('mlp_w1', 1, D, 4 * D), ('mlp_w2', 1, 4 * D, D)]
CAST_ROWS = sum(k * n for (_, _, k, n) in WNAMES) // 2048 // NCORES


def build_cast():
    P = Prog()
    w = P.dram_in("w", [CAST_ROWS, 2048])
    o = P.dram_out("o", [CAST_ROWS, 2048], BF16)
    outs = []
    RB = 64
    for i in range(CAST_ROWS // RB):
        P.dma('pool', lambda e, i=i: e.dma_start(out=o[i * RB:(i + 1) * RB, :], in_=w[i * RB:(i + 1) * RB, :]),
              writes=[('o', i)], key=('o', i % 8))
        outs.append(('o', i))
    return P.finish(outs)


def cast_weights(inp):
    flat = np.concatenate([np.asarray(inp[nm][li], dtype=np.float32).reshape(-1) for (nm, li, _, _) in WNAMES]).reshape(NCORES, CAST_ROWS, 2048)
    nc = build_cast()
    res = run_bass_kernel_spmd(nc, [{"w": flat[c]} for c in range(NCORES)], core_ids=list(range(NCORES)))
    ob = np.concatenate([r["o"].reshape(-1) for r in res.results])
    out = {}
    off = 0
    for (nm, li, k, n) in WNAMES:
        out[(nm, li)] = to_panels(ob[off:off + k * n].reshape(k, n), k, n)
        off += k * n
    return out


def kernel(**inputs):
    inp = {k: np.asarray(v) for k, v in inputs.items()}
    x = np.ascontiguousarray(inp['x'][0], dtype=np.float32)
    W = cast_weights(inp)
    h = run_inproj_even(x, W[('even_w_in', 0)])
    u = np.ascontiguousarray(h[:, :1024])
    q = h[:, 1024:2048].reshape(L, 4, 256)
    k = h[:, 2048:3072].reshape(L, 4, 256)
    v = h[:, 3072:4096].reshape(L, 4, 256)
    gate = np.ascontiguousarray(h[:, 4096:])
    yf, yb = run_s5(u, inp)
    of, ob = run_ret(q, k, v, inp['ret_log_decay'][0])
    x2, h2 = run_mid(x, yf, yb, of.reshape(L, 1024), ob.reshape(L, 1024), gate, inp, W)
    q2 = h2[:, :2048].reshape(L, 16, 128)
    k2 = h2[:, 2048:2560].reshape(L, 4, 128)
    v2 = h2[:, 2560:].reshape(L, 4, 128)
    o = run_attn(q2, k2, v2, inp['attn_sink'][0])
    y = run_tail(x2, o, W[('odd_w_out', 0)], W[('mlp_w1', 1)], W[('mlp_w2', 1)], lnp_layout(inp['ln_g'], inp['ln_b'], 1))
    return y.reshape(1, L, D).astype(np.float32)
```

```python
import numpy as np
from contextlib import ExitStack
import concourse.bass as bass
import concourse.mybir as mybir
from concourse.bass_utils import run_bass_kernel_spmd

F32 = mybir.dt.float32
BF16 = mybir.dt.bfloat16
ALU = mybir.AluOpType
AF = mybir.ActivationFunctionType
ENGS = ['pe', 'dve', 'act', 'pool', 'sp']

NCORES = 8
D = 2048
L = 8192
T = L // NCORES
TH = 512
ALPHA = float(4 ** 0.25)
LN_EPS = 1e-5


class Prog:
    def __init__(self):
        self.nc = bass.Bass("TRN2", target_bir_lowering=False)
        self.q = {e: [] for e in ENGS}
        self.cnt = {}
        self.seen = {e: {} for e in ENGS}
        self.lastw = {}
        self.readers = {}
        self.semkeys = []
        self.es = ExitStack()
        self.plan = False
        self.panels = []
        self.pi = 0
        self.issued = 0
        self.psi = 0

    def sb(self, name, shape, dt):
        return self.es.enter_context(self.nc.sbuf_tensor(name, shape, dt))

    def ps(self, name, shape, dt=F32):
        return self.es.enter_context(self.nc.psum_tensor(name, shape, dt))

    def dram_in(self, name, shape, dt=F32):
        return self.nc.dram_tensor(name, list(shape), dt, kind="ExternalInput").ap()

    def dram_out(self, name, shape, dt=F32):
        return self.nc.dram_tensor(name, list(shape), dt, kind="ExternalOutput").ap()

    def _deps(self, eng, reads, writes):
        deps = []
        for r in reads:
            t = self.lastw.get(r)
            if t: deps.append(t)
            if isinstance(r, tuple) and isinstance(r[0], str) and r[0].startswith('ps'):
                deps.extend(x for x in self.readers.get(r, []) if x[2] != eng)
        for w in writes:
            t = self.lastw.get(w)
            if t: deps.append(t)
            deps.extend(self.readers.get(w, []))
        waits = []
        deps.sort(key=lambda t: (t[0], -t[1]))
        for (sk, val, deng) in deps:
            if deng == eng and eng == 'pe':
                continue
            if self.seen[eng].get(sk, 0) >= val:
                continue
            self.seen[eng][sk] = val
            waits.append((sk, val))
        return waits

    def _bump(self, sk, inc):
        if sk not in self.cnt:
            self.cnt[sk] = 0
            self.semkeys.append(sk)
        self.cnt[sk] += inc
        return self.cnt[sk]

    def _record(self, tok, reads, writes):
        for w in writes:
            self.lastw[w] = tok
            self.readers[w] = []
        for r in reads:
            self.readers.setdefault(r, []).append(tok)

    def emit(self, eng, fn, reads=(), writes=()):
        if self.plan: return
        waits = self._deps(eng, reads, writes)
        sk = 'e_' + eng
        v = self._bump(sk, 1)
        self.q[eng].append((waits, fn, sk, 1))
        self._record((sk, v, eng), reads, writes)

    def dma(self, eng, fn, reads=(), writes=(), key=None):
        if self.plan: return
        waits = self._deps(eng, reads, writes)
        sk = 'd_' + str(key)
        v = self._bump(sk, 16)
        self.q[eng].append((waits, fn, sk, 16))
        self._record((sk, v, 'dma'), reads, writes)

    def barrier(self):
        if self.plan: return
        toks = []
        for sk in self.semkeys:
            toks.append((sk, self.cnt[sk]))
        for e in ENGS:
            waits = []
            for (sk, v) in toks:
                if self.seen[e].get(sk, 0) >= v: continue
                self.seen[e][sk] = v
                waits.append((sk, v))
            if waits:
                self.q[e].append((waits, None, None, 0))

    def tt(self, eng, out, in0, in1, op, reads, writes):
        self.emit(eng, lambda e: e.tensor_tensor(out=out, in0=in0, in1=in1, op=op), reads, writes)

    def ts(self, eng, out, in0, s1, s2, op0, op1, reads, writes):
        if op1 is None:
            self.emit(eng, lambda e: e.tensor_scalar(out=out, in0=in0, scalar1=s1, scalar2=None, op0=op0), reads, writes)
        else:
            self.emit(eng, lambda e: e.tensor_scalar(out=out, in0=in0, scalar1=s1, scalar2=s2, op0=op0, op1=op1), reads, writes)

    def stt(self, out, in0, scalar, in1, op0, op1, reads, writes):
        self.emit('dve', lambda e: e.scalar_tensor_tensor(out=out, in0=in0, scalar=scalar, in1=in1, op0=op0, op1=op1), reads, writes)

    def act(self, out, in_, func, reads, writes, bias=None, scale=None):
        kw = {}
        if bias is not None: kw['bias'] = bias
        if scale is not None: kw['scale'] = scale
        self.emit('act', lambda e: e.activation(out=out, in_=in_, func=func, **kw), reads, writes)

    def mm(self, out, lhsT, rhs, start, stop, reads, writes):
        self.emit('pe', lambda e: e.matmul(out, lhsT=lhsT, rhs=rhs, start=start, stop=stop), reads, writes)

    def finish(self, final_res):
        waits = self._deps('sp', final_res, ())
        self.q['sp'].append((waits, None, None, 0))
        nc = self.nc
        sems = {}
        for i, sk in enumerate(self.semkeys):
            sems[sk] = self.es.enter_context(nc.semaphore(f"s{i}"))
        with nc.Block() as block:
            def run(e, engobj):
                for (waits, fn, sk, inc) in self.q[e]:
                    for (wk, wv) in waits:
                        engobj.wait_ge(sems[wk], wv)
                    if fn is not None:
                        fn(engobj).then_inc(sems[sk], inc)

            @block.tensor
            def _(e): run('pe', e)

            @block.vector
            def _(e): run('dve', e)

            @block.scalar
            def _(e): run('act', e)

            @block.gpsimd
            def _(e): run('pool', e)

            @block.sync
            def _(e): run('sp', e)
        self.es.close()
        return nc


def panel_dims(K, N):
    KC = K // 128
    CW = min(N, 8192 // KC, 512)
    return KC, CW, N // CW


def to_panels(wb, K, N):
    KC, CW, NP = panel_dims(K, N)
    return np.ascontiguousarray(wb.reshape(KC, 128, NP, CW).transpose(2, 1, 0, 3).reshape(NP, 128, KC * CW))


class TokProg(Prog):
    NSLOT = 2
    WELEMS = 8192

    def dcp(self, eng, out, in_, reads, writes, key):
        if eng == 'sp':
            eng = 'pool'
        self.dma(eng, lambda e: e.dma_start(out=out, in_=in_), reads=reads, writes=writes, key=key)

    def wparam(self, name, K, N):
        KC, CW, NP = panel_dims(K, N)
        return self.dram_in(name, [NP, 128, KC * CW], BF16)

    def setup_common(self):
        self.wbuf = [self.sb(f"wbuf{i}", [128, self.WELEMS], BF16) for i in range(self.NSLOT)]
        self.pss = [self.ps(f"ps{i}", [128, 512]) for i in range(8)]
        self.ones32 = self.sb("ones32", [128, 128], F32)
        self.sq = [self.sb(f"sq{i}", [128, TH], F32) for i in range(2)]
        self.tmp = [self.sb(f"tmp{i}", [128, TH], F32) for i in range(2)]
        self.mean = self.sb("mean", [128, TH], F32)
        self.m2 = self.sb("m2", [128, TH], F32)
        self.rstd = self.sb("rstd", [128, TH], F32)
        self.sqi = 0
        self.tmpi = 0

    def start(self):
        self.psi = 0
        self.emit('dve', lambda e: e.memset(self.ones32[:, :], 1.0), writes=[('ones32',)])

    def next_ps(self):
        i = self.psi % 4
        self.psi += 1
        return self.pss[i], ('ps', i)

    def issue_panel(self, i):
        (w_ap, KC, c0, CW) = self.panels[i]
        slot = i % self.NSLOT
        wv = self.wbuf[slot][:, 0:KC * CW]
        src = w_ap[c0 // CW, :, :]
        self.dma('sp', lambda e: e.dma_start(out=wv, in_=src), writes=[('wb', slot)], key=('wb', slot))

    def next_panel(self, w_ap, KC, c0, CW):
        if self.plan:
            self.panels.append((w_ap, KC, c0, CW))
            return None
        i = self.pi
        self.pi += 1
        assert self.panels[i][1:] == (KC, c0, CW)
        while self.issued < min(len(self.panels), i + self.NSLOT):
            self.issue_panel(self.issued)
            self.issued += 1
        slot = i % self.NSLOT
        return self.wbuf[slot][:, 0:KC * CW].rearrange("p (kc n) -> p kc n", kc=KC), slot

    def lin(self, w_ap, K, N, rhs_fn, rhs_res, evac_fn, tw=TH):
        KC, CW, _ = panel_dims(K, N)
        for c0 in range(0, N, CW):
            r = self.next_panel(w_ap, KC, c0, CW)
            if self.plan: continue
            wv, slot = r
            for nn in range(CW // 128):
                n = c0 // 128 + nn
                ps, pr = self.next_ps()
                for kc in range(KC):
                    self.mm(ps[:, 0:tw], wv[:, kc, nn * 128:(nn + 1) * 128], rhs_fn(kc), kc == 0, kc == KC - 1,
                            reads=[('wb', slot), rhs_res(kc)], writes=[pr])
                evac_fn(n, ps, pr)

    def stats(self, chunks, eps):
        nck = len(chunks)
        ps1, pr1 = self.pss[4], ('ps', 4)
        ps2, pr2 = self.pss[5], ('ps', 5)
        for kc, (ap, rs) in enumerate(chunks):
            self.mm(ps1[:, :], self.ones32[:, :], ap, kc == 0, kc == nck - 1, reads=[('ones32',), rs], writes=[pr1])
        for kc, (ap, rs) in enumerate(chunks):
            sq = self.sq[self.sqi % 2]; sqr = ('sq', self.sqi % 2); self.sqi += 1
            self.act(sq[:, :], ap, AF.Square, reads=[rs], writes=[sqr])
            self.mm(ps2[:, :], self.ones32[:, :], sq[:, :], kc == 0, kc == nck - 1, reads=[('ones32',), sqr], writes=[pr2])
        inv = 1.0 / (nck * 128)
        self.act(self.mean[:, :], ps1[:, :], AF.Copy, reads=[pr1], writes=[('mean',)], scale=inv)
        self.tt('dve', self.m2[:, :], self.mean[:, :], self.mean[:, :], ALU.mult, reads=[('mean',)], writes=[('m2',)])
        self.stt(self.m2[:, :], ps2[:, :], inv, self.m2[:, :], ALU.mult, ALU.subtract, reads=[pr2, ('m2',)], writes=[('m2',)])
        self.ts('dve', self.m2[:, :], self.m2[:, :], eps, None, ALU.add, None, reads=[('m2',)], writes=[('m2',)])
        self.act(self.m2[:, :], self.m2[:, :], AF.Sqrt, reads=[('m2',)], writes=[('m2',)])
        self.emit('dve', lambda e: e.reciprocal(out=self.rstd[:, :], in_=self.m2[:, :]), reads=[('m2',)], writes=[('rstd',)])

    def layernorm(self, S, sres, Dst, dres, Bd, bres, g_ap, b_ap, nck=16):
        self.stats([(S[:, kc, :], sres(kc)) for kc in range(nck)], LN_EPS)
        for kc in range(nck):
            tmp = self.tmp[self.tmpi % 2]; tr = ('tmp', self.tmpi % 2); self.tmpi += 1
            self.tt('dve', tmp[:, :], S[:, kc, :], self.mean[:, :], ALU.subtract, reads=[sres(kc), ('mean',)], writes=[tr])
            self.stt(tmp[:, :], tmp[:, :], g_ap[:, kc:kc + 1], self.rstd[:, :], ALU.mult, ALU.mult,
                     reads=[tr, ('rstd',), ('lnp',)], writes=[tr])
            self.act(Dst[:, kc, :], tmp[:, :], AF.Identity, reads=[tr, ('lnp',)], writes=[dres(kc)], bias=b_ap[:, kc:kc + 1])
            if Bd is not None:
                self.act(Bd[:, kc, :], Dst[:, kc, :], AF.Copy, reads=[dres(kc)], writes=[bres(kc)])


def build_tail(layer_has_inproj=False):
    P = TokProg()
    xT = P.dram_in("xT", [D, T])
    aT = P.dram_in("aT", [D, T])
    w_out = P.wparam("w_out", D, D)
    w1 = P.wparam("w1", D, 4 * D)
    w2 = P.wparam("w2", 4 * D, D)
    lnp = P.dram_in("lnp", [128, 4, 16])
    yT = P.dram_out("yT", [D, T])

    X = P.sb("X", [128, 16, TH], F32)
    Rb = P.sb("R", [128, 16, TH], F32)
    Bb = P.sb("Bb", [128, 16, TH], BF16)
    H = P.sb("H", [128, 64, TH], BF16)
    lnp_sb = P.sb("lnp_sb", [128, 4, 16], F32)
    P.setup_common()
    xres = lambda kc: ('X', kc)
    rres = lambda kc: ('R', kc)
    bres = lambda kc: ('B', kc)
    hres = lambda kc: ('H', kc)
    xTv = xT.rearrange("(kc p) t -> p kc t", p=128)
    aTv = aT.rearrange("(kc p) t -> p kc t", p=128)
    yTv = yT.rearrange("(kc p) t -> p kc t", p=128)

    def body():
        P.start()
        P.dcp('sp', lnp_sb[:, :, :], lnp[:, :, :], (), [('lnp',)], 'lnp')
        outs = []
        for half in range(T // TH):
            t0 = half * TH
            for kc in range(16):
                P.dcp('sp', X[:, kc, :], xTv[:, kc, t0:t0 + TH], (), [xres(kc)], ('X', kc))
                P.dcp('sp', Rb[:, kc, :], aTv[:, kc, t0:t0 + TH], (), [rres(kc)], ('R', kc))
                P.act(Bb[:, kc, :], Rb[:, kc, :], AF.Copy, reads=[rres(kc)], writes=[bres(kc)])

            def ev1(n, ps, pr):
                P.stt(Rb[:, n, :], X[:, n, :], ALPHA, ps[:, :], ALU.mult, ALU.add, reads=[xres(n), pr], writes=[rres(n)])
            P.lin(w_out, D, D, lambda kc: Bb[:, kc, :], bres, ev1)
            P.layernorm(Rb, rres, X, xres, Bb, bres, lnp_sb[:, 0, :], lnp_sb[:, 1, :])

            def ev2(n, ps, pr):
                P.act(H[:, n, :], ps[:, :], AF.Relu, reads=[pr], writes=[hres(n)])
                P.tt('dve', H[:, n, :], H[:, n, :], H[:, n, :], ALU.mult, reads=[hres(n)], writes=[hres(n)])
            P.lin(w1, D, 4 * D, lambda kc: Bb[:, kc, :], bres, ev2)

            def ev3(n, ps, pr):
                P.stt(X[:, n, :], X[:, n, :], ALPHA, ps[:, :], ALU.mult, ALU.add, reads=[xres(n), pr], writes=[xres(n)])
            P.lin(w2, 4 * D, D, lambda kc: H[:, kc, :], hres, ev3)
            P.layernorm(X, xres, Rb, rres, None, None, lnp_sb[:, 2, :], lnp_sb[:, 3, :])
            for kc in range(16):
                P.dcp('sp', yTv[:, kc, t0:t0 + TH], Rb[:, kc, :], [rres(kc)], [('out', half, kc)], ('R', kc))
                outs.append(('out', half, kc))
        return outs

    P.plan = True
    body()
    P.plan = False
    outs = body()
    return P.finish(outs)


def lnp_layout(ln_g, ln_b, layer):
    a = np.stack([ln_g[layer, 0], ln_b[layer, 0], ln_g[layer, 1], ln_b[layer, 1]], 0)
    return np.ascontiguousarray(a.reshape(4, 16, 128).transpose(2, 0, 1))


def run_tail(x_tok, a_tok, w_out, w1, w2, lnp):
    nc = build_tail()
    in_maps = []
    for c in range(NCORES):
        sl = slice(c * T, (c + 1) * T)
        in_maps.append({"xT": np.ascontiguousarray(x_tok[sl].T), "aT": np.ascontiguousarray(a_tok[sl].T),
                        "w_out": w_out, "w1": w1, "w2": w2, "lnp": lnp})
    res = run_bass_kernel_spmd(nc, in_maps, core_ids=list(range(NCORES)))
    return np.concatenate([r["yT"].T for r in res.results], 0)


def rope_tables(pos, half, theta):
    inv_freq = (np.float32(1.0) / np.power(np.float32(theta), (np.arange(half, dtype=np.float32) / np.float32(half)))).astype(np.float32)
    ang = (pos.astype(np.float32)[None, :] * inv_freq[:, None]).astype(np.float32)
    return np.cos(ang.astype(np.float64)).astype(np.float32), np.sin(ang.astype(np.float64)).astype(np.float32)


def build_inproj_even():
    P = TokProg()
    NOUT = 5120
    xT = P.dram_in("xT", [D, T])
    w_in = P.wparam("w_in", D, NOUT)
    cosT = P.dram_in("cosT", [128, T])
    sinT = P.dram_in("sinT", [128, T])
    hT = P.dram_out("hT", [NOUT, T])
    Rb = P.sb("R", [128, 16, TH], F32)
    Bb = P.sb("Bb", [128, 16, TH], BF16)
    O = P.sb("O", [128, 40, TH], F32)
    cs = P.sb("cs", [128, 2, T], F32)
    P.setup_common()
    rres = lambda kc: ('R', kc)
    bres = lambda kc: ('B', kc)
    xTv = xT.rearrange("(kc p) t -> p kc t", p=128)
    hTv = hT.rearrange("(kc p) t -> p kc t", p=128)

    def body():
        P.start()
        P.dcp('sp', cs[:, 0, :], cosT[:, :], (), [('cs',)], 'cs0')
        P.dcp('sp', cs[:, 1, :], sinT[:, :], (), [('cs',)], 'cs1')
        outs = []
        for half in range(T // TH):
            t0 = half * TH
            for kc in range(16):
                P.dcp('sp', Rb[:, kc, :], xTv[:, kc, t0:t0 + TH], (), [rres(kc)], ('R', kc))
                P.act(Bb[:, kc, :], Rb[:, kc, :], AF.Copy, reads=[rres(kc)], writes=[bres(kc)])
            pend = {}
            cosv = cs[:, 0, t0:t0 + TH]
            sinv = cs[:, 1, t0:t0 + TH]

            def ev(n, ps, pr):
                ores = ('O', n)
                if 8 <= n < 24:
                    if n % 2 == 0:
                        pend['a'] = (ps, pr)
                        return
                    pa, pra = pend['a']
                    t1 = P.tmp[0]; t2 = P.tmp[1]
                    P.tt('dve', t1[:, :], pa[:, :], cosv, ALU.mult, reads=[pra, ('cs',)], writes=[('tmp', 0)])
                    P.tt('dve', t2[:, :], ps[:, :], sinv, ALU.mult, reads=[pr, ('cs',)], writes=[('tmp', 1)])
                    P.tt('dve', O[:, n - 1, :], t1[:, :], t2[:, :], ALU.subtract, reads=[('tmp', 0), ('tmp', 1)], writes=[('O', n - 1)])
                    P.tt('dve', t1[:, :], ps[:, :], cosv, ALU.mult, reads=[pr, ('cs',)], writes=[('tmp', 0)])
                    P.tt('dve', t2[:, :], pa[:, :], sinv, ALU.mult, reads=[pra, ('cs',)], writes=[('tmp', 1)])
                    P.tt('dve', O[:, n, :], t1[:, :], t2[:, :], ALU.add, reads=[('tmp', 0), ('tmp', 1)], writes=[('O', n)])
                    for m in (n - 1, n):
                        P.dcp('sp', hTv[:, m, t0:t0 + TH], O[:, m, :], [('O', m)], [('out', half, m)], ('O', m))
                        outs.append(('out', half, m))
                else:
                    P.act(O[:, n, :], ps[:, :], AF.Copy, reads=[pr], writes=[ores])
                    P.dcp('sp', hTv[:, n, t0:t0 + TH], O[:, n, :], [ores], [('out', half, n)], ('O', n))
                    outs.append(('out', half, n))
            P.lin(w_in, D, NOUT, lambda kc: Bb[:, kc, :], bres, ev)
        return outs

    P.plan = True
    body()
    P.plan = False
    outs = body()
    return P.finish(outs)


def run_inproj_even(x_tok, w_in):
    nc = build_inproj_even()
    in_maps = []
    for c in range(NCORES):
        sl = slice(c * T, (c + 1) * T)
        cosT, sinT = rope_tables(np.arange(c * T, (c + 1) * T), 128, 10000.0)
        in_maps.append({"xT": np.ascontiguousarray(x_tok[sl].T), "w_in": w_in, "cosT": cosT, "sinT": sinT})
    res = run_bass_kernel_spmd(nc, in_maps, core_ids=list(range(NCORES)))
    return np.concatenate([r["hT"].T for r in res.results], 0)


NB5 = L // TH


def s5_disc(P, pre, F):
    names = ['step', 'lr', 'ex', 'mag', 'th', 's', 'c', 't1', 't2', 's2', 'c2', 'ar', 'ai', 'nr', 'den', 'zr', 'zi', 't3']
    tl = {n: P.sb(f"{pre}_{n}", [128, F], F32) for n in names}
    r = lambda n: (pre, n)
    A = lambda n: tl[n][:, :]

    def run(lamre, lamim, lstep):
        P.act(A('step'), lstep, AF.Exp, reads=[(pre, 'in')], writes=[r('step')])
        P.ts('dve', A('lr'), lamre, -1e-4, None, ALU.min, None, reads=[(pre, 'in')], writes=[r('lr')])
        P.tt('dve', A('ex'), A('lr'), A('step'), ALU.mult, reads=[r('lr'), r('step')], writes=[r('ex')])
        P.act(A('mag'), A('ex'), AF.Exp, reads=[r('ex')], writes=[r('mag')])
        P.tt('dve', A('th'), lamim, A('step'), ALU.mult, reads=[(pre, 'in'), r('step')], writes=[r('th')])
        P.act(A('s'), A('th'), AF.Sin, reads=[r('th')], writes=[r('s')], scale=1.0 / 16)
        P.act(A('c'), A('th'), AF.Sin, reads=[r('th'), ('halfpi',)], writes=[r('c')], scale=1.0 / 16, bias=P.halfpi[:, 0:1])
        cs, ss = 'c', 's'
        for it in range(4):
            cn, sn = ('c2', 's2') if it % 2 == 0 else ('c', 's')
            P.tt('dve', A('t1'), A(cs), A(cs), ALU.mult, reads=[r(cs)], writes=[r('t1')])
            P.tt('dve', A('t2'), A(ss), A(ss), ALU.mult, reads=[r(ss)], writes=[r('t2')])
            P.stt(A(sn), A(cs), 2.0, A(ss), ALU.mult, ALU.mult, reads=[r(cs), r(ss)], writes=[r(sn)])
            P.tt('dve', A(cn), A('t1'), A('t2'), ALU.subtract, reads=[r('t1'), r('t2')], writes=[r(cn)])
            cs, ss = cn, sn
        assert cs == 'c'
        P.tt('dve', A('ar'), A('mag'), A('c'), ALU.mult, reads=[r('mag'), r('c')], writes=[r('ar')])
        P.tt('dve', A('ai'), A('mag'), A('s'), ALU.mult, reads=[r('mag'), r('s')], writes=[r('ai')])
        P.ts('dve', A('nr'), A('ar'), -1.0, None, ALU.add, None, reads=[r('ar')], writes=[r('nr')])
        P.tt('dve', A('t1'), A('lr'), A('lr'), ALU.mult, reads=[r('lr')], writes=[r('t1')])
        P.tt('dve', A('t2'), lamim, lamim, ALU.mult, reads=[(pre, 'in')], writes=[r('t2')])
        P.tt('dve', A('den'), A('t1'), A('t2'), ALU.add, reads=[r('t1'), r('t2')], writes=[r('den')])
        P.emit('dve', lambda e: e.reciprocal(out=A('den'), in_=A('den')), reads=[r('den')], writes=[r('den')])
        P.tt('dve', A('t1'), A('nr'), A('lr'), ALU.mult, reads=[r('nr'), r('lr')], writes=[r('t1')])
        P.tt('dve', A('t2'), A('ai'), lamim, ALU.mult, reads=[r('ai'), (pre, 'in')], writes=[r('t2')])
        P.tt('dve', A('t3'), A('t1'), A('t2'), ALU.add, reads=[r('t1'), r('t2')], writes=[r('t3')])
        P.tt('dve', A('zr'), A('t3'), A('den'), ALU.mult, reads=[r('t3'), r('den')], writes=[r('zr')])
        P.tt('dve', A('t1'), A('ai'), A('lr'), ALU.mult, reads=[r('ai'), r('lr')], writes=[r('t1')])
        P.tt('dve', A('t2'), A('nr'), lamim, ALU.mult, reads=[r('nr'), (pre, 'in')], writes=[r('t2')])
        P.tt('dve', A('t3'), A('t1'), A('t2'), ALU.subtract, reads=[r('t1'), r('t2')], writes=[r('t3')])
        P.tt('dve', A('zi'), A('t3'), A('den'), ALU.mult, reads=[r('t3'), r('den')], writes=[r('zi')])
    return tl, run


def build_s5():
    P = Prog()
    uT = P.dram_in("uT", [2, 128, L])
    prow = P.dram_in("prow", [128, 3, 1024])
    pcol = P.dram_in("pcol", [128, 3, 8])
    BT = P.dram_in("BT", [128, 2, 1024])
    CT = P.dram_in("CT", [128, 2, 8, 128])
    dsk = P.dram_in("dsk", [128, 1])
    yT = P.dram_out("yT", [2, 128, L])

    u32 = [[P.sb(f"u32_{d}_{i}", [128, TH], F32) for i in range(2)] for d in range(2)]
    ub = [[P.sb(f"ub{d}_{i}", [128, TH], BF16) for i in range(2)] for d in range(2)]
    prow_sb = P.sb("prow_sb", [128, 3, 1024], F32)
    pcol_sb = P.sb("pcol_sb", [128, 3, 8], F32)
    BT_sb = P.sb("BT_sb", [128, 2, 1024], F32)
    CT_sb = P.sb("CT_sb", [128, 2, 8, 128], F32)
    dsk_sb = P.sb("dsk_sb", [128, 1], F32)
    P.halfpi = P.sb("halfpi", [128, 1], F32)
    Bbar = P.sb("Bbar", [128, 2, 1024], BF16)
    Cb = P.sb("Cb", [128, 2, 8, 128], BF16)
    tw = [P.sb(f"tw{i}", [128, 512], F32) for i in range(2)]
    tabc = P.sb("tabc", [128, 8, TH], F32)
    tabs = P.sb("tabs", [128, 8, TH], F32)
    magT = P.sb("magT", [128, 8, TH], F32)
    cur = [P.sb(f"cur{i}", [128, 2, 8], F32) for i in range(2)]
    curt = P.sb("curt", [128, 2, 8], F32)
    ttmp = P.sb("ttmp", [128, TH], F32)
    init = P.sb("init", [128, 2, 8], F32)
    ini_t = P.sb("ini_t", [128, 2, 8], F32)
    NW = 2
    m = [[P.sb(f"m{i}_{k}", [128, TH], F32) for k in range(4)] for i in range(NW)]
    v = [[P.sb(f"v{i}_{k}", [128, TH], F32) for k in range(2)] for i in range(NW)]
    pm = [[P.sb(f"pm{i}_{k}", [128, TH], F32) for k in range(4)] for i in range(NW)]
    hb = [[P.sb(f"hb{i}_{k}", [128, TH], BF16) for k in range(2)] for i in range(NW)]
    yo = [P.sb(f"yo{i}", [128, TH], F32) for i in range(2)]
    psb = [P.ps(f"psb{i}", [128, 512]) for i in range(4)]
    psy = [P.ps(f"psy{i}", [128, 512]) for i in range(2)]
    psw = [P.ps(f"psw{i}", [128, 512]) for i in range(2)]

    rowt, rowrun = s5_disc(P, 'row', 512)
    colt, colrun = s5_disc(P, 'col', 8)

    P.dcp = lambda eng, out, in_, reads, writes, key: P.dma(eng, lambda e: e.dma_start(out=out, in_=in_), reads=reads, writes=writes, key=key)
    P.dcp('sp', prow_sb[:, :, :], prow[:, :, :], (), [('row', 'in')], 'prow')
    P.dcp('sp', pcol_sb[:, :, :], pcol[:, :, :], (), [('col', 'in')], 'pcol')
    P.dcp('sp', BT_sb[:, :, :], BT[:, :, :], (), [('BT',)], 'BT')
    P.dcp('sp', CT_sb[:, :, :, :], CT[:, :, :, :], (), [('CT',)], 'CT')
    P.dcp('sp', dsk_sb[:, :], dsk[:, :], (), [('dsk',)], 'dsk')
    P.emit('dve', lambda e: e.memset(P.halfpi[:, :], float(np.pi / 2)), writes=[('halfpi',)])
    colrun(pcol_sb[:, 0, :], pcol_sb[:, 1, :], pcol_sb[:, 2, :])
    for d in range(2):
        ds_ = slice(d * 512, (d + 1) * 512)
        rowrun(prow_sb[:, 0, ds_], prow_sb[:, 1, ds_], prow_sb[:, 2, ds_])
        zr, zi = rowt['zr'][:, :], rowt['zi'][:, :]
        zres = [('row', 'zr'), ('row', 'zi'), ('BT',)]
        P.tt('dve', tw[0][:, :], zr, BT_sb[:, 0, ds_], ALU.mult, reads=zres, writes=[('tw', 0)])
        P.tt('dve', tw[1][:, :], zi, BT_sb[:, 1, ds_], ALU.mult, reads=zres, writes=[('tw', 1)])
        P.tt('dve', Bbar[:, 0, ds_], tw[0][:, :], tw[1][:, :], ALU.subtract, reads=[('tw', 0), ('tw', 1)], writes=[('Bbar',)])
        P.tt('dve', tw[0][:, :], zr, BT_sb[:, 1, ds_], ALU.mult, reads=zres + [('Bbar',)], writes=[('tw', 0)])
        P.tt('dve', tw[1][:, :], zi, BT_sb[:, 0, ds_], ALU.mult, reads=zres + [('Bbar',)], writes=[('tw', 1)])
        P.tt('dve', Bbar[:, 1, ds_], tw[0][:, :], tw[1][:, :], ALU.add, reads=[('tw', 0), ('tw', 1)], writes=[('Bbar',)])
    P.act(Cb[:, 0, :, :], CT_sb[:, 0, :, :], AF.Copy, reads=[('CT',)], writes=[('Cb',)])
    P.act(Cb[:, 1, :, :], CT_sb[:, 1, :, :], AF.Copy, reads=[('CT',)], writes=[('Cb',)], scale=-1.0)
    for dj in range(8):
        P.emit('dve', lambda e, dj=dj: e.memset(magT[:, dj, :], 1.0), writes=[('magT',)])
        P.ts('dve', magT[:, dj, :], magT[:, dj, :], colt['mag'][:, dj:dj + 1], None, ALU.mult, None,
             reads=[('magT',), ('col', 'mag')], writes=[('magT',)])
    P.emit('dve', lambda e: e.memset(tabc[:, :, 0:1], 1.0), writes=[('tab',)])
    P.emit('dve', lambda e: e.memset(tabs[:, :, 0:1], 0.0), writes=[('tab',)])
    P.emit('dve', lambda e: e.tensor_copy(out=cur[0][:, 0, :], in_=colt['c'][:, :]), reads=[('col', 'c')], writes=[('cur', 0)])
    P.emit('dve', lambda e: e.tensor_copy(out=cur[0][:, 1, :], in_=colt['s'][:, :]), reads=[('col', 's')], writes=[('cur', 0)])
    ci = 0
    for k in range(9):
        n = 1 << k
        cc = cur[ci]
        for dj in range(8):
            cs_c = cc[:, 0, dj:dj + 1]; cs_s = cc[:, 1, dj:dj + 1]
            rd = [('tab',), ('cur', ci)]
            P.ts('dve', ttmp[:, 0:n], tabs[:, dj, 0:n], cs_s, None, ALU.mult, None, reads=rd, writes=[('ttmp',)])
            P.stt(tabc[:, dj, n:2 * n], tabc[:, dj, 0:n], cs_c, ttmp[:, 0:n], ALU.mult, ALU.subtract, reads=rd + [('ttmp',)], writes=[('tab',)])
            P.ts('dve', ttmp[:, 0:n], tabs[:, dj, 0:n], cs_c, None, ALU.mult, None, reads=rd, writes=[('ttmp',)])
            P.stt(tabs[:, dj, n:2 * n], tabc[:, dj, 0:n], cs_s, ttmp[:, 0:n], ALU.mult, ALU.add, reads=rd + [('ttmp',)], writes=[('tab',)])
        cn = cur[1 - ci]
        P.tt('dve', curt[:, 0, :], cc[:, 0, :], cc[:, 0, :], ALU.mult, reads=[('cur', ci)], writes=[('curt',)])
        P.tt('dve', curt[:, 1, :], cc[:, 1, :], cc[:, 1, :], ALU.mult, reads=[('cur', ci)], writes=[('curt',)])
        P.tt('dve', cn[:, 0, :], curt[:, 0, :], curt[:, 1, :], ALU.subtract, reads=[('curt',)], writes=[('cur', 1 - ci)])
        P.stt(cn[:, 1, :], cc[:, 0, :], 2.0, cc[:, 1, :], ALU.mult, ALU.mult, reads=[('cur', ci)], writes=[('cur', 1 - ci)])
        ci = 1 - ci
    Rc = cur[ci]
    Rres = ('cur', ci)
    P.emit('dve', lambda e: e.memset(init[:, :, :], 0.0), writes=[('init',)])
    Rneg = P.sb("Rneg", [128, 8], F32)
    P.ts('dve', Rneg[:, :], Rc[:, 1, :], -1.0, None, ALU.mult, None, reads=[Rres], writes=[('Rneg',)])

    outs = []
    st = {'wi': 0, 'pbi': 0}

    def stage0(b, d, j):
        t0 = b * TH
        ui = b % 2
        if d == 0 and j == 0:
            for dd in range(2):
                P.dcp('sp', u32[dd][ui][:, :], uT[dd, :, t0:t0 + TH], (), [('u32', dd, ui)], ('u32', dd, ui))
                P.act(ub[dd][ui][:, :], u32[dd][ui][:, :], AF.Copy, reads=[('u32', dd, ui)], writes=[('ub', dd, ui)])
        dj = d * 4 + j
        i = st['wi'] % NW; st['wi'] += 1
        pbi = st['pbi']; st['pbi'] += 1
        pr_ = psb[(pbi * 2) % 4]; pi_ = psb[(pbi * 2 + 1) % 4]
        prr = ('psb', (pbi * 2) % 4); pir = ('psb', (pbi * 2 + 1) % 4)
        col0 = d * 512 + j * 128
        P.mm(pr_[:, :], Bbar[:, 0, col0:col0 + 128], ub[d][ui][:, :], True, True, reads=[('Bbar',), ('ub', d, ui)], writes=[prr])
        P.mm(pi_[:, :], Bbar[:, 1, col0:col0 + 128], ub[d][ui][:, :], True, True, reads=[('Bbar',), ('ub', d, ui)], writes=[pir])
        return (b, d, j, i, pr_, pi_, prr, pir)

    def stage1(ctx0):
        (b, d, j, i, pr_, pi_, prr, pir) = ctx0
        dj = d * 4 + j
        tc = tabc[:, dj, :]; tsn = tabs[:, dj, :]
        mr = lambda k: ('m', i, k)
        P.tt('dve', m[i][0][:, :], pr_[:, :], tc, ALU.mult, reads=[prr, ('tab',)], writes=[mr(0)])
        P.tt('dve', m[i][1][:, :], pi_[:, :], tsn, ALU.mult, reads=[pir, ('tab',)], writes=[mr(1)])
        P.tt('dve', m[i][2][:, :], pi_[:, :], tc, ALU.mult, reads=[pir, ('tab',)], writes=[mr(2)])
        P.tt('dve', m[i][3][:, :], pr_[:, :], tsn, ALU.mult, reads=[prr, ('tab',)], writes=[mr(3)])
        P.tt('pool', v[i][0][:, :], m[i][0][:, :], m[i][1][:, :], ALU.add, reads=[mr(0), mr(1)], writes=[('v', i, 0)])
        P.tt('pool', v[i][1][:, :], m[i][2][:, :], m[i][3][:, :], ALU.subtract, reads=[mr(2), mr(3)], writes=[('v', i, 1)])
        return (b, d, j, i)

    def stage2(ctx):
        (b, d, j, i) = ctx
        t0 = b * TH
        ui = b % 2
        dj = d * 4 + j
        py = psy[d]; pyr = ('psy', d)
        tc = tabc[:, dj, :]; tsn = tabs[:, dj, :]
        for k in range(2):
            P.emit('dve', lambda e, i=i, k=k, dj=dj: e.tensor_tensor_scan(
                out=psw[k][:, :], data0=magT[:, dj, :], data1=v[i][k][:, :], initial=init[:, k, dj:dj + 1],
                op0=ALU.mult, op1=ALU.add), reads=[('magT',), ('v', i, k), ('init',)], writes=[('psw', k)])
        pmr = lambda k: ('pm', i, k)
        P.tt('dve', pm[i][0][:, :], psw[0][:, :], tc, ALU.mult, reads=[('psw', 0), ('tab',)], writes=[pmr(0)])
        P.tt('dve', pm[i][1][:, :], psw[1][:, :], tsn, ALU.mult, reads=[('psw', 1), ('tab',)], writes=[pmr(1)])
        P.tt('dve', pm[i][2][:, :], psw[0][:, :], tsn, ALU.mult, reads=[('psw', 0), ('tab',)], writes=[pmr(2)])
        P.tt('dve', pm[i][3][:, :], psw[1][:, :], tc, ALU.mult, reads=[('psw', 1), ('tab',)], writes=[pmr(3)])
        wl_r = psw[0][:, TH - 1:TH]; wl_i = psw[1][:, TH - 1:TH]
        wres = [('psw', 0), ('psw', 1), Rres, ('Rneg',)]
        P.act(ini_t[:, 0, dj:dj + 1], wl_i, AF.Copy, reads=wres, writes=[('ini_t',)], scale=Rneg[:, dj:dj + 1])
        P.act(ini_t[:, 1, dj:dj + 1], wl_i, AF.Copy, reads=wres, writes=[('ini_t',)], scale=Rc[:, 0, dj:dj + 1])
        P.act(init[:, 0, dj:dj + 1], wl_r, AF.Identity, reads=wres + [('ini_t',)], writes=[('init',)],
              scale=Rc[:, 0, dj:dj + 1], bias=ini_t[:, 0, dj:dj + 1])
        P.act(init[:, 1, dj:dj + 1], wl_r, AF.Identity, reads=wres + [('ini_t',)], writes=[('init',)],
              scale=Rc[:, 1, dj:dj + 1], bias=ini_t[:, 1, dj:dj + 1])
        P.tt('pool', hb[i][0][:, :], pm[i][0][:, :], pm[i][1][:, :], ALU.subtract, reads=[pmr(0), pmr(1)], writes=[('hb', i, 0)])
        P.tt('pool', hb[i][1][:, :], pm[i][2][:, :], pm[i][3][:, :], ALU.add, reads=[pmr(2), pmr(3)], writes=[('hb', i, 1)])
        P.mm(py[:, :], Cb[:, 0, dj, :], hb[i][0][:, :], j == 0, False, reads=[('Cb',), ('hb', i, 0)], writes=[pyr])
        P.mm(py[:, :], Cb[:, 1, dj, :], hb[i][1][:, :], False, j == 3, reads=[('Cb',), ('hb', i, 1)], writes=[pyr])
        if j == 3:
            yr = ('yo', d)
            if d == 0:
                P.stt(yo[d][:, :], u32[0][ui][:, :], dsk_sb[:, 0:1], py[:, :], ALU.mult, ALU.add,
                      reads=[('u32', 0, ui), ('dsk',), pyr], writes=[yr])
            else:
                P.act(yo[d][:, :], py[:, :], AF.Copy, reads=[pyr], writes=[yr])
            P.dcp('sp', yT[d, :, t0:t0 + TH], yo[d][:, :], [yr], [('out', b, d)], ('yo', d))
            outs.append(('out', b, d))

    its = [(b, d, j) for b in range(NB5) for d in range(2) for j in range(4)]
    NI = len(its)
    c0s = {}
    c1s = {}
    for n in range(NI + 2):
        if n < NI:
            c0s[n] = stage0(*its[n])
        if 0 <= n - 1 < NI:
            c1s[n - 1] = stage1(c0s.pop(n - 1))
        if 0 <= n - 2 < NI:
            stage2(c1s.pop(n - 2))
    return P.finish(outs)


def s5_host_layout(inp, c):
    g0 = 8 * c
    lamre = inp['s5_lambda_re'][0][:, g0:g0 + 8, :]
    lamim = inp['s5_lambda_im'][0][:, g0:g0 + 8, :]
    lstep = np.broadcast_to(inp['s5_log_step'][0][:, g0:g0 + 8, None], (2, 8, 64))
    st = np.stack([lamre, lamim, lstep], 0).reshape(3, 2, 512)
    prow = np.ascontiguousarray(np.broadcast_to(st.reshape(1, 3, 1024), (128, 3, 1024))).astype(np.float32)
    pcol = np.ascontiguousarray(st.reshape(3, 2, 4, 128).transpose(3, 0, 1, 2).reshape(128, 3, 8)).astype(np.float32)
    BT = np.zeros((128, 2, 2, 512), np.float32)
    CT = np.zeros((128, 2, 2, 4, 128), np.float32)
    for ri, (bk, ck) in enumerate([('s5_b_re', 's5_c_re'), ('s5_b_im', 's5_c_im')]):
        Bm = inp[bk][0][:, g0:g0 + 8]
        Cm = inp[ck][0][:, g0:g0 + 8]
        for d in range(2):
            for g in range(8):
                BT[g * 16:(g + 1) * 16, ri, d, g * 64:(g + 1) * 64] = Bm[d, g].T
                j, gg = g // 2, g % 2
                CT[gg * 64:(gg + 1) * 64, ri, d, j, g * 16:(g + 1) * 16] = Cm[d, g].T
    dsk = inp['s5_d'][0][g0:g0 + 8].reshape(128, 1).astype(np.float32)
    return {"prow": prow, "pcol": pcol, "BT": BT.reshape(128, 2, 1024), "CT": CT.reshape(128, 2, 8, 128), "dsk": np.ascontiguousarray(dsk)}


def run_s5(u_tok, inp):
    nc = build_s5()
    in_maps = []
    for c in range(NCORES):
        uc = u_tok[:, c * 128:(c + 1) * 128].T
        m = s5_host_layout(inp, c)
        m["uT"] = np.ascontiguousarray(np.stack([uc, uc[:, ::-1]], 0))
        in_maps.append(m)
    res = run_bass_kernel_spmd(nc, in_maps, core_ids=list(range(NCORES)))
    yf = np.concatenate([r["yT"][0].T for r in res.results], 1)
    yb = np.concatenate([r["yT"][1][:, ::-1].T for r in res.results], 1)
    return yf, yb


def build_ret():
    P = Prog()
    NCH = L // 128
    qT = P.dram_in("qT", [2, 128, L])
    kT = P.dram_in("kT", [2, 128, L])
    ktok = P.dram_in("ktok", [L, 256])
    vtok = P.dram_in("vtok", [L, 256])
    lg = P.dram_in("lg", [128, 1])
    maskT = P.dram_in("maskT", [128, 128])
    irow = P.dram_in("irow", [128, TH])
    icol = P.dram_in("icol", [128, 1])
    o = P.dram_out("o", [L, 256])
    P.dcp = lambda eng, out, in_, reads, writes, key: P.dma(eng, lambda e: e.dma_start(out=out, in_=in_), reads=reads, writes=writes, key=key)

    lg_sb = P.sb("lg_sb", [128, 1], F32)
    lgn = P.sb("lgn", [128, 1], F32)
    lgp = P.sb("lgp", [128, 1], F32)
    mask_sb = P.sb("mask_sb", [128, 128], F32)
    irow_sb = P.sb("irow_sb", [128, TH], F32)
    icol_sb = P.sb("icol_sb", [128, 1], F32)
    gq = P.sb("gq", [128, TH], F32)
    gk = P.sb("gk", [128, TH], F32)
    gkc = P.sb("gkc", [128, 1], F32)
    gC = P.sb("gC", [128, 1], F32)
    c128 = P.sb("c128", [128, 1], F32)
    q32 = [P.sb(f"q32_{i}", [128, 2, TH], F32) for i in range(2)]
    k32 = [P.sb(f"k32_{i}", [128, 2, TH], F32) for i in range(2)]
    qb = [P.sb(f"qb_{i}", [128, 2, TH], BF16) for i in range(2)]
    kb = [P.sb(f"kb_{i}", [128, 2, TH], BF16) for i in range(2)]
    kt32 = [P.sb(f"kt32_{i}", [128, 4, 256], F32) for i in range(2)]
    vt32 = [P.sb(f"vt32_{i}", [128, 4, 256], F32) for i in range(2)]
    ktb = [P.sb(f"ktb_{i}", [128, 4, 256], BF16) for i in range(2)]
    vtb = [P.sb(f"vtb_{i}", [128, 4, 256], BF16) for i in range(2)]
    Sm = [P.sb(f"Sm{i}", [128, 128], BF16) for i in range(2)]
    Tst = P.sb("Tst", [128, 2, 256], F32)
    Sbf = P.sb("Sbf", [128, 2, 256], BF16)
    osb = [P.sb(f"osb{i}", [128, 256], F32) for i in range(2)]
    psS = [P.ps(f"psS{i}", [128, 512]) for i in range(2)]
    psO = [P.ps(f"psO{i}", [128, 512]) for i in range(2)]
    psK = [P.ps(f"psK{i}", [128, 512]) for i in range(2)]

    P.dcp('sp', lg_sb[:, :], lg[:, :], (), [('lg',)], 'lg')
    P.dcp('sp', mask_sb[:, :], maskT[:, :], (), [('mask',)], 'mask')
    P.dcp('sp', irow_sb[:, :], irow[:, :], (), [('irow',)], 'irow')
    P.dcp('sp', icol_sb[:, :], icol[:, :], (), [('icol',)], 'icol')
    P.act(lgp[:, :], lg_sb[:, :], AF.Abs, reads=[('lg',)], writes=[('lgp',)])
    P.ts('dve', lgn[:, :], lgp[:, :], -1.0, None, ALU.mult, None, reads=[('lgp',)], writes=[('lgn',)])
    P.act(gq[:, :], irow_sb[:, :], AF.Exp, reads=[('irow',), ('lgn',)], writes=[('gq',)], scale=lgn[:, 0:1])
    P.act(gk[:, :], irow_sb[:, :], AF.Exp, reads=[('irow',), ('lgp',)], writes=[('gk',)], scale=lgp[:, 0:1])
    P.ts('dve', gk[:, :], gk[:, :], 1.0 / 16, None, ALU.mult, None, reads=[('gk',)], writes=[('gk',)])
    P.act(gkc[:, :], icol_sb[:, :], AF.Exp, reads=[('icol',), ('lgp',)], writes=[('gkc',)], scale=lgp[:, 0:1])
    P.ts('dve', gkc[:, :], gkc[:, :], 1.0 / 16, None, ALU.mult, None, reads=[('gkc',)], writes=[('gkc',)])
    P.emit('dve', lambda e: e.memset(c128[:, :], 128.0), writes=[('c128',)])
    P.act(gC[:, :], c128[:, :], AF.Exp, reads=[('c128',), ('lgn',)], writes=[('gC',)], scale=lgn[:, 0:1])
    P.emit('dve', lambda e: e.memset(Tst[:, :, :], 0.0), writes=[('Tst', 0), ('Tst', 1)])

    outs = []
    for sbk in range(L // TH):
        t0 = sbk * TH
        bi = sbk % 2
        for kc in range(2):
            P.dcp('sp', q32[bi][:, kc, :], qT[kc, :, t0:t0 + TH], (), [('q32', bi, kc)], ('q32', bi, kc))
            P.dcp('sp', k32[bi][:, kc, :], kT[kc, :, t0:t0 + TH], (), [('k32', bi, kc)], ('k32', bi, kc))
            P.tt('dve', qb[bi][:, kc, :], q32[bi][:, kc, :], gq[:, :], ALU.mult, reads=[('q32', bi, kc), ('gq',)], writes=[('qb', bi, kc)])
            P.tt('dve', kb[bi][:, kc, :], k32[bi][:, kc, :], gk[:, :], ALU.mult, reads=[('k32', bi, kc), ('gk',)], writes=[('kb', bi, kc)])
        P.dcp('sp', kt32[bi][:, :, :], ktok[t0:t0 + TH, :].rearrange("(c p) e -> p c e", p=128), (), [('kt32', bi)], ('kt32', bi))
        P.dcp('sp', vt32[bi][:, :, :], vtok[t0:t0 + TH, :].rearrange("(c p) e -> p c e", p=128), (), [('vt32', bi)], ('vt32', bi))
        P.ts('dve', ktb[bi][:, :, :], kt32[bi][:, :, :], gkc[:, 0:1], None, ALU.mult, None, reads=[('kt32', bi), ('gkc',)], writes=[('ktb', bi)])
        P.act(vtb[bi][:, :, :], vt32[bi][:, :, :], AF.Copy, reads=[('vt32', bi)], writes=[('vtb', bi)])
        for cc in range(4):
            n = sbk * 4 + cc
            cs = slice(cc * 128, (cc + 1) * 128)
            pS = psS[n % 2]; pSr = ('psS', n % 2)
            pO = psO[n % 2]; pOr = ('psO', n % 2)
            for kc in range(2):
                P.mm(pS[:, 0:128], kb[bi][:, kc, cs], qb[bi][:, kc, cs], kc == 0, kc == 1,
                     reads=[('kb', bi, kc), ('qb', bi, kc)], writes=[pSr])
            sm = Sm[n % 2]; smr = ('Sm', n % 2)
            P.tt('dve', sm[:, :], pS[:, 0:128], mask_sb[:, :], ALU.mult, reads=[pSr, ('mask',)], writes=[smr])
            P.mm(pO[:, 0:256], sm[:, :], vtb[bi][:, cc, :], True, n == 0, reads=[smr, ('vtb', bi)], writes=[pOr])
            if n > 0:
                for kc in range(2):
                    P.mm(pO[:, 0:256], qb[bi][:, kc, cs], Sbf[:, kc, :], False, kc == 1,
                         reads=[('qb', bi, kc), ('Sbf', kc)], writes=[pOr])
            ob = osb[n % 2]; obr = ('osb', n % 2)
            P.act(ob[:, :], pO[:, 0:256], AF.Copy, reads=[pOr], writes=[obr])
            P.dcp('sp', o[n * 128:(n + 1) * 128, :], ob[:, :], [obr], [('out', n)], ('osb', n % 2))
            outs.append(('out', n))
            if n < NCH - 1:
                for kc in range(2):
                    pK = psK[kc]; pKr = ('psK', kc)
                    P.mm(pK[:, 0:256], ktb[bi][:, cc, kc * 128:(kc + 1) * 128], vtb[bi][:, cc, :], True, True,
                         reads=[('ktb', bi), ('vtb', bi)], writes=[pKr])
                    P.stt(Tst[:, kc, :], Tst[:, kc, :], gC[:, 0:1], pK[:, 0:256], ALU.mult, ALU.add,
                          reads=[('Tst', kc), ('gC',), pKr], writes=[('Tst', kc)])
                    P.act(Sbf[:, kc, :], Tst[:, kc, :], AF.Copy, reads=[('Tst', kc), ('gC',)], writes=[('Sbf', kc)], scale=gC[:, 0:1])
    return P.finish(outs)


def run_ret(q_rot, k_rot, v, ret_log_decay):
    nc = build_ret()
    in_maps = []
    ii = np.arange(128)
    irow = np.ascontiguousarray(np.broadcast_to(((np.arange(TH) % 128) + 1).astype(np.float32)[None, :], (128, TH)))
    icol = (ii + 1).astype(np.float32).reshape(128, 1)
    for c in range(NCORES):
        hh, d = c // 2, c % 2
        qq, kk, vv = q_rot[:, hh], k_rot[:, hh], v[:, hh]
        if d == 1:
            qq, kk, vv = qq[::-1], kk[::-1], vv[::-1]
        mk = (ii[None, :] >= ii[:, None]) if d == 0 else (ii[None, :] > ii[:, None])
        in_maps.append({"qT": np.ascontiguousarray(qq.T.reshape(2, 128, L)), "kT": np.ascontiguousarray(kk.T.reshape(2, 128, L)),
                        "ktok": np.ascontiguousarray(kk), "vtok": np.ascontiguousarray(vv),
                        "lg": np.full((128, 1), ret_log_decay[d, hh], np.float32), "maskT": mk.astype(np.float32),
                        "irow": irow, "icol": icol})
    res = run_bass_kernel_spmd(nc, in_maps, core_ids=list(range(NCORES)))
    o_f = np.stack([res.results[2 * hh]["o"] for hh in range(4)], 1)
    o_b = np.stack([res.results[2 * hh + 1]["o"][::-1] for hh in range(4)], 1)
    return o_f, o_b


def build_attn():
    P = Prog()
    TK = T + 256
    NBLK = T // 128
    qT = P.dram_in("qT", [16, 128, T])
    kT = P.dram_in("kT", [4, 128, TK])
    vtok = P.dram_in("vtok", [TK, 512])
    sink = P.dram_in("sink", [128, 16])
    mbias = P.dram_in("mbias", [128, 3, 384])
    ident = P.dram_in("ident", [128, 128])
    oT = P.dram_out("oT", [16, 128, T])
    P.dcp = lambda eng, out, in_, reads, writes, key: P.dma(eng, lambda e: e.dma_start(out=out, in_=in_), reads=reads, writes=writes, key=key)

    qb = P.sb("qb", [128, 16, T], BF16)
    kb = P.sb("kb", [128, 4, TK], BF16)
    vb = P.sb("vb", [128, TK // 128, 512], BF16)
    sink_sb = P.sb("sink_sb", [128, 16], F32)
    mb_sb = P.sb("mb_sb", [128, 3, 384], F32)
    id32 = P.sb("id32", [128, 128], F32)
    idb = P.sb("idb", [128, 128], BF16)
    osb = P.sb("osb", [128, 16, T], F32)
    NR = 4
    Sb = [P.sb(f"Sb{i}", [128, 384], F32) for i in range(NR)]
    Pe = [P.sb(f"Pe{i}", [128, 384], BF16) for i in range(NR)]
    Pn = [P.sb(f"Pn{i}", [128, 384], BF16) for i in range(NR)]
    PT = [P.sb(f"PT{i}", [128, 3, 128], BF16) for i in range(NR)]
    sm = [P.sb(f"sm{i}", [128, 8], F32) for i in range(NR)]
    psS = [P.ps(f"psS{i}", [128, 512]) for i in range(2)]
    psT = [P.ps(f"psT{i}", [128, 3, 128], BF16) for i in range(2)]
    psO = [P.ps(f"psO{i}", [128, 512]) for i in range(2)]

    for h in range(16):
        P.dcp('pool', qb[:, h, :], qT[h, :, :], (), [('qb', h)], ('qb', h))
    for h in range(4):
        P.dcp('pool', kb[:, h, :], kT[h, :, :], (), [('kb', h)], ('kb', h))
    P.dcp('pool', vb[:, :, :], vtok.rearrange("(c p) e -> p c e", p=128), (), [('vb',)], 'vb')
    P.dcp('sp', sink_sb[:, :], sink[:, :], (), [('sink',)], 'sink')
    P.dcp('sp', mb_sb[:, :, :], mbias[:, :, :], (), [('mb',)], 'mb')
    P.dcp('sp', id32[:, :], ident[:, :], (), [('id32',)], 'id32')
    P.act(idb[:, :], id32[:, :], AF.Copy, reads=[('id32',)], writes=[('idb',)])
    SCALE = float(128 ** -0.5)
    items = [(c, hq) for c in range(NBLK) for hq in range(16)]

    def phA(n):
        c, hq = items[n]
        mi = 0 if c == 0 else (2 if c == NBLK - 1 else 1)
        qs = slice(c * 128, (c + 1) * 128)
        kv = hq // 4
        i = n % NR
        pS = psS[n % 2]; pSr = ('psS', n % 2)
        P.mm(pS[:, 0:384], qb[:, hq, qs], kb[:, kv, c * 128:c * 128 + 384], True, True,
             reads=[('qb', hq), ('kb', kv)], writes=[pSr])
        P.stt(Sb[i][:, :], pS[:, 0:384], SCALE, mb_sb[:, mi, :], ALU.mult, ALU.add, reads=[pSr, ('mb',)], writes=[('Sb', i)])
        s_ = sm[i]; smr = ('sm', i)
        P.emit('dve', lambda e, i=i, s_=s_: e.reduce_max(out=s_[:, 0:1], in_=Sb[i][:, :], axis=mybir.AxisListType.X),
               reads=[('Sb', i)], writes=[smr])
        P.tt('dve', s_[:, 1:2], s_[:, 0:1], sink_sb[:, hq:hq + 1], ALU.max, reads=[smr, ('sink',)], writes=[smr])
        P.ts('dve', s_[:, 2:3], s_[:, 1:2], -1.0, None, ALU.mult, None, reads=[smr], writes=[smr])
        P.emit('act', lambda e, i=i, s_=s_: e.activation(out=Pe[i][:, :], in_=Sb[i][:, :], func=AF.Exp, bias=s_[:, 2:3],
                                                        accum_out=s_[:, 3:4]), reads=[('Sb', i), smr], writes=[('Pe', i), smr])
        P.act(s_[:, 4:5], sink_sb[:, hq:hq + 1], AF.Exp, reads=[('sink',), smr], writes=[smr], bias=s_[:, 2:3])

    def phB(n):
        i = n % NR
        s_ = sm[i]; smr = ('sm', i)
        P.tt('dve', s_[:, 5:6], s_[:, 3:4], s_[:, 4:5], ALU.add, reads=[smr], writes=[smr])
        P.emit('dve', lambda e, s_=s_: e.reciprocal(out=s_[:, 6:7], in_=s_[:, 5:6]), reads=[smr], writes=[smr])
        P.ts('dve', Pn[i][:, :], Pe[i][:, :], s_[:, 6:7], None, ALU.mult, None, reads=[('Pe', i), smr], writes=[('Pn', i)])
        pT = psT[n % 2]; pTr = ('psT', n % 2)
        for kk in range(3):
            P.emit('pe', lambda e, i=i, kk=kk, pT=pT: e.transpose(out=pT[:, kk, :], in_=Pn[i][:, kk * 128:(kk + 1) * 128], identity=idb[:, :]),
                   reads=[('Pn', i), ('idb',)], writes=[pTr])
        P.act(PT[i][:, :, :], pT[:, :, :], AF.Copy, reads=[pTr], writes=[('PT', i)])

    def phC(n):
        c, hq = items[n]
        qs = slice(c * 128, (c + 1) * 128)
        kv = hq // 4
        i = n % NR
        pO = psO[n % 2]; pOr = ('psO', n % 2)
        for kk in range(3):
            P.mm(pO[:, 0:128], vb[:, c + kk, kv * 128:(kv + 1) * 128], PT[i][:, kk, :], kk == 0, kk == 2,
                 reads=[('vb',), ('PT', i)], writes=[pOr])
        P.emit('dve', lambda e, hq=hq, qs=qs, pO=pO: e.tensor_copy(out=osb[:, hq, qs], in_=pO[:, 0:128]), reads=[pOr], writes=[('osb', hq)])

    NI = len(items)
    for n in range(NI + 2):
        if n < NI: phA(n)
        if 0 <= n - 1 < NI: phB(n - 1)
        if 0 <= n - 2 < NI: phC(n - 2)
    outs = []
    for hq in range(16):
        P.dcp('sp', oT[hq, :, :], osb[:, hq, :], [('osb', hq)], [('out', hq)], ('osb', hq))
        outs.append(('out', hq))
    return P.finish(outs)


def attn_masks():
    t = np.arange(128)[:, None]; s = np.arange(384)[None, :]
    inwin = np.abs(t - s + 128) <= 128
    NEG = np.float32(-30000.0)
    mid = np.where(inwin, 0.0, NEG).astype(np.float32)
    first = np.where(inwin & (s >= 128), 0.0, NEG).astype(np.float32)
    last = np.where(inwin & (s < 256), 0.0, NEG).astype(np.float32)
    return mid, first, last


def run_attn(q_rot, k_rot, v, sink):
    nc = build_attn()
    mid, first, last = attn_masks()
    kp = np.zeros((L + 256, 4, 128), np.float32); kp[128:L + 128] = k_rot
    vp = np.zeros((L + 256, 4, 128), np.float32); vp[128:L + 128] = v
    in_maps = []
    for c in range(NCORES):
        sl = slice(c * T, (c + 1) * T)
        hs = slice(c * T, (c + 1) * T + 256)
        mb = np.stack([first if c == 0 else mid, mid, last if c == NCORES - 1 else mid], 1)
        in_maps.append({"qT": np.ascontiguousarray(q_rot[sl].transpose(1, 2, 0)), "kT": np.ascontiguousarray(kp[hs].transpose(1, 2, 0)),
                        "vtok": np.ascontiguousarray(vp[hs].reshape(T + 256, 512)),
                        "sink": np.ascontiguousarray(np.broadcast_to(sink.reshape(1, 16), (128, 16))).astype(np.float32),
                        "mbias": np.ascontiguousarray(mb), "ident": np.eye(128, dtype=np.float32)})
    res = run_bass_kernel_spmd(nc, in_maps, core_ids=list(range(NCORES)))
    return np.concatenate([r["oT"].transpose(2, 0, 1).reshape(T, 2048) for r in res.results], 0)


def build_mid(stop=0):
    P = TokProg()
    xT = P.dram_in("xT", [D, T])
    yfT = P.dram_in("yfT", [1024, T])
    ybT = P.dram_in("ybT", [1024, T])
    ofT = P.dram_in("ofT", [1024, T])
    obT = P.dram_in("obT", [1024, T])
    gT = P.dram_in("gT", [1024, T])
    w_glu = P.wparam("w_glu", 1024, 1024)
    b_glu = P.dram_in("b_glu", [128, 8])
    w_out = P.wparam("w_out", D, D)
    w1 = P.wparam("w1", D, 4 * D)
    w2 = P.wparam("w2", 4 * D, D)
    lnp = P.dram_in("lnp", [128, 4, 16])
    w_in2 = P.wparam("w_in2", D, 3072)
    cosF = P.dram_in("cosF", [128, T])
    sinF = P.dram_in("sinF", [128, T])
    perm = P.dram_in("perm", [128, 128])
    x2T = P.dram_out("x2T", [D, T])
    h2T = P.dram_out("h2T", [3072, T])

    X = P.sb("X", [128, 16, TH], F32)
    Rb = P.sb("R", [128, 16, TH], F32)
    Bb = P.sb("Bb", [128, 16, TH], BF16)
    H = P.sb("H", [128, 64, TH], BF16)
    lnp_sb = P.sb("lnp_sb", [128, 4, 16], F32)
    bglu_sb = P.sb("bglu_sb", [128, 8], F32)
    cs = P.sb("cs", [128, 2, T], F32)
    perm32 = P.sb("perm32", [128, 128], F32)
    permb = P.sb("permb", [128, 128], BF16)
    hb16 = [P.sb(f"hb16_{i}", [128, TH], BF16) for i in range(2)]
    P.setup_common()
    xres = lambda kc: ('X', kc)
    rres = lambda kc: ('R', kc)
    bres = lambda kc: ('B', kc)
    hres = lambda kc: ('H', kc)
    v = lambda ap: ap.rearrange("(kc p) t -> p kc t", p=128)
    xTv, yfv, ybv, ofv, obv, gv, x2v, h2v = v(xT), v(yfT), v(ybT), v(ofT), v(obT), v(gT), v(x2T), v(h2T)
    YB = H

    def body():
        P.start()
        P.dcp('sp', lnp_sb[:, :, :], lnp[:, :, :], (), [('lnp',)], 'lnp')
        P.dcp('sp', bglu_sb[:, :], b_glu[:, :], (), [('bglu',)], 'bglu')
        P.dcp('sp', cs[:, 0, :], cosF[:, :], (), [('cs',)], 'cs0')
        P.dcp('sp', cs[:, 1, :], sinF[:, :], (), [('cs',)], 'cs1')
        P.dcp('sp', perm32[:, :], perm[:, :], (), [('perm32',)], 'perm32')
        P.act(permb[:, :], perm32[:, :], AF.Copy, reads=[('perm32',)], writes=[('permb',)])
        outs = []
        hbi = [0]
        for half in range(T // TH):
            t0 = half * TH
            ts_ = slice(t0, t0 + TH)
            for n in range(8):
                P.dcp('sp', Rb[:, n, :], yfv[:, n, ts_], (), [rres(n)], ('R', n))
                P.dcp('sp', Rb[:, 8 + n, :], ybv[:, n, ts_], (), [rres(8 + n)], ('R', 8 + n))
            for n in range(8):
                xs = X[:, n, :]
                t1 = P.tmp[0]; t2 = P.tmp[1]
                P.tt('dve', xs, Rb[:, n, :], Rb[:, 8 + n, :], ALU.add, reads=[rres(n), rres(8 + n)], writes=[xres(n)])
                P.tt('dve', t1[:, :], xs, xs, ALU.mult, reads=[xres(n)], writes=[('tmp', 0)])
                P.ts('dve', t1[:, :], t1[:, :], 0.044715, 1.0, ALU.mult, ALU.add, reads=[('tmp', 0)], writes=[('tmp', 0)])
                P.tt('dve', t1[:, :], t1[:, :], xs, ALU.mult, reads=[('tmp', 0), xres(n)], writes=[('tmp', 0)])
                P.act(t2[:, :], t1[:, :], AF.Sigmoid, reads=[('tmp', 0)], writes=[('tmp', 1)], scale=1.5957691216057308)
                P.tt('dve', xs, xs, t2[:, :], ALU.mult, reads=[xres(n), ('tmp', 1)], writes=[xres(n)])
                P.act(YB[:, n, :], xs, AF.Copy, reads=[xres(n)], writes=[hres(n)])

            def evg(n, ps, pr):
                t2 = P.tmp[1]
                P.act(t2[:, :], ps[:, :], AF.Sigmoid, reads=[pr, ('bglu',)], writes=[('tmp', 1)], bias=bglu_sb[:, n:n + 1])
                P.tt('dve', Bb[:, n, :], X[:, n, :], t2[:, :], ALU.mult, reads=[xres(n), ('tmp', 1)], writes=[bres(n)])
            P.lin(w_glu, 1024, 1024, lambda kc: YB[:, kc, :], hres, evg)
            if stop == 1:
                for kc in range(8):
                    P.dcp('sp', x2v[:, kc, ts_], X[:, kc, :], [xres(kc), bres(kc)], [('out', half, kc)], ('X', kc))
                    outs.append(('out', half, kc))
                return outs
            for n in range(8):
                P.dcp('sp', Rb[:, n, :], ofv[:, n, ts_], (), [rres(n)], ('R', n))
                P.dcp('sp', Rb[:, 8 + n, :], obv[:, n, ts_], (), [rres(8 + n)], ('R', 8 + n))
            for n in range(8):
                P.tt('dve', X[:, 8 + n, :], Rb[:, n, :], Rb[:, 8 + n, :], ALU.add, reads=[rres(n), rres(8 + n)], writes=[xres(8 + n)])
            for n in range(8):
                P.dcp('sp', Rb[:, n, :], gv[:, n, ts_], (), [rres(n)], ('R', n))
            for hh in range(4):
                P.stats([(X[:, 8 + 2 * hh + k, :], xres(8 + 2 * hh + k)) for k in range(2)], 1e-6)
                for k in range(2):
                    n = 2 * hh + k
                    t1 = P.tmp[0]; t2 = P.tmp[1]
                    P.act(t2[:, :], Rb[:, n, :], AF.Silu, reads=[rres(n)], writes=[('tmp', 1)])
                    P.tt('dve', t1[:, :], X[:, 8 + n, :], P.mean[:, :], ALU.subtract, reads=[xres(8 + n), ('mean',)], writes=[('tmp', 0)])
                    P.tt('dve', t1[:, :], t1[:, :], P.rstd[:, :], ALU.mult, reads=[('tmp', 0), ('rstd',)], writes=[('tmp', 0)])
                    P.tt('dve', Bb[:, 8 + n, :], t1[:, :], t2[:, :], ALU.mult, reads=[('tmp', 0), ('tmp', 1)], writes=[bres(8 + n)])
            if stop == 2:
                for kc in range(16):
                    P.dcp('sp', x2v[:, kc, ts_], X[:, kc, :], [xres(kc), bres(kc)], [('out', half, kc)], ('X', kc))
                    outs.append(('out', half, kc))
                return outs
            for kc in range(16):
                P.dcp('sp', X[:, kc, :], xTv[:, kc, ts_], (), [xres(kc)], ('X', kc))

            def ev1(n, ps, pr):
                P.stt(Rb[:, n, :], X[:, n, :], ALPHA, ps[:, :], ALU.mult, ALU.add, reads=[xres(n), pr], writes=[rres(n)])
            P.lin(w_out, D, D, lambda kc: Bb[:, kc, :], bres, ev1)
            P.layernorm(Rb, rres, X, xres, Bb, bres, lnp_sb[:, 0, :], lnp_sb[:, 1, :])

            def ev2(n, ps, pr):
                P.act(H[:, n, :], ps[:, :], AF.Relu, reads=[pr], writes=[hres(n)])
                P.tt('dve', H[:, n, :], H[:, n, :], H[:, n, :], ALU.mult, reads=[hres(n)], writes=[hres(n)])
            P.lin(w1, D, 4 * D, lambda kc: Bb[:, kc, :], bres, ev2)

            def ev3(n, ps, pr):
                P.stt(X[:, n, :], X[:, n, :], ALPHA, ps[:, :], ALU.mult, ALU.add, reads=[xres(n), pr], writes=[xres(n)])
            P.lin(w2, 4 * D, D, lambda kc: H[:, kc, :], hres, ev3)
            P.layernorm(X, xres, Rb, rres, Bb, bres, lnp_sb[:, 2, :], lnp_sb[:, 3, :])
            for kc in range(16):
                P.dcp('sp', x2v[:, kc, ts_], Rb[:, kc, :], [rres(kc)], [('out', half, kc)], ('R', kc))
                outs.append(('out', half, kc))
            if stop == 3:
                continue
            cosv = cs[:, 0, ts_]; sinv = cs[:, 1, ts_]

            def ev4(n, ps, pr):
                ores = ('X', n % 16)
                ob = X[:, n % 16, :]
                if n < 20:
                    i = hbi[0] % 2; hbi[0] += 1
                    P.act(hb16[i][:, :], ps[:, :], AF.Copy, reads=[pr], writes=[('hb16', i), pr])
                    pp = P.pss[6 + i]; ppr = ('ps', 6 + i)
                    P.mm(pp[:, :], permb[:, :], hb16[i][:, :], True, True, reads=[('permb',), ('hb16', i)], writes=[ppr])
                    t1 = P.tmp[0]; t2 = P.tmp[1]
                    P.tt('dve', t1[:, :], ps[:, :], cosv, ALU.mult, reads=[pr, ('cs',)], writes=[('tmp', 0)])
                    P.tt('dve', t2[:, :], pp[:, :], sinv, ALU.mult, reads=[ppr, ('cs',)], writes=[('tmp', 1)])
                    P.tt('dve', ob, t1[:, :], t2[:, :], ALU.add, reads=[('tmp', 0), ('tmp', 1)], writes=[ores])
                else:
                    P.act(ob, ps[:, :], AF.Copy, reads=[pr], writes=[ores])
                P.dcp('sp', h2v[:, n, ts_], ob, [ores], [('outh', half, n)], ('X', n % 16))
                outs.append(('outh', half, n))
            P.lin(w_in2, D, 3072, lambda kc: Bb[:, kc, :], bres, ev4)
        return outs

    P.plan = True
    body()
    P.plan = False
    outs = body()
    return P.finish(outs)


def odd_rope_tables(pos):
    c, s = rope_tables(pos, 16, 500000.0)
    n = len(pos)
    cosF = np.ones((128, n), np.float32); sinF = np.zeros((128, n), np.float32)
    cosF[0:16] = c; cosF[16:32] = c
    sinF[0:16] = -s; sinF[16:32] = s
    perm = np.zeros((128, 128), np.float32)
    for m in range(16):
        perm[m + 16, m] = 1.0
        perm[m, m + 16] = 1.0
    return cosF, sinF, perm


def tm(a, c):
    return np.ascontiguousarray(a[c * T:(c + 1) * T].T)


def run_mid(x_tok, yf, yb, of, ob, gate, inp, W, stop=0):
    nc = build_mid(stop)
    in_maps = []
    lnp = lnp_layout(inp['ln_g'], inp['ln_b'], 0)
    bglu = np.ascontiguousarray(inp['s5_b_glu'][0].reshape(8, 128).T)
    for c in range(NCORES):
        cosF, sinF, perm = odd_rope_tables(np.arange(c * T, (c + 1) * T))
        in_maps.append({"xT": tm(x_tok, c), "yfT": tm(yf, c), "ybT": tm(yb, c), "ofT": tm(of, c), "obT": tm(ob, c), "gT": tm(gate, c),
                        "w_glu": W[('s5_w_glu', 0)], "b_glu": bglu, "w_out": W[('even_w_out', 0)], "w1": W[('mlp_w1', 0)],
                        "w2": W[('mlp_w2', 0)], "lnp": lnp, "w_in2": W[('odd_w_in', 0)], "cosF": cosF, "sinF": sinF, "perm": perm})
    res = run_bass_kernel_spmd(nc, in_maps, core_ids=list(range(NCORES)))
    x2 = np.concatenate([r["x2T"].T for r in res.results], 0)
    h2 = np.concatenate([r["h2T"].T for r in res.results], 0)
    return x2, h2


def build_inproj_odd(mode=0):
    P = TokProg()
    xT = P.dram_in("xT", [D, T])
    w_in2 = P.wparam("w_in2", D, 3072)
    cosF = P.dram_in("cosF", [128, T])
    sinF = P.dram_in("sinF", [128, T])
    perm = P.dram_in("perm", [128, 128])
    h2T = P.dram_out("h2T", [3072, T])
    Rb = P.sb("R", [128, 16, TH], F32)
    Bb = P.sb("Bb", [128, 16, TH], BF16)
    O = P.sb("O", [128, 24, TH], F32)
    cs = P.sb("cs", [128, 2, T], F32)
    perm32 = P.sb("perm32", [128, 128], F32)
    permb = P.sb("permb", [128, 128], BF16)
    hb16 = [P.sb(f"hb16_{i}", [128, TH], BF16) for i in range(2)]
    P.setup_common()
    rres = lambda kc: ('R', kc)
    bres = lambda kc: ('B', kc)
    xTv = xT.rearrange("(kc p) t -> p kc t", p=128)
    h2v = h2T.rearrange("(kc p) t -> p kc t", p=128)

    def body():
        P.start()
        P.dcp('sp', cs[:, 0, :], cosF[:, :], (), [('cs',)], 'cs0')
        P.dcp('sp', cs[:, 1, :], sinF[:, :], (), [('cs',)], 'cs1')
        P.dcp('sp', perm32[:, :], perm[:, :], (), [('perm32',)], 'perm32')
        P.act(permb[:, :], perm32[:, :], AF.Copy, reads=[('perm32',)], writes=[('permb',)])
        outs = []
        hbi = [0]
        for half in range(T // TH):
            t0 = half * TH
            ts_ = slice(t0, t0 + TH)
            for kc in range(16):
                P.dcp('sp', Rb[:, kc, :], xTv[:, kc, ts_], (), [rres(kc)], ('R', kc))
                P.act(Bb[:, kc, :], Rb[:, kc, :], AF.Copy, reads=[rres(kc)], writes=[bres(kc)])
            cosv = cs[:, 0, ts_]; sinv = cs[:, 1, ts_]

            def ev4(n, ps, pr):
                ores = ('O', n)
                ob = O[:, n, :]
                if n < 20 and mode == 0:
                    i = hbi[0] % 2; hbi[0] += 1
                    P.act(hb16[i][:, :], ps[:, :], AF.Copy, reads=[pr], writes=[('hb16', i), pr])
                    pp = P.pss[6 + i]; ppr = ('ps', 6 + i)
                    P.mm(pp[:, :], permb[:, :], hb16[i][:, :], True, True, reads=[('permb',), ('hb16', i)], writes=[ppr])
                    t1 = P.tmp[0]; t2 = P.tmp[1]
                    P.tt('dve', t1[:, :], ps[:, :], cosv, ALU.mult, reads=[pr, ('cs',)], writes=[('tmp', 0)])
                    P.tt('dve', t2[:, :], pp[:, :], sinv, ALU.mult, reads=[ppr, ('cs',)], writes=[('tmp', 1)])
                    P.tt('dve', ob, t1[:, :], t2[:, :], ALU.add, reads=[('tmp', 0), ('tmp', 1)], writes=[ores])
                else:
                    P.act(ob, ps[:, :], AF.Copy, reads=[pr], writes=[ores])
                P.dcp('sp', h2v[:, n, ts_], ob, [ores], [('outh', half, n)], ('O', n))
                outs.append(('outh', half, n))
            P.lin(w_in2, D, 3072, lambda kc: Bb[:, kc, :], bres, ev4)
        return outs

    P.plan = True
    body()
    P.plan = False
    outs = body()
    return P.finish(outs)


def run_inproj_odd(x_tok, w_in2, mode=0):
    nc = build_inproj_odd(mode)
    in_maps = []
    for c in range(NCORES):
        cosF, sinF, perm = odd_rope_tables(np.arange(c * T, (c + 1) * T))
        in_maps.append({"xT": tm(x_tok, c), "w_in2": w_in2, "cosF": cosF, "sinF": sinF, "perm": perm})
    res = run_bass_kernel_spmd(nc, in_maps, core_ids=list(range(NCORES)))
    return np.concatenate([r["h2T"].T for r in res.results], 0)


WNAMES = [('even_w_in', 0, D, 5120), ('even_w_out', 0, D, D), ('s5_w_glu', 0, 1024, 1024), ('mlp_w1', 0, D, 4 * D), ('mlp_w2', 0, 4 * D, D),
          ('odd_w_in', 0, D, 3072), ('odd_w_out', 0, D, D), ('mlp_w1', 1, D, 4 * D), ('mlp_w2', 1, 4 * D, D)]
CAST_ROWS = sum(k * n for (_, _, k, n) in WNAMES) // 2048 // NCORES


def build_cast():
    P = Prog()
    w = P.dram_in("w", [CAST_ROWS, 2048])
    o = P.dram_out("o", [CAST_ROWS, 2048], BF16)
    outs = []
    RB = 64
    for i in range(CAST_ROWS // RB):
        P.dma('pool', lambda e, i=i: e.dma_start(out=o[i * RB:(i + 1) * RB, :], in_=w[i * RB:(i + 1) * RB, :]),
              writes=[('o', i)], key=('o', i % 8))
        outs.append(('o', i))
    return P.finish(outs)


def cast_weights(inp):
    flat = np.concatenate([np.asarray(inp[nm][li], dtype=np.float32).reshape(-1) for (nm, li, _, _) in WNAMES]).reshape(NCORES, CAST_ROWS, 2048)
    nc = build_cast()
    res = run_bass_kernel_spmd(nc, [{"w": flat[c]} for c in range(NCORES)], core_ids=list(range(NCORES)))
    ob = np.concatenate([r["o"].reshape(-1) for r in res.results])
    out = {}
    off = 0
    for (nm, li, k, n) in WNAMES:
        out[(nm, li)] = to_panels(ob[off:off + k * n].reshape(k, n), k, n)
        off += k * n
    return out


def kernel(**inputs):
    inp = {k: np.asarray(v) for k, v in inputs.items()}
    x = np.ascontiguousarray(inp['x'][0], dtype=np.float32)
    W = cast_weights(inp)
    h = run_inproj_even(x, W[('even_w_in', 0)])
    u = np.ascontiguousarray(h[:, :1024])
    q = h[:, 1024:2048].reshape(L, 4, 256)
    k = h[:, 2048:3072].reshape(L, 4, 256)
    v = h[:, 3072:4096].reshape(L, 4, 256)
    gate = np.ascontiguousarray(h[:, 4096:])
    yf, yb = run_s5(u, inp)
    of, ob = run_ret(q, k, v, inp['ret_log_decay'][0])
    x2, h2 = run_mid(x, yf, yb, of.reshape(L, 1024), ob.reshape(L, 1024), gate, inp, W)
    q2 = h2[:, :2048].reshape(L, 16, 128)
    k2 = h2[:, 2048:2560].reshape(L, 4, 128)
    v2 = h2[:, 2560:].reshape(L, 4, 128)
    o = run_attn(q2, k2, v2, inp['attn_sink'][0])
    y = run_tail(x2, o, W[('odd_w_out', 0)], W[('mlp_w1', 1)], W[('mlp_w2', 1)], lnp_layout(inp['ln_g'], inp['ln_b'], 1))
    return y.reshape(1, L, D).astype(np.float32)
```
